# Optimizing a Trainium2 kernel written in Bass

```python
import jax, jax.numpy as jnp
from jax import lax
import numpy as np

D_MODEL = 2048
BATCH = 4
SEQ = 2048
DEPTH = 2
DEC_BATCH = 128
DEC_SEQ = 1
PAST_LEN = 16384
PAGE_SIZE = 128

N_MIXERS = 2
N_POOL_LAYERS = (DEPTH + 1) // 2
N_DELTA_LAYERS = DEPTH // 2
POOL_WINDOWS = (2, 4, 8, 16)
N_POOL_GROUPS = len(POOL_WINDOWS)
POOL_GROUP_DIM = D_MODEL // N_POOL_GROUPS
POOL_STATE = max(POOL_WINDOWS) - 1
N_QK_HEADS = 16
N_V_HEADS = 32
HEAD_K = 128
HEAD_V = 128
KEY_DIM = N_QK_HEADS * HEAD_K
VALUE_DIM = N_V_HEADS * HEAD_V
CONV_WIDTH = 4
CONV_DIM = 2 * KEY_DIM + VALUE_DIM
PROJ_DIM = CONV_DIM + VALUE_DIM + 2 * N_V_HEADS
CHUNK = 64
D_FF = 5632
FFN_CONV_WIDTH = 3
EPS = 1e-6

kernel_name = 'hybrid_pool_gdn_convffn_decode_step'


def rmsnorm(x, w):
    xf = x.astype(jnp.float32)
    y = xf * lax.rsqrt(jnp.mean(xf * xf, -1, keepdims=True) + EPS) * w.astype(jnp.float32)
    return y.astype(x.dtype)


def l2norm(x):
    return x * lax.rsqrt(jnp.sum(x * x, -1, keepdims=True) + EPS)


def causal_dwconv(x, past, w):
    width, ch = w.shape
    xp = jnp.concatenate([past.astype(x.dtype), x], axis=1)
    y = lax.conv_general_dilated(xp, w.astype(x.dtype)[:, None, :], window_strides=(1,), padding='VALID',
                                 dimension_numbers=('NWC', 'WIO', 'NWC'), feature_group_count=ch)
    return y, xp[:, -(width - 1):]


def pool_mixer(h, past, n_past, w_group, scale):
    B, T, _ = h.shape
    hp = jnp.concatenate([past.astype(h.dtype), h], axis=1).astype(jnp.float32)
    cs = jnp.concatenate([jnp.zeros((B, 1, D_MODEL), jnp.float32), jnp.cumsum(hp, axis=1)], axis=1)
    pos = jnp.arange(T) + n_past
    means = []
    for gi, w in enumerate(POOL_WINDOWS):
        sl = slice(gi * POOL_GROUP_DIM, (gi + 1) * POOL_GROUP_DIM)
        end = cs[:, POOL_STATE + 1:POOL_STATE + 1 + T, sl]
        start = cs[:, POOL_STATE + 1 - w:POOL_STATE + 1 - w + T, sl]
        cnt = jnp.minimum(pos + 1, w).astype(jnp.float32)[None, :, None]
        means.append((end - start) / cnt)
    mean = jnp.stack(means, axis=2)
    d = mean - hp[:, POOL_STATE:].reshape(B, T, N_POOL_GROUPS, POOL_GROUP_DIM)
    y = jnp.einsum('btgc,gcd->btgd', d, w_group.astype(jnp.float32)).reshape(B, T, D_MODEL)
    y = y * scale.astype(jnp.float32)
    return y.astype(h.dtype), hp[:, -POOL_STATE:].astype(h.dtype)


def chunk_gated_delta(q, k, v, beta, g, s0):
    B, T, H, dk = q.shape
    dv = v.shape[-1]
    C = min(CHUNK, T)
    N = -(-T // C)
    pad = N * C - T

    def prep(a):
        a = jnp.pad(a, [(0, 0), (0, pad)] + [(0, 0)] * (a.ndim - 2))
        a = jnp.moveaxis(a, 2, 1)
        return a.reshape((B, H, N, C) + a.shape[3:])

    q, k, v, beta, g = prep(q), prep(k), prep(v), prep(beta), prep(g)
    gc = jnp.cumsum(g, axis=-1)
    tril = jnp.tril(jnp.ones((C, C), bool))
    strict = jnp.tril(jnp.ones((C, C), bool), -1)
    decay = jnp.exp(jnp.where(tril, gc[..., :, None] - gc[..., None, :], -jnp.inf))
    kb = k * beta[..., None]
    vb = v * beta[..., None]
    lower = jnp.where(strict, jnp.einsum('bhnid,bhnjd->bhnij', kb, k) * decay, 0.0)
    eye = jnp.eye(C, dtype=jnp.float32)
    tinv = lax.linalg.triangular_solve(eye + lower, jnp.broadcast_to(eye, lower.shape), left_side=True,
                                       lower=True, unit_diagonal=True)
    u = jnp.einsum('bhnij,bhnjd->bhnid', tinv, vb)
    wk = jnp.einsum('bhnij,bhnjd->bhnid', tinv, kb * jnp.exp(gc)[..., None])
    qk = jnp.where(tril, jnp.einsum('bhnid,bhnjd->bhnij', q, k) * decay, 0.0)

    def step(s, xs):
        q_i, k_i, u_i, w_i, qk_i, g_i = xs
        v_new = u_i - jnp.einsum('bhcd,bhde->bhce', w_i, s)
        o = jnp.einsum('bhcd,bhde->bhce', q_i * jnp.exp(g_i)[..., None], s) + jnp.einsum('bhij,bhje->bhie', qk_i, v_new)
        g_last = g_i[..., -1]
        s = s * jnp.exp(g_last)[..., None, None] + jnp.einsum(
            'bhcd,bhce->bhde', k_i * jnp.exp(g_last[..., None] - g_i)[..., None], v_new)
        return s, o

    xs = tuple(jnp.moveaxis(a, 2, 0) for a in (q, k, u, wk, qk, gc))
    s, o = lax.scan(step, s0, xs)
    o = jnp.moveaxis(o, 0, 2).reshape(B, H, N * C, dv)[:, :, :T]
    return jnp.moveaxis(o, 1, 2), s


def delta_mixer(h, conv_past, s0, w_in, conv_w, a_log, dt_bias, norm_w, w_out):
    B, T, _ = h.shape
    proj = h @ w_in
    qkv = proj[..., :CONV_DIM]
    z = proj[..., CONV_DIM:CONV_DIM + VALUE_DIM]
    b_raw = proj[..., CONV_DIM + VALUE_DIM:CONV_DIM + VALUE_DIM + N_V_HEADS]
    a_raw = proj[..., CONV_DIM + VALUE_DIM + N_V_HEADS:]
    qkv, new_conv = causal_dwconv(qkv, conv_past, conv_w)
    qkv = jax.nn.silu(qkv.astype(jnp.float32))
    rep = N_V_HEADS // N_QK_HEADS
    q = jnp.repeat(l2norm(qkv[..., :KEY_DIM].reshape(B, T, N_QK_HEADS, HEAD_K)), rep, axis=2) * (HEAD_K ** -0.5)
    k = jnp.repeat(l2norm(qkv[..., KEY_DIM:2 * KEY_DIM].reshape(B, T, N_QK_HEADS, HEAD_K)), rep, axis=2)
    v = qkv[..., 2 * KEY_DIM:].reshape(B, T, N_V_HEADS, HEAD_V)
    beta = jax.nn.sigmoid(b_raw.astype(jnp.float32))
    g = -jnp.exp(a_log.astype(jnp.float32)) * jax.nn.softplus(a_raw.astype(jnp.float32) + dt_bias.astype(jnp.float32))
    o, s_new = chunk_gated_delta(q, k, v, beta, g, s0.astype(jnp.float32))
    o = o * lax.rsqrt(jnp.mean(o * o, -1, keepdims=True) + EPS) * norm_w.astype(jnp.float32)
    o = o * jax.nn.silu(z.astype(jnp.float32).reshape(B, T, N_V_HEADS, HEAD_V))
    y = o.reshape(B, T, VALUE_DIM).astype(h.dtype) @ w_out
    return y, new_conv, s_new


def conv_ffn(h, past, w_gate, w_up, conv_w, w_down):
    gate, new_past = causal_dwconv(h @ w_gate, past, conv_w)
    a = jax.nn.silu(gate) * (h @ w_up)
    return a @ w_down, new_past


def trunk(x, c, n_past, pool_st, conv_st, rec_st, ffn_st, norm_w, ada_w, ada_b, pool_w, pool_scale,
          dn_w_in, dn_conv_w, dn_a_log, dn_dt_bias, dn_norm_w, dn_w_out, ffn_w_gate, ffn_w_up, ffn_conv_w, ffn_w_down):
    new_pool, new_conv, new_rec, new_ffn = [], [], [], []
    cs = jax.nn.silu(c)
    for i in range(DEPTH):
        mod = cs @ ada_w[i] + ada_b[i]
        sh_m, sc_m, gt_m, sh_f, sc_f, gt_f = [m[:, None, :] for m in jnp.split(mod, 6, axis=-1)]
        h = rmsnorm(x, norm_w[i, 0]) * (1 + sc_m) + sh_m
        j = i // N_MIXERS
        if i % N_MIXERS == 0:
            y, st = pool_mixer(h, pool_st[j], n_past, pool_w[j], pool_scale[j])
            new_pool.append(st)
        else:
            y, cst, s = delta_mixer(h, conv_st[j], rec_st[j], dn_w_in[j], dn_conv_w[j], dn_a_log[j],
                                    dn_dt_bias[j], dn_norm_w[j], dn_w_out[j])
            new_conv.append(cst)
            new_rec.append(s)
        x = x + gt_m * rmsnorm(y, norm_w[i, 1])
        h = rmsnorm(x, norm_w[i, 2]) * (1 + sc_f) + sh_f
        y, fst = conv_ffn(h, ffn_st[i], ffn_w_gate[i], ffn_w_up[i], ffn_conv_w[i], ffn_w_down[i])
        new_ffn.append(fst)
        x = x + gt_f * rmsnorm(y, norm_w[i, 3])
    return x, jnp.stack(new_pool), jnp.stack(new_conv), jnp.stack(new_rec), jnp.stack(new_ffn)


def setup_inputs(seed: int = 0) -> dict:
    key = jax.random.key(seed)
    ks = jax.random.split(key, 24)

    def nrm(k, shape, s):
        return jax.random.normal(k, shape, jnp.float32) * s

    return {
        'x_prompt': nrm(ks[0], (BATCH, SEQ, D_MODEL), 1.0),
        'x_sample': nrm(ks[1], (DEC_BATCH, DEC_SEQ, D_MODEL), 1.0),
        'c_prompt': nrm(ks[2], (BATCH, D_MODEL), 1.0),
        'c_sample': nrm(ks[3], (DEC_BATCH, D_MODEL), 1.0),
        'cache_pool': nrm(ks[4], (N_POOL_LAYERS, DEC_BATCH, POOL_STATE, D_MODEL), 1.0),
        'state_conv': nrm(ks[5], (N_DELTA_LAYERS, DEC_BATCH, CONV_WIDTH - 1, CONV_DIM), 1.0),
        'state_rec': nrm(ks[6], (N_DELTA_LAYERS, DEC_BATCH, N_V_HEADS, HEAD_K, HEAD_V), HEAD_K ** -0.5),
        'cache_ffn_conv': nrm(ks[7], (DEPTH, DEC_BATCH, FFN_CONV_WIDTH - 1, D_FF), 1.0),
        'norm_w': 1.0 + nrm(ks[8], (DEPTH, 4, D_MODEL), 0.05),
        'ada_w': nrm(ks[9], (DEPTH, D_MODEL, 6 * D_MODEL), D_MODEL ** -0.5),
        'ada_b': nrm(ks[10], (DEPTH, 6 * D_MODEL), 0.01),
        'pool_w': nrm(ks[11], (N_POOL_LAYERS, N_POOL_GROUPS, POOL_GROUP_DIM, POOL_GROUP_DIM), POOL_GROUP_DIM ** -0.5),
        'pool_scale': 1.0 + nrm(ks[12], (N_POOL_LAYERS, D_MODEL), 0.1),
        'dn_w_in': nrm(ks[13], (N_DELTA_LAYERS, D_MODEL, PROJ_DIM), D_MODEL ** -0.5),
        'dn_conv_w': nrm(ks[14], (N_DELTA_LAYERS, CONV_WIDTH, CONV_DIM), CONV_WIDTH ** -0.5),
        'dn_a_log': jnp.log(jax.random.uniform(ks[15], (N_DELTA_LAYERS, N_V_HEADS), jnp.float32, 1.0, 16.0)),
        'dn_dt_bias': nrm(ks[16], (N_DELTA_LAYERS, N_V_HEADS), 0.1),
        'dn_norm_w': 1.0 + nrm(ks[17], (N_DELTA_LAYERS, HEAD_V), 0.05),
        'dn_w_out': nrm(ks[18], (N_DELTA_LAYERS, VALUE_DIM, D_MODEL), VALUE_DIM ** -0.5),
        'ffn_w_gate': nrm(ks[19], (DEPTH, D_MODEL, D_FF), D_MODEL ** -0.5),
        'ffn_w_up': nrm(ks[20], (DEPTH, D_MODEL, D_FF), D_MODEL ** -0.5),
        'ffn_conv_w': nrm(ks[21], (DEPTH, FFN_CONV_WIDTH, D_FF), FFN_CONV_WIDTH ** -0.5),
        'ffn_w_down': nrm(ks[22], (DEPTH, D_FF, D_MODEL), D_FF ** -0.5),
    }


def reference(x_prompt, x_sample, c_prompt, c_sample, cache_pool, state_conv, state_rec, cache_ffn_conv,
              norm_w, ada_w, ada_b, pool_w, pool_scale, dn_w_in, dn_conv_w, dn_a_log, dn_dt_bias, dn_norm_w,
              dn_w_out, ffn_w_gate, ffn_w_up, ffn_conv_w, ffn_w_down):
    bp = x_prompt.shape[0]
    dt = x_prompt.dtype
    pool0 = jnp.zeros((N_POOL_LAYERS, bp, POOL_STATE, D_MODEL), dt)
    conv0 = jnp.zeros((N_DELTA_LAYERS, bp, CONV_WIDTH - 1, CONV_DIM), dt)
    rec0 = jnp.zeros((N_DELTA_LAYERS, bp, N_V_HEADS, HEAD_K, HEAD_V), jnp.float32)
    ffn0 = jnp.zeros((DEPTH, bp, FFN_CONV_WIDTH - 1, D_FF), dt)
    y_prompt, pool_p, conv_p, rec_p, ffn_p = trunk(
        x_prompt, c_prompt, 0, pool0, conv0, rec0, ffn0, norm_w, ada_w, ada_b, pool_w, pool_scale,
        dn_w_in, dn_conv_w, dn_a_log, dn_dt_bias, dn_norm_w, dn_w_out, ffn_w_gate, ffn_w_up, ffn_conv_w, ffn_w_down)
    y_sample, pool_s, conv_s, rec_s, ffn_s = trunk(
        x_sample, c_sample, PAST_LEN, cache_pool, state_conv, state_rec, cache_ffn_conv, norm_w, ada_w, ada_b,
        pool_w, pool_scale, dn_w_in, dn_conv_w, dn_a_log, dn_dt_bias, dn_norm_w, dn_w_out, ffn_w_gate, ffn_w_up,
        ffn_conv_w, ffn_w_down)
    return (y_prompt, y_sample, pool_p, pool_s, conv_p, conv_s, rec_p, rec_s, ffn_p, ffn_s)
```

```python
import numpy as np
import concourse.bass as bass
import concourse.mybir as mybir
from concourse.bass_utils import run_bass_kernel_spmd

F32 = mybir.dt.float32
BF16 = mybir.dt.bfloat16
AF = mybir.ActivationFunctionType
ALU = mybir.AluOpType

D = 2048
NCH = 16
DFF = 5632
NFF = 44
NH = 32
EPS = 1e-6
TM = 1024
NS = 16
CONV_DIM = 8192
PROJ = 12352
BIG = 30000.0


class Buf:
    __slots__ = ("name", "excl")

    def __init__(self, name="", excl=False):
        self.name = name
        self.excl = excl


class Op:
    __slots__ = ("eng", "fn", "reads", "writes", "dma", "nofence", "deps", "signal", "count", "semidx", "waits")

    def __init__(self, eng, fn, reads, writes, dma, nofence):
        self.eng, self.fn, self.reads, self.writes, self.dma, self.nofence = eng, fn, reads, writes, dma, nofence
        self.deps = set()
        self.signal = False
        self.count = 0
        self.semidx = 0
        self.waits = []


ENGS = ("pe", "act", "dve", "pool", "sp")
NDSEM = 8


class Sched:
    def __init__(self):
        self.ops = []
        self.fences = []

    def op(self, eng, fn, reads=(), writes=(), nofence=False):
        reads, writes = list(reads), list(writes)
        for b in reads:
            if b.excl and b not in writes:
                writes.append(b)
        self.ops.append(Op(eng, fn, reads, writes, False, nofence))

    def dma(self, eng, out, in_, reads=(), writes=(), nofence=False):
        self.ops.append(Op(eng, (lambda e: e.dma_start(out=out, in_=in_)), list(reads), list(writes), True, nofence))

    def fence(self):
        self.fences.append(len(self.ops))

    def resolve(self):
        ops = self.ops
        last_w, readers = {}, {}
        last_comp = {}
        recent_dma = {e: [] for e in ENGS}
        pending = {e: None for e in ENGS}
        fset = set(self.fences)
        for i, op in enumerate(ops):
            if i in fset:
                F = set(last_comp.values())
                for e in ENGS:
                    F.update(recent_dma[e])
                for e in ENGS:
                    pending[e] = set(F) if pending[e] is None else (pending[e] | F)
            deps = set()
            for b in op.reads:
                if b in last_w:
                    deps.add(last_w[b])
            for b in op.writes:
                if b in last_w:
                    deps.add(last_w[b])
                deps.update(readers.get(b, ()))
            if not op.nofence and pending[op.eng] is not None:
                deps.update(pending[op.eng])
                pending[op.eng] = None
            deps.discard(i)
            op.deps = set(p for p in deps if not (ops[p].eng == op.eng == "pe"))
            for b in op.reads:
                readers.setdefault(b, []).append(i)
            for b in op.writes:
                last_w[b] = i
                readers[b] = []
            if op.dma:
                recent_dma[op.eng].append(i)
                if len(recent_dma[op.eng]) > NDSEM:
                    recent_dma[op.eng].pop(0)
            else:
                last_comp[op.eng] = i
        for op in ops:
            for p in op.deps:
                ops[p].signal = True
        cnt = {e: 0 for e in ENGS}
        ndma = {e: 0 for e in ENGS}
        for op in ops:
            if op.dma:
                j = ndma[op.eng]
                op.semidx = j % NDSEM
                op.count = 16 * (j // NDSEM + 1)
                ndma[op.eng] += 1
            elif op.signal:
                cnt[op.eng] += 1
                op.count = cnt[op.eng]
        known = {e: {} for e in ENGS}
        for op in ops:
            need = {}
            if op.dma and op.count > 16:
                need[(op.eng, "d", op.semidx)] = op.count - 16
            for p in op.deps:
                po = ops[p]
                key = (po.eng, "d", po.semidx) if po.dma else (po.eng, "c", 0)
                if need.get(key, 0) < po.count:
                    need[key] = po.count
            kn = known[op.eng]
            op.waits = []
            for key, val in need.items():
                if kn.get(key, 0) < val:
                    kn[key] = val
                    op.waits.append((key, val))
        self.final_dma = {e: ndma[e] for e in ENGS}

    def emit(self, nc):
        import contextlib
        with contextlib.ExitStack() as st:
            csem = {e: st.enter_context(nc.semaphore("c_" + e)) for e in ("pe", "act", "dve", "pool")}
            dsem = {e: [st.enter_context(nc.semaphore("d_%s%d" % (e, i))) for i in range(NDSEM)] for e in ("pool", "sp")}
            block = st.enter_context(nc.Block())
            ops = self.ops
            final_dma = self.final_dma

            def semof(key):
                return dsem[key[0]][key[2]] if key[1] == "d" else csem[key[0]]

            def run(ename, e):
                for op in ops:
                    if op.eng != ename:
                        continue
                    for key, val in op.waits:
                        e.wait_ge(semof(key), val)
                    ins = op.fn(e)
                    if op.dma:
                        ins.then_inc(dsem[ename][op.semidx], 16)
                    elif op.signal:
                        ins.then_inc(csem[ename], 1)
                if ename == "sp":
                    for q in ("pool", "sp"):
                        n = final_dma[q]
                        for r in range(NDSEM):
                            k = (n - r + NDSEM - 1) // NDSEM if n > r else 0
                            if k > 0:
                                e.wait_ge(dsem[q][r], 16 * k)

            @block.tensor
            def _(e):
                run("pe", e)

            @block.scalar
            def _(e):
                run("act", e)

            @block.vector
            def _(e):
                run("dve", e)

            @block.gpsimd
            def _(e):
                run("pool", e)

            @block.sync
            def _(e):
                run("sp", e)


POOL_W = (2, 4, 8, 16)
C_ID, C_TRI, C_MA, C_MB, C_FIX, C_END = 0, 128, 256, 384, 512, 576


def host_consts():
    c = np.zeros((128, C_END), np.float32)
    i = np.arange(128)
    c[:, C_ID:C_ID + 128] = np.eye(128, dtype=np.float32)
    c[:, C_TRI:C_TRI + 128] = (i[:, None] <= i[None, :]).astype(np.float32)
    c[:, C_MA:C_MA + 128] = np.where(i[None, :] >= i[:, None], BIG, 0.0)
    c[:, C_MB:C_MB + 128] = np.where(i[None, :] < i[:, None], -BIG, 0.0)
    for g, w in enumerate(POOL_W):
        t = np.arange(15)
        c[:, C_FIX + g * 15:C_FIX + (g + 1) * 15] = (1.0 / np.minimum(t + 1, w))[None, :]
    return c


def host_poolsel():
    s = np.zeros((2, 120, 4, 16), np.float32)
    for k in range(2):
        for bl in range(8):
            for r in range(15):
                for g, w in enumerate(POOL_W):
                    if r >= 15 - (w - 1):
                        s[k, bl * 15 + r, g, 8 * k + bl] = 1.0 / w
    return s


def build_program(debug=False, stop_after=None):
    nc = bass.Bass("TRN2", target_bir_lowering=False)
    S = Sched()
    dbg_outs = []

    def din(name, shape):
        return nc.dram_tensor(name, list(shape), F32, kind="ExternalInput").ap()

    def dout(name, shape):
        return nc.dram_tensor(name, list(shape), F32, kind="ExternalOutput").ap()

    def dscr(name, shape, dt=F32):
        if debug:
            dbg_outs.append(name)
            return nc.dram_tensor(name, list(shape), dt, kind="ExternalOutput").ap()
        return nc.dram_tensor(name, list(shape), dt).ap()

    xin = din("xin", [2 * TM + NS, D])
    cin = din("cin", [NS + 1, D])
    consts_d = din("consts", [128, C_END])
    poolsel_d = din("poolsel", [2, 120, 4, 16])
    cpool_d = din("cache_pool_c", [NS, 15, D])
    sconv_d = din("state_conv_c", [NS, 3, CONV_DIM])
    srec_d = din("state_rec_c", [NS, NH, 128, 128])
    cffn_d = din("cache_ffn_c", [2, NS, 2, DFF])
    norm_w = din("norm_w", [2, 4, D])
    ada_w = din("ada_w", [2, D, 6 * D])
    ada_b = din("ada_b", [2, 6 * D])
    pool_w = din("pool_w", [1, 4, 512, 512])
    pool_scale = din("pool_scale", [1, D])
    dn_w_in = din("dn_w_in", [1, D, PROJ])
    dn_conv_w = din("dn_conv_w", [1, 4, CONV_DIM])
    dn_a_log = din("dn_a_log", [1, NH])
    dn_dt_bias = din("dn_dt_bias", [1, NH])
    dn_norm_w = din("dn_norm_w", [1, 128])
    dn_w_out = din("dn_w_out", [1, 4096, D])
    ffn_w_gate = din("ffn_w_gate", [2, D, DFF])
    ffn_w_up = din("ffn_w_up", [2, D, DFF])
    ffn_conv_w = din("ffn_conv_w", [2, 3, DFF])
    ffn_w_down = din("ffn_w_down", [2, DFF, D])

    y_p = dout("y_p", [2 * TM, D])
    y_s = dout("y_s", [NS, D])
    pool_p = dout("pool_p", [16, D])
    pool_s = dout("pool_s", [NS, 15, D])
    conv_p = dout("conv_p", [3, CONV_DIM])
    conv_s = dout("conv_s", [NS, 3, CONV_DIM])
    rec_p = dout("rec_p", [NH, 128, 128])
    rec_s = dout("rec_s", [NS, NH, 128, 128])
    ffn_p = dout("ffn_p", [2, 2, DFF])
    ffn_s = dout("ffn_s", [2, NS, 2, DFF])

    TMAX = TM + NS
    X_d = dscr("X_scr", [128, NCH, TMAX])
    Y_d = dscr("Y_scr", [128, NCH, TMAX])
    ON_d = dscr("ON_scr", [128, NH, TMAX], BF16)
    SD_d = dscr("SD_scr", [128, NH, 128])

    class Arena:
        def __init__(self, base, limit):
            self.base, self.limit, self.off, self.n = base, limit, base, 0

        def reset(self):
            self.off = self.base

        def alloc(self, shape, dt):
            nbytes = int(np.prod(shape[1:])) * (4 if dt == F32 else 2)
            nbytes = (nbytes + 63) // 64 * 64
            off = self.off
            assert off + nbytes <= self.limit, ("SBUF arena overflow", shape, off, nbytes, self.limit)
            self.off += nbytes
            self.n += 1
            return nc.alloc_sbuf_tensor_at("t%d" % self.n, list(shape), dt, offset=off)

    PERS = Arena(16512, 16512 + 60 * 1024)
    PH = Arena(16512 + 60 * 1024, 229376)

    cst = PERS.alloc([128, C_END], F32)
    ident = cst[:, C_ID:C_ID + 128]
    tri = cst[:, C_TRI:C_TRI + 128]
    maskA = cst[:, C_MA:C_MA + 128]
    maskB = cst[:, C_MB:C_MB + 128]
    identb = PERS.alloc([128, 128], BF16)
    onesb = PERS.alloc([128, 128], BF16)
    onesf = PERS.alloc([128, 128], F32)
    epsT = PERS.alloc([128, 2], F32)
    normwT = PERS.alloc([128, 128], F32)
    pscT = PERS.alloc([128, 16], F32)
    adabT = PERS.alloc([128, 192], F32)
    fcwT = PERS.alloc([128, 264], F32)
    dcwT = PERS.alloc([128, 256], F32)
    dnwbc = PERS.alloc([128, 128], F32)
    hb = PERS.alloc([128, 64], F32)
    MOD = PERS.alloc([128, 2, 96, NS + 1], F32)
    poolc = PERS.alloc([128, NCH, 15], F32)
    gatec = PERS.alloc([128, 2, NFF, 2], F32)
    convc = PERS.alloc([128, 64, 3], F32)
    RSTDY = PERS.alloc([128, TM + NS], F32)
    dnwcol = PERS.alloc([128, 2], F32)
    RING_BYTES = 8192
    NRING = 4
    ring_off = []
    for r in range(NRING):
        t = PERS.alloc([128, RING_BYTES // 2], BF16)
        ring_off.append(PERS.off - RING_BYTES)
    ring_buf = [Buf("ring%d" % r) for r in range(NRING)]
    ring_views = {}
    ring_next = [0]
    B_const = Buf("const")
    B_par = Buf("par")
    B_mod = Buf("mod")
    B_poolc, B_gatec, B_convc = Buf("poolc"), Buf("gatec"), Buf("convc")
    B_ry = Buf("rstdy")

    def wload(dram_view, shape, alt=None):
        if alt is None:
            offs, bufs, nxt, views, n = ring_off, ring_buf, ring_next, ring_views, NRING
        else:
            offs, bufs, nxt, views, n = alt
        r = nxt[0] % n
        nxt[0] += 1
        key = (r, tuple(shape))
        if key not in views:
            views[key] = nc.alloc_sbuf_tensor_at("rv%d_%d_%d" % (r, len(views), offs[r]), list(shape), BF16, offset=offs[r])
        v = views[key]
        S.dma("pool", v[:], dram_view, writes=[bufs[r]], nofence=True)
        return v, bufs[r]

    psb = [nc.alloc_psum_tensor("ps%d" % i, [128, 512], F32) for i in range(8)]
    psb16 = [p.bitcast(BF16) for p in psb]
    ps_buf = [Buf("ps%d" % i, excl=True) for i in range(8)]
    ps_rr = [0]
    ps_pool = [list(range(8))]

    def pbank():
        lst = ps_pool[0]
        i = lst[ps_rr[0] % len(lst)]
        ps_rr[0] += 1
        return i

    def mm(out, lhsT, rhs, start, stop, reads, writes):
        S.op("pe", lambda e: e.matmul(out, lhsT=lhsT, rhs=rhs, start=start, stop=stop), reads, writes)

    def tr(out, in_, idn, reads, writes):
        S.op("pe", lambda e: e.transpose(out=out, in_=in_, identity=idn), reads, writes)

    def act(out, in_, func, reads, writes, bias=None, scale=None, accum=None):
        kw = {}
        if bias is not None:
            kw["bias"] = bias
        if scale is not None:
            kw["scale"] = scale
        if accum is not None:
            kw["accum_out"] = accum
        S.op("act", lambda e: e.activation(out=out, in_=in_, func=func, **kw), reads, writes)

    def tt(eng, out, in0, in1, op, reads, writes):
        S.op(eng, lambda e: e.tensor_tensor(out=out, in0=in0, in1=in1, op=op), reads, writes)

    def ts(eng, out, in0, s1, op0, reads, writes, s2=None, op1=None):
        if op1 is None:
            S.op(eng, lambda e: e.tensor_scalar(out=out, in0=in0, scalar1=s1, scalar2=None, op0=op0), reads, writes)
        else:
            S.op(eng, lambda e: e.tensor_scalar(out=out, in0=in0, scalar1=s1, scalar2=s2, op0=op0, op1=op1), reads, writes)

    def stt(eng, out, in0, scalar, in1, op0, op1, reads, writes):
        S.op(eng, lambda e: e.scalar_tensor_tensor(out=out, in0=in0, scalar=scalar, in1=in1, op0=op0, op1=op1), reads, writes)

    def cp(eng, out, in_, reads, writes):
        if eng == "act":
            act(out, in_, AF.Copy, reads, writes)
        else:
            S.op(eng, lambda e: e.tensor_copy(out=out, in_=in_), reads, writes)

    def recip(out, in_, reads, writes):
        S.op("dve", lambda e: e.reciprocal(out=out, in_=in_), reads, writes)

    def memset(eng, ap, val, writes):
        S.op(eng, lambda e: e.memset(ap, val), [], writes)

    S.dma("sp", cst[:], consts_d[:, :], writes=[B_const])
    memset("dve", onesf[:], 1.0, [B_const])
    memset("dve", onesb[:], 1.0, [B_const])
    memset("dve", epsT[:, 0:1], EPS, [B_const])
    memset("dve", epsT[:, 1:2], 1.0, [B_const])
    cp("dve", identb[:], ident, [B_const], [B_const])
    memset("dve", poolc[:], 0.0, [B_poolc])
    memset("dve", gatec[:], 0.0, [B_gatec])
    memset("dve", convc[:], 0.0, [B_convc])
    eps = epsT[:, 0:1]
    one = epsT[:, 1:2]

    PH.reset()
    stg = PH.alloc([128, 9, 128], F32)
    B_stg = Buf("stg")
    prm = [
        (norm_w.rearrange("l i (c p) -> (l i c) p", p=128), 128, normwT[:, 0:128]),
        (pool_scale.rearrange("o (c p) -> (o c) p", p=128), 16, pscT[:, 0:16]),
        (ada_b[0].rearrange("(c p) -> c p", p=128), 96, adabT[:, 0:96]),
        (ada_b[1].rearrange("(c p) -> c p", p=128), 96, adabT[:, 96:192]),
        (ffn_conv_w.rearrange("l k (c p) -> (l k c) p", p=128)[0:88], 88, fcwT[:, 0:88]),
        (ffn_conv_w.rearrange("l k (c p) -> (l k c) p", p=128)[88:176], 88, fcwT[:, 88:176]),
        (ffn_conv_w.rearrange("l k (c p) -> (l k c) p", p=128)[176:264], 88, fcwT[:, 176:264]),
        (dn_conv_w.rearrange("o k (c p) -> (o k c) p", p=128)[0:128], 128, dcwT[:, 0:128]),
        (dn_conv_w.rearrange("o k (c p) -> (o k c) p", p=128)[128:256], 128, dcwT[:, 128:256]),
    ]
    for i, (src, n, dst) in enumerate(prm):
        S.dma("sp", stg[0:n, i, :], src, writes=[B_stg])
    for i, (src, n, dst) in enumerate(prm):
        b = pbank()
        tr(psb[b][:, 0:n], stg[0:n, i, :], ident[0:n, 0:n], [B_stg, B_const], [ps_buf[b]])
        cp("act", dst, psb[b][:, 0:n], [ps_buf[b]], [B_par])
    rowt = PH.alloc([1, 256], F32)
    B_row = Buf("row")
    S.dma("sp", rowt[0:1, 0:128], dn_norm_w[0:1, :], writes=[B_row])
    S.dma("sp", rowt[0:1, 128:160], dn_a_log[0:1, :], writes=[B_row])
    S.dma("sp", rowt[0:1, 160:192], dn_dt_bias[0:1, :], writes=[B_row])
    S.dma("sp", dnwcol[:, 0:1], dn_norm_w.rearrange("o p -> p o"), writes=[B_par])
    b = pbank()
    mm(psb[b][:, 0:192], onesf[0:1, :], rowt[0:1, 0:192], True, True, [B_row, B_const], [ps_buf[b]])
    cp("act", dnwbc[:], psb[b][:, 0:128], [ps_buf[b]], [B_par])
    act(hb[:, 0:32], psb[b][:, 128:160], AF.Exp, [ps_buf[b]], [B_par])
    ts("dve", hb[:, 0:32], hb[:, 0:32], -1.0, ALU.mult, [B_par], [B_par])
    cp("act", hb[:, 32:64], psb[b][:, 160:192], [ps_buf[b]], [B_par])

    ctm = PH.alloc([NS + 1, D], F32)
    csT = PH.alloc([128, NCH, NS + 1], BF16)
    B_ctm, B_csT = Buf("ctm"), Buf("csT")
    S.dma("sp", ctm[:], cin[:, :], writes=[B_ctm])
    act(ctm[:], ctm[:], AF.Silu, [B_ctm], [B_ctm])
    b = pbank()
    for c in range(NCH):
        tr(psb[b][:, c * 17:(c + 1) * 17], ctm[0:17, c * 128:(c + 1) * 128], ident[0:17, 0:17], [B_ctm, B_const], [ps_buf[b]])
    cp("act", csT[:].rearrange("p c n -> p (c n)"), psb[b][:, 0:NCH * 17], [ps_buf[b]], [B_csT])

    NPRO = 6
    PRO_BYTES = 16384
    pro_off = []
    for r in range(NPRO):
        PH.alloc([128, PRO_BYTES // 2], BF16)
        pro_off.append(PH.off - PRO_BYTES)
    pro_ring = (pro_off, [Buf("pring%d" % r) for r in range(NPRO)], [0], {}, NPRO)
    for l in range(2):
        wv = ada_w[l].rearrange("(k p) n -> p k n", p=128)
        modtm = [PH.alloc([NS + 1, 512], F32) for _ in range(2)]
        B_modtm = [Buf("modtm0"), Buf("modtm1")]
        for blk in range(24):
            w, wb = wload(wv[:, :, blk * 512:(blk + 1) * 512], [128, NCH, 512], alt=pro_ring)
            b = pbank()
            for k in range(NCH):
                mm(psb[b][0:NS + 1, 0:512], csT[:, k, :], w[:, k, :], k == 0, k == NCH - 1, [wb, B_csT], [ps_buf[b]])
            mt, bmt = modtm[blk % 2], B_modtm[blk % 2]
            cp("act", mt[:, :], psb[b][0:NS + 1, 0:512], [ps_buf[b]], [bmt])
            b2 = pbank()
            for jj in range(4):
                tr(psb[b2][:, jj * 17:(jj + 1) * 17], mt[:, jj * 128:(jj + 1) * 128], ident[0:NS + 1, 0:NS + 1], [bmt, B_const], [ps_buf[b2]])
            j0 = blk * 4
            for jj in range(4):
                ts("dve", MOD[:, l, j0 + jj, :], psb[b2][:, jj * 17:(jj + 1) * 17], adabT[:, l * 96 + j0 + jj:l * 96 + j0 + jj + 1],
                   ALU.add, [ps_buf[b2], B_par], [B_mod])
        for c in range(NCH):
            nw = lambda i: normwT[:, (l * 4 + i) * 16 + c:(l * 4 + i) * 16 + c + 1]
            ts("dve", MOD[:, l, 16 + c, :], MOD[:, l, 16 + c, :], 1.0, ALU.add, [B_mod, B_par], [B_mod], s2=nw(0), op1=ALU.mult)
            ts("dve", MOD[:, l, 32 + c, :], MOD[:, l, 32 + c, :], nw(1), ALU.mult, [B_mod, B_par], [B_mod])
            ts("dve", MOD[:, l, 64 + c, :], MOD[:, l, 64 + c, :], 1.0, ALU.add, [B_mod, B_par], [B_mod], s2=nw(2), op1=ALU.mult)
            ts("dve", MOD[:, l, 80 + c, :], MOD[:, l, 80 + c, :], nw(3), ALU.mult, [B_mod, B_par], [B_mod])

    if debug:
        mod_dbg = dscr("MOD_dbg", [128, 2 * 96 * (NS + 1)])
        S.dma("sp", mod_dbg[:, :], MOD[:].rearrange("p l j n -> p (l j n)"), reads=[B_mod])

    def modv(l, q, c):
        return MOD[:, l, q * 16 + c, NS:NS + 1]

    def mods(l, q, c):
        return MOD[:, l, q * 16 + c, 0:NS]

    def tiles_of(T):
        if T == TM:
            return [(0, 512), (512, 1024)]
        return [(0, 352), (352, 704), (704, T)]

    class Blk:
        pass

    def ss_open(T):
        nt = tiles_of(T)
        banks = [7 - i for i in range(len(nt))]
        ps_pool[0] = [i for i in range(8) if i not in banks]
        return banks

    def ss_add(banks, T, src, src_bufs, sq, B_sq, first, last):
        act(sq[:, 0:T], src, AF.Square, src_bufs, [B_sq])
        for (a, bnd), bk in zip(tiles_of(T), banks):
            mm(psb[bk][:, 0:bnd - a], onesb[:], sq[:, a:bnd], first, last, [B_sq, B_const], [ps_buf[bk]])

    def ss_close(banks, T, rstd, B_rstd, div):
        for (a, bnd), bk in zip(tiles_of(T), banks):
            act(rstd[:, a:bnd], psb[bk][:, 0:bnd - a], AF.Sqrt, [ps_buf[bk], B_const], [B_rstd], bias=eps, scale=1.0 / div)
        recip(rstd[:, 0:T], rstd[:, 0:T], [B_rstd], [B_rstd])
        ps_pool[0] = list(range(8))

    def make_h(l, qA, qB, xT, B_x, rstd, B_rstd, T, out, B_out, tmp, B_tmp):
        for c in range(NCH):
            stt("dve", tmp[:, 0:TM], xT[:, c, 0:TM], modv(l, qA, c), rstd[:, 0:TM], ALU.mult, ALU.mult,
                [B_x[c], B_rstd, B_mod], [B_tmp])
            act(out[:, c, 0:TM], tmp[:, 0:TM], AF.Identity, [B_tmp, B_mod], [B_out[c]], bias=modv(l, qB, c), scale=1.0)
            if T > TM:
                tt("dve", tmp[:, TM:T], xT[:, c, TM:T], rstd[:, TM:T], ALU.mult, [B_x[c], B_rstd], [B_tmp])
                tt("dve", tmp[:, TM:T], tmp[:, TM:T], mods(l, qA, c), ALU.mult, [B_tmp, B_mod], [B_tmp])
                tt("dve", out[:, c, TM:T], tmp[:, TM:T], mods(l, qB, c), ALU.add, [B_tmp, B_mod], [B_out[c]])

    def norm_of_x(xT, B_x, T, rstd, B_rstd, sq, B_sq):
        banks = ss_open(T)
        for c in range(NCH):
            ss_add(banks, T, xT[:, c, 0:T], [B_x[c]], sq, B_sq, c == 0, c == NCH - 1)
        ss_close(banks, T, rstd, B_rstd, float(D))

    def residual(l, qG, T, xT, B_x, rstd_y, B_ry, load_x):
        for c in range(NCH):
            ys = PHs["ystage"][c % 2]
            B_ys = PHs["B_ystage"][c % 2]
            S.dma("sp", ys[:, 0:T], Y_d[:, c, 0:T], writes=[B_ys])
            if load_x:
                S.dma("sp", xT[:, c, 0:T], X_d[:, c, 0:T], writes=[B_x[c]])
            stt("dve", ys[:, 0:TM], ys[:, 0:TM], modv(l, qG, c), rstd_y[:, 0:TM], ALU.mult, ALU.mult, [B_ys, B_ry, B_mod], [B_ys])
            if T > TM:
                tt("dve", ys[:, TM:T], ys[:, TM:T], rstd_y[:, TM:T], ALU.mult, [B_ys, B_ry], [B_ys])
                tt("dve", ys[:, TM:T], ys[:, TM:T], mods(l, qG, c), ALU.mult, [B_ys, B_mod], [B_ys])
            tt("dve", xT[:, c, 0:T], xT[:, c, 0:T], ys[:, 0:T], ALU.add, [B_x[c], B_ys], [B_x[c]])

    PHs = {}

    def out_stream(T, nk, wview_fn, wshape, srcT, B_src, rstd_out, B_ro):
        PHs_ystage = [PH.alloc([128, TMAX], F32) for _ in range(2)]
        B_ys = [Buf("ys0"), Buf("ys1")]
        sq = PH.alloc([128, TMAX], BF16)
        B_sq = Buf("sq")
        banks = ss_open(T)
        nt = tiles_of(T)
        nsplit = 2 if nk * 128 * 2 > RING_BYTES else 1
        kh = nk // nsplit
        for c in range(NCH):
            wparts = []
            for sp_ in range(nsplit):
                wparts.append(wload(wview_fn(c)[:, sp_ * kh:(sp_ + 1) * kh, :], [128, kh, 128]))
            ys, bys = PHs_ystage[c % 2], B_ys[c % 2]
            for (a, bnd) in nt:
                b = pbank()
                for k in range(nk):
                    w, wb = wparts[k // kh]
                    mm(psb[b][:, 0:bnd - a], w[:, k % kh, :], srcT[:, k, a:bnd], k == 0, k == nk - 1, [wb, B_src[k]], [ps_buf[b]])
                cp("act", ys[:, a:bnd], psb[b][:, 0:bnd - a], [ps_buf[b]], [bys])
            ss_add(banks, T, ys[:, 0:T], [bys], sq, B_sq, c == 0, c == NCH - 1)
            S.dma("sp", Y_d[:, c, 0:T], ys[:, 0:T], reads=[bys])
        ss_close(banks, T, rstd_out, B_ro, float(D))

    def run_block(bi):
        T = TM + NS if bi == 0 else TM
        row0 = 0 if bi == 0 else TM + NS
        nt = tiles_of(T)
        has_s = bi == 0
        last = bi == 1

        S.fence()
        PH.reset()
        xT = PH.alloc([128, NCH, TMAX], F32)
        B_x = [Buf("x%d" % c) for c in range(NCH)]
        rstd = PH.alloc([128, TMAX], F32)
        B_rstd = Buf("rstd")
        sq = PH.alloc([128, TMAX], BF16)
        B_sq = Buf("sq")
        tmp = PH.alloc([128, TMAX], F32)
        B_tmp = Buf("tmp")
        off_xst = PH.off
        xst = [PH.alloc([128, D], F32) for _ in range(2)]
        B_xst = [Buf("xst0"), Buf("xst1")]
        nrt = (T + 127) // 128
        for r in range(nrt):
            n = min(128, T - r * 128)
            st_, bst = xst[r % 2], B_xst[r % 2]
            S.dma("sp", st_[0:n, :], xin[row0 + r * 128:row0 + r * 128 + n, :], writes=[bst])
            for q4 in range(4):
                b = pbank()
                for jj in range(4):
                    c = q4 * 4 + jj
                    tr(psb[b][:, jj * 128:jj * 128 + n], st_[0:n, c * 128:(c + 1) * 128], ident[0:n, 0:n], [bst, B_const], [ps_buf[b]])
                for jj in range(4):
                    c = q4 * 4 + jj
                    cp("act" if jj % 2 == 0 else "dve", xT[:, c, r * 128:r * 128 + n], psb[b][:, jj * 128:jj * 128 + n], [ps_buf[b]], [B_x[c]])
        for c in range(NCH):
            S.dma("sp", X_d[:, c, 0:T], xT[:, c, 0:T], reads=[B_x[c]])
        if stop_after == "A1":
            return False
        norm_of_x(xT, B_x, T, rstd, B_rstd, sq, B_sq)
        if stop_after == "A2":
            return False
        make_h(0, 1, 0, xT, B_x, rstd, B_rstd, T, xT, B_x, tmp, B_tmp)
        h0, B_h0 = xT, B_x
        if stop_after == "A":
            return False
        S.fence()
        PH.off = off_xst

        if has_s:
            S.dma("sp", pool_s[:, 0:14, :], cpool_d[:, 1:15, :])
            hs_tm = PH.alloc([NS, D], F32)
            B_hs = Buf("hs")
            for q4 in range(4):
                b = pbank()
                for jj in range(4):
                    c = q4 * 4 + jj
                    tr(psb[b][0:NS, jj * 128:(jj + 1) * 128], h0[:, c, TM:T], ident, [B_h0[c], B_const], [ps_buf[b]])
                cp("act", hs_tm[:, q4 * 512:(q4 + 1) * 512], psb[b][0:NS, :], [ps_buf[b]], [B_hs])
            S.dma("sp", pool_s[:, 14, :], hs_tm[:], reads=[B_hs])
            cache = [PH.alloc([120, D], F32) for _ in range(2)]
            B_cache = Buf("cache")
            sel = PH.alloc([120, 2, 4, 16], F32)
            B_sel = Buf("sel")
            for k in range(2):
                S.dma("sp", cache[k][:], cpool_d[8 * k:8 * k + 8].rearrange("b r d -> (b r) d"), writes=[B_cache])
                S.dma("sp", sel[:, k, :, :], poolsel_d[k], writes=[B_sel])
        ext = [PH.alloc([128, 4, 15 + TM], F32) for _ in range(2)]
        B_ext = [Buf("ext0"), Buf("ext1")]
        dT = PH.alloc([128, 4, TMAX], BF16)
        B_dT = Buf("dT")
        ysg_one = PH.alloc([128, TMAX], F32)
        ysg = [ysg_one, ysg_one]
        B_ysg_one = Buf("ysg")
        B_ysg = [B_ysg_one, B_ysg_one]
        rstd_y = RSTDY
        banks = ss_open(T)
        for g in range(4):
            w_ = POOL_W[g]
            cs_ = slice(4 * g, 4 * g + 4)
            E0, E1 = ext
            cp("dve", E0[:, :, 0:15], poolc[:, cs_, :], [B_poolc], [B_ext[0]])
            cp("act", E0[:, :, 15:15 + TM], h0[:, cs_, 0:TM], [B_h0[4 * g + i] for i in range(4)], [B_ext[0]])
            cur, oth = 0, 1
            sh = 1
            while sh < w_:
                A_, Bt = ext[cur], ext[oth]
                tt("dve", Bt[:, :, sh:15 + TM], A_[:, :, sh:15 + TM], A_[:, :, 0:15 + TM - sh], ALU.add, [B_ext[cur]], [B_ext[oth]])
                cur, oth = oth, cur
                sh *= 2
            Sw = ext[cur]
            stt("dve", dT[:, :, 0:TM], Sw[:, :, 15:15 + TM], 1.0 / w_, h0[:, cs_, 0:TM], ALU.mult, ALU.subtract,
                [B_ext[cur]] + [B_h0[4 * g + i] for i in range(4)], [B_dT])
            if bi == 0:
                for i in range(4):
                    tt("dve", Sw[:, i, 15:30], Sw[:, i, 15:30], cst[:, C_FIX + g * 15:C_FIX + (g + 1) * 15], ALU.mult,
                       [B_ext[cur], B_const], [B_ext[cur]])
                tt("dve", dT[:, :, 0:15], Sw[:, :, 15:30], h0[:, cs_, 0:15], ALU.subtract,
                   [B_ext[cur]] + [B_h0[4 * g + i] for i in range(4)], [B_dT])
            if has_s:
                b = pbank()
                for i in range(4):
                    c = 4 * g + i
                    for k in range(2):
                        mm(psb[b][:, i * 16:(i + 1) * 16], cache[k][:, c * 128:(c + 1) * 128], sel[:, k, g, :], k == 0, k == 1,
                           [B_cache, B_sel], [ps_buf[b]])
                for i in range(4):
                    c = 4 * g + i
                    stt("dve", dT[:, i, TM:T], h0[:, c, TM:T], 1.0 / w_ - 1.0, psb[b][:, i * 16:(i + 1) * 16], ALU.mult, ALU.add,
                        [B_h0[c], ps_buf[b]], [B_dT])
            w, wb = wload(pool_w[0, g].rearrange("(k p) n -> p k n", p=128), [128, 4, 512])
            for i in range(4):
                c = 4 * g + i
                ys, bys = ysg[c % 2], B_ysg[c % 2]
                for (a, bnd) in nt:
                    b = pbank()
                    for k in range(4):
                        mm(psb[b][:, 0:bnd - a], w[:, k, i * 128:(i + 1) * 128], dT[:, k, a:bnd], k == 0, k == 3, [wb, B_dT], [ps_buf[b]])
                    act(ys[:, a:bnd], psb[b][:, 0:bnd - a], AF.Copy, [ps_buf[b], B_par], [bys], scale=pscT[:, c:c + 1])
                ss_add(banks, T, ys[:, 0:T], [bys], sq, B_sq, c == 0, c == NCH - 1)
                S.dma("sp", Y_d[:, c, 0:T], ys[:, 0:T], reads=[bys])
        ss_close(banks, T, rstd_y, B_ry, float(D))
        cp("dve", poolc[:], h0[:, :, TM - 15:TM], list(B_h0), [B_poolc])
        if last:
            pp = PH.alloc([16, D], F32)
            B_pp = Buf("pp")
            for q4 in range(4):
                b = pbank()
                for jj in range(4):
                    c = q4 * 4 + jj
                    tr(psb[b][0:16, jj * 128:(jj + 1) * 128], h0[:, c, TM - 16:TM], ident, [B_h0[c], B_const], [ps_buf[b]])
                cp("act", pp[:, q4 * 512:(q4 + 1) * 512], psb[b][0:16, :], [ps_buf[b]], [B_pp])
            S.dma("sp", pool_p[:, :], pp[:], reads=[B_pp])
        if stop_after == "pool":
            return False

        HT_BYTES = NCH * TMAX * 2

        def rn_phase(lG, qG, lH, qA, qB, final=False):
            S.fence()
            PH.reset()
            hT = PH.alloc([128, NCH, TMAX], BF16)
            B_h = [Buf("h%d" % c) for c in range(NCH)]
            xT = PH.alloc([128, NCH, TMAX], F32)
            B_x = [Buf("x%d" % c) for c in range(NCH)]
            PHs["ystage"] = [PH.alloc([128, TMAX], F32) for _ in range(2)]
            PHs["B_ystage"] = [Buf("ys0"), Buf("ys1")]
            residual(lG, qG, T, xT, B_x, RSTDY, B_ry, True)
            if final:
                ost = [PH.alloc([128, D], F32) for _ in range(2)]
                B_ost = [Buf("ost0"), Buf("ost1")]
                for r in range(nrt):
                    n = min(128, T - r * 128)
                    o_, bo = ost[r % 2], B_ost[r % 2]
                    for q4 in range(4):
                        b = pbank()
                        for jj in range(4):
                            c = q4 * 4 + jj
                            tr(psb[b][0:n, jj * 128:(jj + 1) * 128], xT[:, c, r * 128:r * 128 + n], ident, [B_x[c], B_const], [ps_buf[b]])
                        cp("act" if q4 % 2 == 0 else "dve", o_[0:n, q4 * 512:(q4 + 1) * 512], psb[b][0:n, :], [ps_buf[b]], [bo])
                    if r < 8:
                        S.dma("sp", y_p[bi * TM + r * 128:bi * TM + (r + 1) * 128, :], o_[:, :], reads=[bo])
                    else:
                        S.dma("sp", y_s[:, :], o_[0:NS, :], reads=[bo])
                return None, None
            for c in range(NCH):
                S.dma("sp", X_d[:, c, 0:T], xT[:, c, 0:T], reads=[B_x[c]])
            rstd = PH.alloc([128, TMAX], F32)
            B_rstd = Buf("rstd")
            sq = PH.alloc([128, TMAX], BF16)
            B_sq = Buf("sq")
            tmp = PH.alloc([128, TMAX], F32)
            B_tmp = Buf("tmp")
            norm_of_x(xT, B_x, T, rstd, B_rstd, sq, B_sq)
            make_h(lH, qA, qB, xT, B_x, rstd, B_rstd, T, hT, B_h, tmp, B_tmp)
            return hT, B_h

        def ffn_phase(l, hT, B_h):
            S.fence()
            PH.reset()
            PH.off += HT_BYTES
            aT = PH.alloc([128, NFF, TMAX], BF16)
            B_a = [Buf("a%d" % c) for c in range(NFF)]
            ge = PH.alloc([128, 2 + TMAX], F32)
            B_ge = Buf("ge")
            tm = PH.alloc([128, TMAX], F32)
            B_tm = Buf("tm")
            if has_s:
                gcT = PH.alloc([128, NFF, 2 * NS], F32)
                gsT = PH.alloc([128, NFF, NS], F32)
                B_gcT, B_gsT = Buf("gcT"), Buf("gsT")
                S.dma("sp", ffn_s[l, :, 0, :], cffn_d[l, :, 1, :])
                gst = PH.alloc([2 * NS, 1408], F32)
                B_gst = Buf("gst")
                for pc in range(4):
                    for r_ in range(2):
                        S.dma("sp", gst[r_ * NS:(r_ + 1) * NS, :], cffn_d[l, :, r_, pc * 1408:(pc + 1) * 1408], writes=[B_gst])
                    for q in range(3):
                        b = pbank()
                        nn = 4 if q < 2 else 3
                        for jj in range(nn):
                            tr(psb[b][:, jj * 32:(jj + 1) * 32], gst[:, (q * 4 + jj) * 128:(q * 4 + jj + 1) * 128], ident[0:32, 0:32],
                               [B_gst, B_const], [ps_buf[b]])
                        c0 = pc * 11 + q * 4
                        cp("act", gcT[:, c0:c0 + nn, :], psb[b][:, 0:nn * 32].rearrange("p (c n) -> p c n", n=32), [ps_buf[b]], [B_gcT])
            wg_v = ffn_w_gate[l].rearrange("(k p) n -> p k n", p=128)
            wu_v = ffn_w_up[l].rearrange("(k p) n -> p k n", p=128)
            fw = lambda k, ch: fcwT[:, (l * 3 + k) * NFF + ch:(l * 3 + k) * NFF + ch + 1]
            for blk in range(22):
                wg, wgb = wload(wg_v[:, :, blk * 256:(blk + 1) * 256], [128, NCH, 256])
                wu, wub = wload(wu_v[:, :, blk * 256:(blk + 1) * 256], [128, NCH, 256])
                for jj in range(2):
                    ch = blk * 2 + jj
                    for (a, bnd) in nt:
                        b = pbank()
                        for k in range(NCH):
                            mm(psb[b][:, 0:bnd - a], wg[:, k, jj * 128:(jj + 1) * 128], hT[:, k, a:bnd], k == 0, k == NCH - 1,
                               [wgb, B_h[k]], [ps_buf[b]])
                        cp("act", ge[:, 2 + a:2 + bnd], psb[b][:, 0:bnd - a], [ps_buf[b]], [B_ge])
                    cp("dve", ge[:, 0:2], gatec[:, l, ch, :], [B_gatec], [B_ge])
                    cp("dve", gatec[:, l, ch, :], ge[:, TM:TM + 2], [B_ge], [B_gatec])
                    ts("dve", tm[:, 0:TM], ge[:, 2:2 + TM], fw(2, ch), ALU.mult, [B_ge, B_par], [B_tm])
                    stt("dve", tm[:, 0:TM], ge[:, 1:1 + TM], fw(1, ch), tm[:, 0:TM], ALU.mult, ALU.add, [B_ge, B_par, B_tm], [B_tm])
                    stt("dve", tm[:, 0:TM], ge[:, 0:TM], fw(0, ch), tm[:, 0:TM], ALU.mult, ALU.add, [B_ge, B_par, B_tm], [B_tm])
                    if has_s:
                        cp("dve", gsT[:, ch, :], ge[:, 2 + TM:2 + T], [B_ge], [B_gsT])
                        ts("dve", tm[:, TM:T], ge[:, 2 + TM:2 + T], fw(2, ch), ALU.mult, [B_ge, B_par], [B_tm])
                        stt("dve", tm[:, TM:T], gcT[:, ch, NS:2 * NS], fw(1, ch), tm[:, TM:T], ALU.mult, ALU.add, [B_gcT, B_par, B_tm], [B_tm])
                        stt("dve", tm[:, TM:T], gcT[:, ch, 0:NS], fw(0, ch), tm[:, TM:T], ALU.mult, ALU.add, [B_gcT, B_par, B_tm], [B_tm])
                    act(tm[:, 0:T], tm[:, 0:T], AF.Silu, [B_tm], [B_tm])
                    for (a, bnd) in nt:
                        b = pbank()
                        for k in range(NCH):
                            mm(psb[b][:, 0:bnd - a], wu[:, k, jj * 128:(jj + 1) * 128], hT[:, k, a:bnd], k == 0, k == NCH - 1,
                               [wub, B_h[k]], [ps_buf[b]])
                        tt("dve", aT[:, ch, a:bnd], tm[:, a:bnd], psb[b][:, 0:bnd - a], ALU.mult, [B_tm, ps_buf[b]], [B_a[ch]])
            if has_s:
                orow, B_orow = gst[0:NS, :], B_gst
            else:
                orow = PH.alloc([NS, 1408], F32)
                B_orow = Buf("orow")
            for pc in range(4):
                if has_s:
                    for q in range(3):
                        b = pbank()
                        nn = 4 if q < 2 else 3
                        for jj in range(nn):
                            tr(psb[b][0:NS, jj * 128:(jj + 1) * 128], gsT[:, pc * 11 + q * 4 + jj, :], ident, [B_gsT, B_const], [ps_buf[b]])
                        cp("act", orow[:, q * 512:q * 512 + nn * 128], psb[b][0:NS, 0:nn * 128], [ps_buf[b]], [B_orow])
                    S.dma("sp", ffn_s[l, :, 1, pc * 1408:(pc + 1) * 1408], orow[:, :], reads=[B_orow])
                if last:
                    for q in range(3):
                        b = pbank()
                        nn = 4 if q < 2 else 3
                        for jj in range(nn):
                            tr(psb[b][0:2, jj * 128:(jj + 1) * 128], gatec[:, l, pc * 11 + q * 4 + jj, :], ident, [B_gatec, B_const], [ps_buf[b]])
                        cp("act", orow[0:2, q * 512:q * 512 + nn * 128], psb[b][0:2, 0:nn * 128], [ps_buf[b]], [B_orow])
                    S.dma("sp", ffn_p[l, :, pc * 1408:(pc + 1) * 1408], orow[0:2, :], reads=[B_orow])
            wd_v = ffn_w_down[l].rearrange("(k p) n -> p k n", p=128)
            S.fence()
            PH.off = PH.base
            out_stream(T, NFF, lambda c: wd_v[:, :, c * 128:(c + 1) * 128], [128, NFF, 128], aT, B_a, RSTDY, B_ry)

        def delta_phase(hT, B_h):
            S.fence()
            PH.reset()
            PH.off += HT_BYTES
            ntile = 9 if has_s else 8
            win = dn_w_in[0].rearrange("(k p) n -> p k n", p=128)
            names = ("BETA", "G", "GC", "EGC", "KTS", "NBE", "EGL")
            SC = {nm: PH.alloc([128, 9, NH], F32) for nm in names}
            B_sc = Buf("scal")
            Sf = PH.alloc([128, NH, 128], F32)
            Sb = PH.alloc([128, NH, 128], BF16)
            B_S = [Buf("S%d" % h) for h in range(NH)]
            B_Sb = [Buf("Sb%d" % h) for h in range(NH)]
            if bi == 0:
                memset("dve", Sf[:], 0.0, B_S)
                memset("dve", Sb[:], 0.0, B_Sb)
            else:
                S.dma("sp", Sf[:], SD_d[:, :, :], writes=B_S)
                cp("act", Sb[:], Sf[:], B_S, B_Sb)
            t64 = PH.alloc([128, 64], F32)
            B_t64 = Buf("t64")
            w, wb = wload(win[:, :, 12288:12352], [128, NCH, 64])
            for n in range(ntile):
                m = 128 if n < 8 else NS
                a0 = n * 128
                b = pbank()
                for k in range(NCH):
                    mm(psb[b][0:m, 0:64], hT[:, k, a0:a0 + m], w[:, k, :], k == 0, k == NCH - 1, [wb, B_h[k]], [ps_buf[b]])
                act(SC["BETA"][0:m, n, :], psb[b][0:m, 0:32], AF.Sigmoid, [ps_buf[b]], [B_sc])
                tt("dve", t64[0:m, 0:32], psb[b][0:m, 32:64], hb[0:m, 32:64], ALU.add, [ps_buf[b], B_par], [B_t64])
                act(t64[0:m, 0:32], t64[0:m, 0:32], AF.Exp, [B_t64], [B_t64])
                act(t64[0:m, 0:32], t64[0:m, 0:32], AF.Ln, [B_t64, B_const], [B_t64], bias=one[0:m, :], scale=1.0)
                tt("dve", SC["G"][0:m, n, :], t64[0:m, 0:32], hb[0:m, 0:32], ALU.mult, [B_t64, B_par], [B_sc])
            for n in range(8):
                b = pbank()
                mm(psb[b][:, 0:32], tri, SC["G"][:, n, :], True, True, [B_const, B_sc], [ps_buf[b]])
                mm(psb[b][:, 32:64], onesf[:], SC["G"][:, n, :], True, True, [B_const, B_sc], [ps_buf[b]])
                cp("act", SC["GC"][:, n, :], psb[b][:, 0:32], [ps_buf[b]], [B_sc])
                act(SC["EGC"][:, n, :], psb[b][:, 0:32], AF.Exp, [ps_buf[b]], [B_sc])
                act(SC["EGL"][:, n, :], psb[b][:, 32:64], AF.Exp, [ps_buf[b]], [B_sc])
                tt("dve", SC["KTS"][:, n, :], psb[b][:, 32:64], SC["GC"][:, n, :], ALU.subtract, [ps_buf[b], B_sc], [B_sc])
                act(SC["KTS"][:, n, :], SC["KTS"][:, n, :], AF.Exp, [B_sc], [B_sc])
                stt("dve", SC["NBE"][:, n, :], SC["BETA"][:, n, :], -1.0, SC["EGC"][:, n, :], ALU.mult, ALU.mult, [B_sc], [B_sc])
            if has_s:
                S.dma("sp", conv_s[:, 0:2, :], sconv_d[:, 1:3, :])
                act(SC["EGC"][0:NS, 8, :], SC["G"][0:NS, 8, :], AF.Exp, [B_sc], [B_sc])
                rhsb = PH.alloc([NS, NH, NS], F32)
                B_rhsb = Buf("rhsb")
                BETAbc = PH.alloc([128, NH, NS], F32)
                EGbc = PH.alloc([128, NH, NS], F32)
                B_bc = Buf("bc")
                for (src, dst) in ((SC["BETA"], BETAbc), (SC["EGC"], EGbc)):
                    for h in range(NH):
                        ts("dve", rhsb[:, h, :], ident[0:NS, 0:NS], src[0:NS, 8, h:h + 1], ALU.mult, [B_const, B_sc], [B_rhsb])
                    b = pbank()
                    mm(psb[b][:, 0:512], onesf[0:NS, :], rhsb[:].rearrange("p h b -> p (h b)"), True, True, [B_const, B_rhsb], [ps_buf[b]])
                    cp("act", dst[:].rearrange("p h b -> p (h b)"), psb[b][:, 0:512], [ps_buf[b]], [B_bc])
            ge = PH.alloc([128, 3 + TMAX], F32)
            cv = PH.alloc([128, TMAX], F32)
            B_ge, B_cv = Buf("ge"), Buf("cv")
            sq = PH.alloc([128, TMAX], BF16)
            rq = ge
            B_sq, B_rq = Buf("sq"), B_ge
            qTn = PH.alloc([128, TMAX], BF16)
            kTn = PH.alloc([128, TMAX], BF16)
            B_q, B_k = Buf("qTn"), Buf("kTn")
            vT = PH.alloc([128, 2, TMAX], BF16)
            vTs = PH.alloc([128, 2, NS], F32)
            B_v = [Buf("v0"), Buf("v1")]
            zs = PH.alloc([128, 9, 256], BF16)
            B_zs = Buf("zs")
            onTg = PH.alloc([128, 2, TMAX], BF16)
            B_on = [Buf("on0"), Buf("on1")]
            mk = lambda dt: PH.alloc([128, 128], dt)
            mk4 = lambda dt: PH.alloc([128, 4, 128], dt)
            KT = PH.alloc([128, 16, 128], BF16)
            PQ = PH.alloc([128, 16, 128], BF16)
            YS = PH.alloc([128, 16, 128], BF16)
            VB = PH.alloc([128, 16, 128], BF16)
            B_KTq = [Buf("KT%d" % q) for q in range(4)]
            B_PQq = [Buf("PQ%d" % q) for q in range(4)]
            B_YSq = [Buf("YS%d" % q) for q in range(4)]
            B_VBq = [Buf("VB%d" % q) for q in range(4)]
            QCH = [(mk4(F32), mk4(F32), [mk4(BF16), mk4(BF16)], [mk4(BF16), mk4(BF16)], mk4(BF16)) for _ in range(2)]
            B_QCH = [(Buf("INA"), Buf("INB"), [Buf("LP0"), Buf("LP1")], [Buf("UP0"), Buf("UP1")], Buf("YW")) for _ in range(2)]
            MA4, MB4, I4 = mk4(F32), mk4(F32), mk4(BF16)
            ntri = mk(F32)
            for j in range(4):
                cp("dve", MA4[:, j, :], maskA, [B_const], [B_const])
                cp("dve", MB4[:, j, :], maskB, [B_const], [B_const])
                cp("dve", I4[:, j, :], identb[:], [B_const], [B_const])
            ts("dve", ntri[:], tri, -1.0, ALU.mult, [B_const], [B_const])
            SCN = [(mk(BF16), mk(BF16), mk(F32), mk(F32), mk(BF16), PH.alloc([128, 2], F32)) for _ in range(2)]
            B_SCN = [(Buf("R"), Buf("vn"), Buf("t1"), Buf("om"), Buf("onm"), Buf("ssn")) for _ in range(2)]

            def interleave(gens):
                gens = list(gens)
                while gens:
                    for g_ in list(gens):
                        try:
                            next(g_)
                        except StopIteration:
                            gens.remove(g_)

            if has_s:
                sctm = PH.alloc([48, 128], F32)
                scT = PH.alloc([128, 48], F32)
                B_sctm, B_scT = Buf("sctm"), Buf("scT")
                rsm = PH.alloc([NS, 128], F32)
                B_rsm = Buf("rsm")
                kqs = PH.alloc([128, NS, 2], F32)
                B_kqs = Buf("kqs")
                qkbc = PH.alloc([128, NS], F32)
                zsT = PH.alloc([128, 2, NS], F32)
                B_qkbc, B_zsT = Buf("qkbc"), Buf("zsT")
                Ss = PH.alloc([128, 8, 128], F32)
                B_Ss = Buf("Ss")
                Sn, B_Sn = Ss, B_Ss
                sm = {nm: PH.alloc([128, 8], F32) for nm in ("t", "vn", "o", "o2", "sq", "rn")}
                B_sm = Buf("sm")
                vntm = PH.alloc([8, 128], F32)
                B_vntm = Buf("vntm")
                prod = PH.alloc([128, NS], F32)
            dw = lambda k, cq: dcwT[:, k * 64 + cq:k * 64 + cq + 1]
            for gq in range(16):
                for ci in range(4):
                    cq = (gq, 16 + gq, 32 + 2 * gq, 33 + 2 * gq)[ci]
                    wci, wcib = wload(win[:, :, cq * 128:(cq + 1) * 128], [128, NCH, 128])
                    for (a, bnd) in nt:
                        b = pbank()
                        for k in range(NCH):
                            mm(psb[b][:, 0:bnd - a], wci[:, k, :], hT[:, k, a:bnd], k == 0, k == NCH - 1, [wcib, B_h[k]], [ps_buf[b]])
                        cp("act", ge[:, 3 + a:3 + bnd], psb[b][:, 0:bnd - a], [ps_buf[b]], [B_ge])
                    cp("dve", ge[:, 0:3], convc[:, cq, :], [B_convc], [B_ge])
                    cp("dve", convc[:, cq, :], ge[:, TM:TM + 3], [B_ge], [B_convc])
                    ts("dve", cv[:, 0:TM], ge[:, 3:3 + TM], dw(3, cq), ALU.mult, [B_ge, B_par], [B_cv])
                    for k in (2, 1, 0):
                        stt("dve", cv[:, 0:TM], ge[:, k:k + TM], dw(k, cq), cv[:, 0:TM], ALU.mult, ALU.add, [B_ge, B_par, B_cv], [B_cv])
                    if has_s:
                        for r_ in range(3):
                            S.dma("sp", sctm[r_ * NS:(r_ + 1) * NS, :], sconv_d[:, r_, cq * 128:(cq + 1) * 128], writes=[B_sctm])
                        b = pbank()
                        tr(psb[b][:, 0:48], sctm[:, :], ident[0:48, 0:48], [B_sctm, B_const], [ps_buf[b]])
                        cp("act", scT[:, :], psb[b][:, 0:48], [ps_buf[b]], [B_scT])
                        ts("dve", cv[:, TM:T], ge[:, 3 + TM:3 + T], dw(3, cq), ALU.mult, [B_ge, B_par], [B_cv])
                        for k in (2, 1, 0):
                            stt("dve", cv[:, TM:T], scT[:, k * NS:(k + 1) * NS], dw(k, cq), cv[:, TM:T], ALU.mult, ALU.add,
                                [B_scT, B_par, B_cv], [B_cv])
                        b = pbank()
                        tr(psb[b][0:NS, 0:128], ge[:, 3 + TM:3 + T], ident, [B_ge, B_const], [ps_buf[b]])
                        cp("act", rsm[:, :], psb[b][0:NS, 0:128], [ps_buf[b]], [B_rsm])
                        S.dma("sp", conv_s[:, 2, cq * 128:(cq + 1) * 128], rsm[:, :], reads=[B_rsm])
                    if ci < 2:
                        act(cv[:, 0:T], cv[:, 0:T], AF.Silu, [B_cv], [B_cv])
                        act(sq[:, 0:T], cv[:, 0:T], AF.Square, [B_cv], [B_sq])
                        for (a, bnd) in nt:
                            b = pbank()
                            mm(psb[b][:, 0:bnd - a], onesb[:], sq[:, a:bnd], True, True, [B_sq, B_const], [ps_buf[b]])
                            act(rq[:, a:bnd], psb[b][:, 0:bnd - a], AF.Sqrt, [ps_buf[b], B_const], [B_rq], bias=eps, scale=1.0)
                        recip(rq[:, 0:T], rq[:, 0:T], [B_rq], [B_rq])
                        if ci == 0:
                            stt("dve", qTn[:, 0:T], cv[:, 0:T], 128.0 ** -0.5, rq[:, 0:T], ALU.mult, ALU.mult, [B_cv, B_rq], [B_q])
                            if has_s:
                                stt("dve", kqs[:, :, 1], cv[:, TM:T], 128.0 ** -0.5, rq[:, TM:T], ALU.mult, ALU.mult, [B_cv, B_rq], [B_kqs])
                        else:
                            tt("dve", kTn[:, 0:T], cv[:, 0:T], rq[:, 0:T], ALU.mult, [B_cv, B_rq], [B_k])
                            if has_s:
                                tt("dve", kqs[:, :, 0], cv[:, TM:T], rq[:, TM:T], ALU.mult, [B_cv, B_rq], [B_kqs])
                    else:
                        act(vT[:, ci - 2, 0:T], cv[:, 0:T], AF.Silu, [B_cv], [B_v[ci - 2]])
                        if has_s:
                            act(vTs[:, ci - 2, :], cv[:, TM:T], AF.Silu, [B_cv], [B_v[ci - 2]])
                wz, wzb = wload(win[:, :, 8192 + gq * 256:8192 + (gq + 1) * 256], [128, NCH, 256])
                for n in range(ntile):
                    m = 128 if n < 8 else NS
                    a0 = n * 128
                    b = pbank()
                    for k in range(NCH):
                        mm(psb[b][0:m, 0:256], hT[:, k, a0:a0 + m], wz[:, k, :], k == 0, k == NCH - 1, [wzb, B_h[k]], [ps_buf[b]])
                    act(zs[0:m, n, :], psb[b][0:m, 0:256], AF.Silu, [ps_buf[b]], [B_zs])
                def pre_chain(c):
                    bks = [4 * c + j for j in range(4)]
                    rr = [0]

                    def nb():
                        b_ = bks[rr[0] % 4]
                        rr[0] += 1
                        return b_

                    v4 = lambda b_: psb[b_][:, 0:512].rearrange("p (j n) -> p j n", j=4)
                    v4h = lambda b_: psb16[b_][:, 0:512].rearrange("p (j n) -> p j n", j=4)
                    INA, INB, LP, UP, YW = QCH[c]
                    B_INA, B_INB, B_LP, B_UP, B_YW = B_QCH[c]
                    for q in range(c, 4, 2):
                        prs = [(4 * q + j, 2 * q + j // 2, j % 2) for j in range(4)]
                        jb = lambda j: slice(j * 128, (j + 1) * 128)
                        ck = lambda n: slice(n * 128, (n + 1) * 128)
                        b0 = nb()
                        for jn in range(2):
                            tr(psb16[b0][:, jb(jn)], kTn[:, ck(2 * q + jn)], identb[:], [B_k, B_const], [ps_buf[b0]])
                        for j, (p, n, i) in enumerate(prs):
                            h = 2 * gq + i
                            act(KT[:, p, :], psb16[b0][:, jb(j // 2)], AF.Copy, [ps_buf[b0], B_sc], [B_KTq[q]], scale=SC["KTS"][:, n, h:h + 1])
                        bd = nb()
                        for j, (p, n, i) in enumerate(prs):
                            h = 2 * gq + i
                            gcol = SC["G"][:, n, h:h + 1].broadcast_to([128, 128])
                            mm(psb[bd][:, jb(j)], gcol, tri, True, False, [B_sc, B_const], [ps_buf[bd]])
                            mm(psb[bd][:, jb(j)], ntri[:], gcol, False, True, [B_sc, B_const], [ps_buf[bd]])
                        yield
                        tt("dve", INA[:], v4(bd), MA4[:], ALU.add, [ps_buf[bd], B_const], [B_INA])
                        tt("dve", INB[:], v4(bd), MB4[:], ALU.add, [ps_buf[bd], B_const], [B_INB])
                        bkk = nb()
                        for j, (p, n, i) in enumerate(prs):
                            mm(psb[bkk][:, jb(j)], kTn[:, ck(n)], kTn[:, ck(n)], True, True, [B_k], [ps_buf[bkk]])
                        bqk = nb()
                        for j, (p, n, i) in enumerate(prs):
                            mm(psb[bqk][:, jb(j)], kTn[:, ck(n)], qTn[:, ck(n)], True, True, [B_k, B_q], [ps_buf[bqk]])
                        yield
                        act(INA[:], INA[:], AF.Exp, [B_INA], [B_INA], scale=-1.0)
                        act(INB[:], INB[:], AF.Exp, [B_INB], [B_INB])
                        yield
                        for j, (p, n, i) in enumerate(prs):
                            h = 2 * gq + i
                            stt("dve", LP[0][:, j, :], psb[bkk][:, jb(j)], SC["BETA"][:, n, h:h + 1], INA[:, j, :], ALU.mult, ALU.mult,
                                [ps_buf[bkk], B_sc, B_INA], [B_LP[0]])
                        tt("dve", PQ[:, 4 * q:4 * q + 4, :], v4(bqk), INB[:], ALU.mult, [ps_buf[bqk], B_INB], [B_PQq[q]])
                        yield
                        bu = nb()
                        for j in range(4):
                            tr(psb16[bu][:, jb(j)], LP[0][:, j, :], identb[:], [B_LP[0], B_const], [ps_buf[bu]])
                        yield
                        cp("act", UP[0][:], v4h(bu), [ps_buf[bu]], [B_UP[0]])
                        tt("dve", YW[:], I4[:], v4h(bu), ALU.subtract, [ps_buf[bu], B_const], [B_YW])
                        yield
                        cur = 0
                        for lev in range(6):
                            nx = 1 - cur
                            bP = nb()
                            for j in range(4):
                                mm(psb[bP][:, jb(j)], UP[cur][:, j, :], LP[cur][:, j, :], True, True, [B_UP[cur], B_LP[cur]], [ps_buf[bP]])
                            if lev < 5:
                                bU = nb()
                                for j in range(4):
                                    mm(psb[bU][:, jb(j)], LP[cur][:, j, :], UP[cur][:, j, :], True, True, [B_UP[cur], B_LP[cur]], [ps_buf[bU]])
                            yield
                            cp("act", LP[nx][:], v4(bP), [ps_buf[bP]], [B_LP[nx]])
                            if lev < 5:
                                cp("dve", UP[nx][:], v4(bU), [ps_buf[bU]], [B_UP[nx]])
                            yield
                            bY = nb()
                            for j in range(4):
                                mm(psb[bY][:, jb(j)], LP[nx][:, j, :], YW[:, j, :], True, True, [B_LP[nx], B_YW], [ps_buf[bY]])
                            yield
                            if lev < 5:
                                tt("dve", YW[:], YW[:], v4(bY), ALU.add, [B_YW, ps_buf[bY]], [B_YW])
                            else:
                                tt("dve", YS[:, 4 * q:4 * q + 4, :], YW[:], v4(bY), ALU.add, [B_YW, ps_buf[bY]], [B_YSq[q]])
                            cur = nx
                            yield
                        bv = nb()
                        for j, (p, n, i) in enumerate(prs):
                            tr(psb16[bv][:, jb(j)], vT[:, i, ck(n)], identb[:], [B_v[i], B_const], [ps_buf[bv]])
                        yield
                        for j, (p, n, i) in enumerate(prs):
                            h = 2 * gq + i
                            if j % 2 == 0:
                                act(VB[:, p, :], psb16[bv][:, jb(j)], AF.Copy, [ps_buf[bv], B_sc], [B_VBq[q]], scale=SC["BETA"][:, n, h:h + 1])
                            else:
                                ts("dve", VB[:, p, :], psb16[bv][:, jb(j)], SC["BETA"][:, n, h:h + 1], ALU.mult, [ps_buf[bv], B_sc], [B_VBq[q]])
                        yield

                def scan_chain(i):
                    bk = [4 * i + j for j in range(4)]
                    h = 2 * gq + i
                    Rm, vn, t1, om, onm, ssn = SCN[i]
                    B_R, B_vn, B_t1, B_om, B_onm, B_ssn = B_SCN[i]
                    for n in range(8):
                        p = 2 * n + i
                        c0, c1 = n * 128, (n + 1) * 128
                        sc = (lambda n: (lambda nm: SC[nm][:, n, h:h + 1]))(n)
                        mm(psb[bk[0]][:, 0:128], kTn[:, c0:c1], Sb[:, h, :], True, True, [B_k, B_Sb[h]], [ps_buf[bk[0]]])
                        mm(psb[bk[2]][:, 0:128], qTn[:, c0:c1], Sb[:, h, :], True, True, [B_q, B_Sb[h]], [ps_buf[bk[2]]])
                        yield
                        stt("dve", Rm[:], psb[bk[0]][:, 0:128], sc("NBE"), VB[:, p, :], ALU.mult, ALU.add, [ps_buf[bk[0]], B_sc, B_VBq[p // 4]], [B_R])
                        act(t1[:], psb[bk[2]][:, 0:128], AF.Copy, [ps_buf[bk[2]], B_sc], [B_t1], scale=sc("EGC"))
                        yield
                        mm(psb[bk[1]][:, 0:128], YS[:, p, :], Rm[:], True, True, [B_YSq[p // 4], B_R], [ps_buf[bk[1]]])
                        yield
                        cp("act", vn[:], psb[bk[1]][:, 0:128], [ps_buf[bk[1]]], [B_vn])
                        yield
                        mm(psb[bk[3]][:, 0:128], PQ[:, p, :], vn[:], True, True, [B_PQq[p // 4], B_vn], [ps_buf[bk[3]]])
                        mm(psb[bk[0]][:, 0:128], KT[:, p, :], vn[:], True, True, [B_KTq[p // 4], B_vn], [ps_buf[bk[0]]])
                        yield
                        tt("dve", om[:], t1[:], psb[bk[3]][:, 0:128], ALU.add, [B_t1, ps_buf[bk[3]]], [B_om])
                        stt("dve", Sf[:, h, :], Sf[:, h, :], sc("EGL"), psb[bk[0]][:, 0:128], ALU.mult, ALU.add, [B_S[h], B_sc, ps_buf[bk[0]]], [B_S[h]])
                        yield
                        cp("act", Sb[:, h, :], Sf[:, h, :], [B_S[h]], [B_Sb[h]])
                        act(t1[:], om[:], AF.Square, [B_om], [B_t1, B_ssn], accum=ssn[:, 0:1])
                        yield
                        act(ssn[:, 0:1], ssn[:, 0:1], AF.Sqrt, [B_ssn, B_const], [B_ssn], bias=eps, scale=1.0 / 128.0)
                        yield
                        recip(ssn[:, 0:1], ssn[:, 0:1], [B_ssn], [B_ssn])
                        yield
                        stt("dve", om[:], om[:], ssn[:, 0:1], dnwbc[:], ALU.mult, ALU.mult, [B_om, B_ssn, B_par], [B_om])
                        yield
                        tt("dve", onm[:], om[:], zs[:, n, i * 128:(i + 1) * 128], ALU.mult, [B_om, B_zs], [B_onm])
                        yield
                        tr(psb16[bk[1]][:, 0:128], onm[:], identb[:], [B_onm, B_const], [ps_buf[bk[1]]])
                        yield
                        cp("act", onTg[:, i, c0:c1], psb16[bk[1]][:, 0:128], [ps_buf[bk[1]]], [B_on[i]])
                        yield

                interleave([pre_chain(c) for c in range(2)])
                interleave([scan_chain(i) for i in range(2)])
                if has_s:
                    tt("dve", prod[:, :], kqs[:, :, 0], kqs[:, :, 1], ALU.mult, [B_kqs], [B_prod])
                    b = pbank()
                    mm(psb[b][:, 0:NS], onesf[:], prod[:, :], True, True, [B_prod, B_const], [ps_buf[b]])
                    cp("act", qkbc[:, :], psb[b][:, 0:NS], [ps_buf[b]], [B_qkbc])
                    for i in range(2):
                        b = pbank()
                        tr(psb16[b][:, 0:NS], zs[0:NS, 8, i * 128:(i + 1) * 128], identb[0:NS, 0:NS], [B_zs, B_const], [ps_buf[b]])
                        cp("act", zsT[:, i, :], psb16[b][:, 0:NS], [ps_buf[b]], [B_zsT])
                    for sub in range(4):
                        i, b0 = sub // 2, (sub % 2) * 8
                        h = 2 * gq + i
                        S.dma("sp", Ss[:, :, :], srec_d[b0:b0 + 8, h].rearrange("b k v -> k b v"), writes=[B_Ss])
                        bp = pbank()
                        for j in range(8):
                            mm(psb[bp][:, 2 * j:2 * j + 2], Ss[:, j, :], kqs[:, b0 + j, :], True, True, [B_Ss, B_kqs], [ps_buf[bp]])
                        KQ = psb[bp][:, 0:16].rearrange("p (j two) -> p j two", two=2)
                        eg = EGbc[:, h, b0:b0 + 8]
                        tt("dve", sm["t"][:, :], KQ[:, :, 0], eg, ALU.mult, [ps_buf[bp], B_bc], [B_sm])
                        tt("dve", sm["t"][:, :], vTs[:, i, b0:b0 + 8], sm["t"][:, :], ALU.subtract, [B_v[i], B_sm], [B_sm])
                        tt("dve", sm["vn"][:, :], sm["t"][:, :], BETAbc[:, h, b0:b0 + 8], ALU.mult, [B_sm, B_bc], [B_sm])
                        tt("dve", sm["o"][:, :], KQ[:, :, 1], eg, ALU.mult, [ps_buf[bp], B_bc], [B_sm])
                        tt("dve", sm["o2"][:, :], sm["vn"][:, :], qkbc[:, b0:b0 + 8], ALU.mult, [B_sm, B_qkbc], [B_sm])
                        tt("dve", sm["o"][:, :], sm["o"][:, :], sm["o2"][:, :], ALU.add, [B_sm], [B_sm])
                        tt("dve", sm["sq"][:, :], sm["o"][:, :], sm["o"][:, :], ALU.mult, [B_sm], [B_sm])
                        b = pbank()
                        mm(psb[b][:, 0:8], onesf[:], sm["sq"][:, :], True, True, [B_sm, B_const], [ps_buf[b]])
                        act(sm["rn"][:, :], psb[b][:, 0:8], AF.Sqrt, [ps_buf[b], B_const], [B_sm], bias=eps, scale=1.0 / 128.0)
                        recip(sm["rn"][:, :], sm["rn"][:, :], [B_sm], [B_sm])
                        stt("dve", sm["o"][:, :], sm["o"][:, :], dnwcol[:, 0:1], sm["rn"][:, :], ALU.mult, ALU.mult, [B_sm, B_par], [B_sm])
                        tt("dve", onTg[:, i, TM + b0:TM + b0 + 8], sm["o"][:, :], zsT[:, i, b0:b0 + 8], ALU.mult, [B_sm, B_zsT], [B_on[i]])
                        bt = pbank()
                        tr(psb[bt][0:8, 0:128], sm["vn"][:, :], ident, [B_sm, B_const], [ps_buf[bt]])
                        cp("act", vntm[:, :], psb[bt][0:8, 0:128], [ps_buf[bt]], [B_vntm])
                        for j in range(8):
                            bb = pbank()
                            mm(psb[bb][:, 0:128], ident[0:8, j:j + 1].broadcast_to([8, 128]), vntm[:, :], True, True, [B_vntm, B_const], [ps_buf[bb]])
                            act(Sn[:, j, :], Ss[:, j, :], AF.Copy, [B_Ss, B_bc], [B_Sn], scale=EGbc[:, h, b0 + j:b0 + j + 1])
                            stt("dve", Sn[:, j, :], psb[bb][:, 0:128], kqs[:, b0 + j, 0:1], Sn[:, j, :], ALU.mult, ALU.add,
                                [ps_buf[bb], B_kqs, B_Sn], [B_Sn])
                        S.dma("sp", rec_s[b0:b0 + 8, h].rearrange("b k v -> k b v"), Sn[:, :, :], reads=[B_Sn])
                for i in range(2):
                    S.dma("sp", ON_d[:, 2 * gq + i, 0:T], onTg[:, i, 0:T], reads=[B_on[i]])
            if last:
                S.dma("sp", rec_p.rearrange("h k v -> k h v"), Sf[:, :, :], reads=B_S)
                cpt = PH.alloc([3, 2048], F32)
                B_cpt = Buf("cpt")
                for q in range(4):
                    for q4 in range(4):
                        b = pbank()
                        for jj in range(4):
                            tr(psb[b][0:3, jj * 128:(jj + 1) * 128], convc[:, q * 16 + q4 * 4 + jj, :], ident, [B_convc, B_const], [ps_buf[b]])
                        cp("act", cpt[:, q4 * 512:(q4 + 1) * 512], psb[b][0:3, :], [ps_buf[b]], [B_cpt])
                    S.dma("sp", conv_p[:, q * 2048:(q + 1) * 2048], cpt[:, :], reads=[B_cpt])
            else:
                S.dma("sp", SD_d[:, :, :], Sf[:, :, :], reads=B_S)

        def outproj_phase():
            S.fence()
            PH.reset()
            onT = PH.alloc([128, NH, TMAX], BF16)
            B_onT = [Buf("onT%d" % h) for h in range(NH)]
            for h in range(NH):
                S.dma("sp", onT[:, h, 0:T], ON_d[:, h, 0:T], writes=[B_onT[h]])
            wo_v = dn_w_out[0].rearrange("(k p) n -> p k n", p=128)
            out_stream(T, NH, lambda c: wo_v[:, :, c * 128:(c + 1) * 128], [128, NH, 128], onT, B_onT, RSTDY, B_ry)

        PH_prod = None
        B_prod = Buf("prod")
        hT, B_h = rn_phase(0, 2, 0, 4, 3)
        ffn_phase(0, hT, B_h)
        if stop_after == "ffn0":
            return False
        hT, B_h = rn_phase(0, 5, 1, 1, 0)
        if stop_after == "rn2":
            return False
        if has_s:
            pass
        delta_phase(hT, B_h)
        outproj_phase()
        if stop_after == "oproj":
            return False
        hT, B_h = rn_phase(1, 2, 1, 4, 3)
        ffn_phase(1, hT, B_h)
        rn_phase(1, 5, None, None, None, final=True)
        return True

    if stop_after != "pro" and run_block(0):
        run_block(1)

    S.resolve()
    S.emit(nc)
    return nc, dbg_outs


_PROG = {}
W_NAMES = ("norm_w", "ada_w", "ada_b", "pool_w", "pool_scale", "dn_w_in", "dn_conv_w", "dn_a_log", "dn_dt_bias",
           "dn_norm_w", "dn_w_out", "ffn_w_gate", "ffn_w_up", "ffn_conv_w", "ffn_w_down")


def make_in_maps(inputs, ncores=8):
    f = lambda a: np.ascontiguousarray(np.asarray(a, dtype=np.float32))
    consts = host_consts()
    sel = host_poolsel()
    maps = []
    for c in range(ncores):
        b = c % 4
        sl = slice(NS * c, NS * (c + 1))
        xp = np.asarray(inputs["x_prompt"])[b]
        xs = np.asarray(inputs["x_sample"])[sl, 0, :]
        m = {
            "xin": f(np.concatenate([xp[0:TM], xs, xp[TM:2 * TM]], axis=0)),
            "cin": f(np.concatenate([np.asarray(inputs["c_sample"])[sl], np.asarray(inputs["c_prompt"])[b:b + 1]], axis=0)),
            "consts": consts,
            "poolsel": sel,
            "cache_pool_c": f(np.asarray(inputs["cache_pool"])[0, sl]),
            "state_conv_c": f(np.asarray(inputs["state_conv"])[0, sl]),
            "state_rec_c": f(np.asarray(inputs["state_rec"])[0, sl]),
            "cache_ffn_c": f(np.asarray(inputs["cache_ffn_conv"])[:, sl]),
        }
        for nm in W_NAMES:
            m[nm] = f(inputs[nm])
        maps.append(m)
    return maps


def kernel(**inputs):
    if "nc" not in _PROG:
        _PROG["nc"] = build_program()[0]
    nc = _PROG["nc"]
    maps = make_in_maps(inputs)
    res = run_bass_kernel_spmd(nc, maps, core_ids=list(range(8))).results
    y_prompt = np.stack([res[b]["y_p"] for b in range(4)], 0)
    y_sample = np.concatenate([res[c]["y_s"] for c in range(8)], 0)[:, None, :]
    pool_prompt = np.stack([res[b]["pool_p"][1:16] for b in range(4)], 0)[None]
    pool_sample = np.concatenate([res[c]["pool_s"] for c in range(8)], 0)[None]
    conv_prompt = np.stack([res[b]["conv_p"] for b in range(4)], 0)[None]
    conv_sample = np.concatenate([res[c]["conv_s"] for c in range(8)], 0)[None]
    rec_prompt = np.stack([res[b]["rec_p"] for b in range(4)], 0)[None]
    rec_sample = np.concatenate([res[c]["rec_s"] for c in range(8)], 0)[None]
    ffn_prompt = np.stack([res[b]["ffn_p"] for b in range(4)], 1)
    ffn_sample = np.concatenate([res[c]["ffn_s"] for c in range(8)], 1)
    outs = (y_prompt, y_sample, pool_prompt, pool_sample, conv_prompt, conv_sample, rec_prompt, rec_sample,
            ffn_prompt, ffn_sample)
    return tuple(np.ascontiguousarray(o, dtype=np.float32) for o in outs)
```

```python
import numpy as np
import concourse.bass as bass
import concourse.mybir as mybir
from concourse.bass_utils import run_bass_kernel_spmd

F32 = mybir.dt.float32
BF16 = mybir.dt.bfloat16
AF = mybir.ActivationFunctionType
ALU = mybir.AluOpType

D = 2048
NCH = 16
DFF = 5632
NFF = 44
NH = 32
EPS = 1e-6
TM = 1024
NS = 16
CONV_DIM = 8192
PROJ = 12352
BIG = 30000.0


class Buf:
    __slots__ = ("name", "excl")

    def __init__(self, name="", excl=False):
        self.name = name
        self.excl = excl


class Op:
    __slots__ = ("eng", "fn", "reads", "writes", "dma", "nofence", "deps", "signal", "count", "semidx", "waits")

    def __init__(self, eng, fn, reads, writes, dma, nofence):
        self.eng, self.fn, self.reads, self.writes, self.dma, self.nofence = eng, fn, reads, writes, dma, nofence
        self.deps = set()
        self.signal = False
        self.count = 0
        self.semidx = 0
        self.waits = []


ENGS = ("pe", "act", "dve", "pool", "sp")
NDSEM = 8


class Sched:
    def __init__(self):
        self.ops = []
        self.fences = []

    def op(self, eng, fn, reads=(), writes=(), nofence=False):
        reads, writes = list(reads), list(writes)
        for b in reads:
            if b.excl and b not in writes:
                writes.append(b)
        self.ops.append(Op(eng, fn, reads, writes, False, nofence))

    def dma(self, eng, out, in_, reads=(), writes=(), nofence=False):
        self.ops.append(Op(eng, (lambda e: e.dma_start(out=out, in_=in_)), list(reads), list(writes), True, nofence))

    def fence(self):
        self.fences.append(len(self.ops))

    def resolve(self):
        ops = self.ops
        last_w, readers = {}, {}
        last_comp = {}
        recent_dma = {e: [] for e in ENGS}
        pending = {e: None for e in ENGS}
        fset = set(self.fences)
        for i, op in enumerate(ops):
            if i in fset:
                F = set(last_comp.values())
                for e in ENGS:
                    F.update(recent_dma[e])
                for e in ENGS:
                    pending[e] = set(F) if pending[e] is None else (pending[e] | F)
            deps = set()
            for b in op.reads:
                if b in last_w:
                    deps.add(last_w[b])
            for b in op.writes:
                if b in last_w:
                    deps.add(last_w[b])
                deps.update(readers.get(b, ()))
            if not op.nofence and pending[op.eng] is not None:
                deps.update(pending[op.eng])
                pending[op.eng] = None
            deps.discard(i)
            op.deps = set(p for p in deps if not (ops[p].eng == op.eng == "pe"))
            for b in op.reads:
                readers.setdefault(b, []).append(i)
            for b in op.writes:
                last_w[b] = i
                readers[b] = []
            if op.dma:
                recent_dma[op.eng].append(i)
                if len(recent_dma[op.eng]) > NDSEM:
                    recent_dma[op.eng].pop(0)
            else:
                last_comp[op.eng] = i
        for op in ops:
            for p in op.deps:
                ops[p].signal = True
        cnt = {e: 0 for e in ENGS}
        ndma = {e: 0 for e in ENGS}
        for op in ops:
            if op.dma:
                j = ndma[op.eng]
                op.semidx = j % NDSEM
                op.count = 16 * (j // NDSEM + 1)
                ndma[op.eng] += 1
            elif op.signal:
                cnt[op.eng] += 1
                op.count = cnt[op.eng]
        known = {e: {} for e in ENGS}
        for op in ops:
            need = {}
            if op.dma and op.count > 16:
                need[(op.eng, "d", op.semidx)] = op.count - 16
            for p in op.deps:
                po = ops[p]
                key = (po.eng, "d", po.semidx) if po.dma else (po.eng, "c", 0)
                if need.get(key, 0) < po.count:
                    need[key] = po.count
            kn = known[op.eng]
            op.waits = []
            for key, val in need.items():
                if kn.get(key, 0) < val:
                    kn[key] = val
                    op.waits.append((key, val))
        self.final_dma = {e: ndma[e] for e in ENGS}

    def emit(self, nc):
        import contextlib
        with contextlib.ExitStack() as st:
            csem = {e: st.enter_context(nc.semaphore("c_" + e)) for e in ("pe", "act", "dve", "pool")}
            dsem = {e: [st.enter_context(nc.semaphore("d_%s%d" % (e, i))) for i in range(NDSEM)] for e in ("pool", "sp")}
            block = st.enter_context(nc.Block())
            ops = self.ops
            final_dma = self.final_dma

            def semof(key):
                return dsem[key[0]][key[2]] if key[1] == "d" else csem[key[0]]

            def run(ename, e):
                for op in ops:
                    if op.eng != ename:
                        continue
                    for key, val in op.waits:
                        e.wait_ge(semof(key), val)
                    ins = op.fn(e)
                    if op.dma:
                        ins.then_inc(dsem[ename][op.semidx], 16)
                    elif op.signal:
                        ins.then_inc(csem[ename], 1)
                if ename == "sp":
                    for q in ("pool", "sp"):
                        n = final_dma[q]
                        for r in range(NDSEM):
                            k = (n - r + NDSEM - 1) // NDSEM if n > r else 0
                            if k > 0:
                                e.wait_ge(dsem[q][r], 16 * k)

            @block.tensor
            def _(e):
                run("pe", e)

            @block.scalar
            def _(e):
                run("act", e)

            @block.vector
            def _(e):
                run("dve", e)

            @block.gpsimd
            def _(e):
                run("pool", e)

            @block.sync
            def _(e):
                run("sp", e)


POOL_W = (2, 4, 8, 16)
C_ID, C_TRI, C_MA, C_MB, C_FIX, C_END = 0, 128, 256, 384, 512, 576


def host_consts():
    c = np.zeros((128, C_END), np.float32)
    i = np.arange(128)
    c[:, C_ID:C_ID + 128] = np.eye(128, dtype=np.float32)
    c[:, C_TRI:C_TRI + 128] = (i[:, None] <= i[None, :]).astype(np.float32)
    c[:, C_MA:C_MA + 128] = np.where(i[None, :] >= i[:, None], BIG, 0.0)
    c[:, C_MB:C_MB + 128] = np.where(i[None, :] < i[:, None], -BIG, 0.0)
    for g, w in enumerate(POOL_W):
        t = np.arange(15)
        c[:, C_FIX + g * 15:C_FIX + (g + 1) * 15] = (1.0 / np.minimum(t + 1, w))[None, :]
    return c


def host_poolsel():
    s = np.zeros((2, 120, 4, 16), np.float32)
    for k in range(2):
        for bl in range(8):
            for r in range(15):
                for g, w in enumerate(POOL_W):
                    if r >= 15 - (w - 1):
                        s[k, bl * 15 + r, g, 8 * k + bl] = 1.0 / w
    return s


def build_program(debug=False, stop_after=None):
    nc = bass.Bass("TRN2", target_bir_lowering=False)
    S = Sched()
    dbg_outs = []

    def din(name, shape):
        return nc.dram_tensor(name, list(shape), F32, kind="ExternalInput").ap()

    def dout(name, shape):
        return nc.dram_tensor(name, list(shape), F32, kind="ExternalOutput").ap()

    def dscr(name, shape, dt=F32):
        if debug:
            dbg_outs.append(name)
            return nc.dram_tensor(name, list(shape), dt, kind="ExternalOutput").ap()
        return nc.dram_tensor(name, list(shape), dt).ap()

    xin = din("xin", [2 * TM + NS, D])
    cin = din("cin", [NS + 1, D])
    consts_d = din("consts", [128, C_END])
    poolsel_d = din("poolsel", [2, 120, 4, 16])
    cpool_d = din("cache_pool_c", [NS, 15, D])
    sconv_d = din("state_conv_c", [NS, 3, CONV_DIM])
    srec_d = din("state_rec_c", [NS, NH, 128, 128])
    cffn_d = din("cache_ffn_c", [2, NS, 2, DFF])
    norm_w = din("norm_w", [2, 4, D])
    ada_w = din("ada_w", [2, D, 6 * D])
    ada_b = din("ada_b", [2, 6 * D])
    pool_w = din("pool_w", [1, 4, 512, 512])
    pool_scale = din("pool_scale", [1, D])
    dn_w_in = din("dn_w_in", [1, D, PROJ])
    dn_conv_w = din("dn_conv_w", [1, 4, CONV_DIM])
    dn_a_log = din("dn_a_log", [1, NH])
    dn_dt_bias = din("dn_dt_bias", [1, NH])
    dn_norm_w = din("dn_norm_w", [1, 128])
    dn_w_out = din("dn_w_out", [1, 4096, D])
    ffn_w_gate = din("ffn_w_gate", [2, D, DFF])
    ffn_w_up = din("ffn_w_up", [2, D, DFF])
    ffn_conv_w = din("ffn_conv_w", [2, 3, DFF])
    ffn_w_down = din("ffn_w_down", [2, DFF, D])

    y_p = dout("y_p", [2 * TM, D])
    y_s = dout("y_s", [NS, D])
    pool_p = dout("pool_p", [16, D])
    pool_s = dout("pool_s", [NS, 15, D])
    conv_p = dout("conv_p", [3, CONV_DIM])
    conv_s = dout("conv_s", [NS, 3, CONV_DIM])
    rec_p = dout("rec_p", [NH, 128, 128])
    rec_s = dout("rec_s", [NS, NH, 128, 128])
    ffn_p = dout("ffn_p", [2, 2, DFF])
    ffn_s = dout("ffn_s", [2, NS, 2, DFF])

    TMAX = TM + NS
    X_d = dscr("X_scr", [128, NCH, TMAX])
    Y_d = dscr("Y_scr", [128, NCH, TMAX])
    ON_d = dscr("ON_scr", [128, NH, TMAX], BF16)
    SD_d = dscr("SD_scr", [128, NH, 128])

    class Arena:
        def __init__(self, base, limit):
            self.base, self.limit, self.off, self.n = base, limit, base, 0

        def reset(self):
            self.off = self.base

        def alloc(self, shape, dt):
            nbytes = int(np.prod(shape[1:])) * (4 if dt == F32 else 2)
            nbytes = (nbytes + 63) // 64 * 64
            off = self.off
            assert off + nbytes <= self.limit, ("SBUF arena overflow", shape, off, nbytes, self.limit)
            self.off += nbytes
            self.n += 1
            return nc.alloc_sbuf_tensor_at("t%d" % self.n, list(shape), dt, offset=off)

    PERS = Arena(16512, 16512 + 60 * 1024)
    PH = Arena(16512 + 60 * 1024, 229376)

    cst = PERS.alloc([128, C_END], F32)
    ident = cst[:, C_ID:C_ID + 128]
    tri = cst[:, C_TRI:C_TRI + 128]
    maskA = cst[:, C_MA:C_MA + 128]
    maskB = cst[:, C_MB:C_MB + 128]
    identb = PERS.alloc([128, 128], BF16)
    onesb = PERS.alloc([128, 128], BF16)
    onesf = PERS.alloc([128, 128], F32)
    epsT = PERS.alloc([128, 2], F32)
    normwT = PERS.alloc([128, 128], F32)
    pscT = PERS.alloc([128, 16], F32)
    adabT = PERS.alloc([128, 192], F32)
    fcwT = PERS.alloc([128, 264], F32)
    dcwT = PERS.alloc([128, 256], F32)
    dnwbc = PERS.alloc([128, 128], F32)
    hb = PERS.alloc([128, 64], F32)
    MOD = PERS.alloc([128, 2, 96, NS + 1], F32)
    poolc = PERS.alloc([128, NCH, 15], F32)
    gatec = PERS.alloc([128, 2, NFF, 2], F32)
    convc = PERS.alloc([128, 64, 3], F32)
    RSTDY = PERS.alloc([128, TM + NS], F32)
    dnwcol = PERS.alloc([128, 2], F32)
    RING_BYTES = 8192
    NRING = 4
    ring_off = []
    for r in range(NRING):
        t = PERS.alloc([128, RING_BYTES // 2], BF16)
        ring_off.append(PERS.off - RING_BYTES)
    ring_buf = [Buf("ring%d" % r) for r in range(NRING)]
    ring_views = {}
    ring_next = [0]
    B_const = Buf("const")
    B_par = Buf("par")
    B_mod = Buf("mod")
    B_poolc, B_gatec, B_convc = Buf("poolc"), Buf("gatec"), Buf("convc")
    B_ry = Buf("rstdy")

    def wload(dram_view, shape, alt=None):
        if alt is None:
            offs, bufs, nxt, views, n = ring_off, ring_buf, ring_next, ring_views, NRING
        else:
            offs, bufs, nxt, views, n = alt
        r = nxt[0] % n
        nxt[0] += 1
        key = (r, tuple(shape))
        if key not in views:
            views[key] = nc.alloc_sbuf_tensor_at("rv%d_%d_%d" % (r, len(views), offs[r]), list(shape), BF16, offset=offs[r])
        v = views[key]
        S.dma("pool", v[:], dram_view, writes=[bufs[r]], nofence=True)
        return v, bufs[r]

    psb = [nc.alloc_psum_tensor("ps%d" % i, [128, 512], F32) for i in range(8)]
    psb16 = [p.bitcast(BF16) for p in psb]
    ps_buf = [Buf("ps%d" % i, excl=True) for i in range(8)]
    ps_rr = [0]
    ps_pool = [list(range(8))]

    def pbank():
        lst = ps_pool[0]
        i = lst[ps_rr[0] % len(lst)]
        ps_rr[0] += 1
        return i

    def mm(out, lhsT, rhs, start, stop, reads, writes):
        S.op("pe", lambda e: e.matmul(out, lhsT=lhsT, rhs=rhs, start=start, stop=stop), reads, writes)

    def tr(out, in_, idn, reads, writes):
        S.op("pe", lambda e: e.transpose(out=out, in_=in_, identity=idn), reads, writes)

    def act(out, in_, func, reads, writes, bias=None, scale=None, accum=None):
        kw = {}
        if bias is not None:
            kw["bias"] = bias
        if scale is not None:
            kw["scale"] = scale
        if accum is not None:
            kw["accum_out"] = accum
        S.op("act", lambda e: e.activation(out=out, in_=in_, func=func, **kw), reads, writes)

    def tt(eng, out, in0, in1, op, reads, writes):
        S.op(eng, lambda e: e.tensor_tensor(out=out, in0=in0, in1=in1, op=op), reads, writes)

    def ts(eng, out, in0, s1, op0, reads, writes, s2=None, op1=None):
        if op1 is None:
            S.op(eng, lambda e: e.tensor_scalar(out=out, in0=in0, scalar1=s1, scalar2=None, op0=op0), reads, writes)
        else:
            S.op(eng, lambda e: e.tensor_scalar(out=out, in0=in0, scalar1=s1, scalar2=s2, op0=op0, op1=op1), reads, writes)

    def stt(eng, out, in0, scalar, in1, op0, op1, reads, writes):
        S.op(eng, lambda e: e.scalar_tensor_tensor(out=out, in0=in0, scalar=scalar, in1=in1, op0=op0, op1=op1), reads, writes)

    def cp(eng, out, in_, reads, writes):
        if eng == "act":
            act(out, in_, AF.Copy, reads, writes)
        else:
            S.op(eng, lambda e: e.tensor_copy(out=out, in_=in_), reads, writes)

    def recip(out, in_, reads, writes):
        S.op("dve", lambda e: e.reciprocal(out=out, in_=in_), reads, writes)

    def memset(eng, ap, val, writes):
        S.op(eng, lambda e: e.memset(ap, val), [], writes)

    S.dma("sp", cst[:], consts_d[:, :], writes=[B_const])
    memset("dve", onesf[:], 1.0, [B_const])
    memset("dve", onesb[:], 1.0, [B_const])
    memset("dve", epsT[:, 0:1], EPS, [B_const])
    memset("dve", epsT[:, 1:2], 1.0, [B_const])
    cp("dve", identb[:], ident, [B_const], [B_const])
    memset("dve", poolc[:], 0.0, [B_poolc])
    memset("dve", gatec[:], 0.0, [B_gatec])
    memset("dve", convc[:], 0.0, [B_convc])
    eps = epsT[:, 0:1]
    one = epsT[:, 1:2]

    PH.reset()
    stg = PH.alloc([128, 9, 128], F32)
    B_stg = Buf("stg")
    prm = [
        (norm_w.rearrange("l i (c p) -> (l i c) p", p=128), 128, normwT[:, 0:128]),
        (pool_scale.rearrange("o (c p) -> (o c) p", p=128), 16, pscT[:, 0:16]),
        (ada_b[0].rearrange("(c p) -> c p", p=128), 96, adabT[:, 0:96]),
        (ada_b[1].rearrange("(c p) -> c p", p=128), 96, adabT[:, 96:192]),
        (ffn_conv_w.rearrange("l k (c p) -> (l k c) p", p=128)[0:88], 88, fcwT[:, 0:88]),
        (ffn_conv_w.rearrange("l k (c p) -> (l k c) p", p=128)[88:176], 88, fcwT[:, 88:176]),
        (ffn_conv_w.rearrange("l k (c p) -> (l k c) p", p=128)[176:264], 88, fcwT[:, 176:264]),
        (dn_conv_w.rearrange("o k (c p) -> (o k c) p", p=128)[0:128], 128, dcwT[:, 0:128]),
        (dn_conv_w.rearrange("o k (c p) -> (o k c) p", p=128)[128:256], 128, dcwT[:, 128:256]),
    ]
    for i, (src, n, dst) in enumerate(prm):
        S.dma("sp", stg[0:n, i, :], src, writes=[B_stg])
    for i, (src, n, dst) in enumerate(prm):
        b = pbank()
        tr(psb[b][:, 0:n], stg[0:n, i, :], ident[0:n, 0:n], [B_stg, B_const], [ps_buf[b]])
        cp("act", dst, psb[b][:, 0:n], [ps_buf[b]], [B_par])
    rowt = PH.alloc([1, 256], F32)
    B_row = Buf("row")
    S.dma("sp", rowt[0:1, 0:128], dn_norm_w[0:1, :], writes=[B_row])
    S.dma("sp", rowt[0:1, 128:160], dn_a_log[0:1, :], writes=[B_row])
    S.dma("sp", rowt[0:1, 160:192], dn_dt_bias[0:1, :], writes=[B_row])
    S.dma("sp", dnwcol[:, 0:1], dn_norm_w.rearrange("o p -> p o"), writes=[B_par])
    b = pbank()
    mm(psb[b][:, 0:192], onesf[0:1, :], rowt[0:1, 0:192], True, True, [B_row, B_const], [ps_buf[b]])
    cp("act", dnwbc[:], psb[b][:, 0:128], [ps_buf[b]], [B_par])
    act(hb[:, 0:32], psb[b][:, 128:160], AF.Exp, [ps_buf[b]], [B_par])
    ts("dve", hb[:, 0:32], hb[:, 0:32], -1.0, ALU.mult, [B_par], [B_par])
    cp("act", hb[:, 32:64], psb[b][:, 160:192], [ps_buf[b]], [B_par])

    ctm = PH.alloc([NS + 1, D], F32)
    csT = PH.alloc([128, NCH, NS + 1], BF16)
    B_ctm, B_csT = Buf("ctm"), Buf("csT")
    S.dma("sp", ctm[:], cin[:, :], writes=[B_ctm])
    act(ctm[:], ctm[:], AF.Silu, [B_ctm], [B_ctm])
    b = pbank()
    for c in range(NCH):
        tr(psb[b][:, c * 17:(c + 1) * 17], ctm[0:17, c * 128:(c + 1) * 128], ident[0:17, 0:17], [B_ctm, B_const], [ps_buf[b]])
    cp("act", csT[:].rearrange("p c n -> p (c n)"), psb[b][:, 0:NCH * 17], [ps_buf[b]], [B_csT])

    NPRO = 6
    PRO_BYTES = 16384
    pro_off = []
    for r in range(NPRO):
        PH.alloc([128, PRO_BYTES // 2], BF16)
        pro_off.append(PH.off - PRO_BYTES)
    pro_ring = (pro_off, [Buf("pring%d" % r) for r in range(NPRO)], [0], {}, NPRO)
    for l in range(2):
        wv = ada_w[l].rearrange("(k p) n -> p k n", p=128)
        modtm = [PH.alloc([NS + 1, 512], F32) for _ in range(2)]
        B_modtm = [Buf("modtm0"), Buf("modtm1")]
        for blk in range(24):
            w, wb = wload(wv[:, :, blk * 512:(blk + 1) * 512], [128, NCH, 512], alt=pro_ring)
            b = pbank()
            for k in range(NCH):
                mm(psb[b][0:NS + 1, 0:512], csT[:, k, :], w[:, k, :], k == 0, k == NCH - 1, [wb, B_csT], [ps_buf[b]])
            mt, bmt = modtm[blk % 2], B_modtm[blk % 2]
            cp("act", mt[:, :], psb[b][0:NS + 1, 0:512], [ps_buf[b]], [bmt])
            b2 = pbank()
            for jj in range(4):
                tr(psb[b2][:, jj * 17:(jj + 1) * 17], mt[:, jj * 128:(jj + 1) * 128], ident[0:NS + 1, 0:NS + 1], [bmt, B_const], [ps_buf[b2]])
            j0 = blk * 4
            for jj in range(4):
                ts("dve", MOD[:, l, j0 + jj, :], psb[b2][:, jj * 17:(jj + 1) * 17], adabT[:, l * 96 + j0 + jj:l * 96 + j0 + jj + 1],
                   ALU.add, [ps_buf[b2], B_par], [B_mod])
        for c in range(NCH):
            nw = lambda i: normwT[:, (l * 4 + i) * 16 + c:(l * 4 + i) * 16 + c + 1]
            ts("dve", MOD[:, l, 16 + c, :], MOD[:, l, 16 + c, :], 1.0, ALU.add, [B_mod, B_par], [B_mod], s2=nw(0), op1=ALU.mult)
            ts("dve", MOD[:, l, 32 + c, :], MOD[:, l, 32 + c, :], nw(1), ALU.mult, [B_mod, B_par], [B_mod])
            ts("dve", MOD[:, l, 64 + c, :], MOD[:, l, 64 + c, :], 1.0, ALU.add, [B_mod, B_par], [B_mod], s2=nw(2), op1=ALU.mult)
            ts("dve", MOD[:, l, 80 + c, :], MOD[:, l, 80 + c, :], nw(3), ALU.mult, [B_mod, B_par], [B_mod])

    if debug:
        mod_dbg = dscr("MOD_dbg", [128, 2 * 96 * (NS + 1)])
        S.dma("sp", mod_dbg[:, :], MOD[:].rearrange("p l j n -> p (l j n)"), reads=[B_mod])

    def modv(l, q, c):
        return MOD[:, l, q * 16 + c, NS:NS + 1]

    def mods(l, q, c):
        return MOD[:, l, q * 16 + c, 0:NS]

    def tiles_of(T):
        if T == TM:
            return [(0, 512), (512, 1024)]
        return [(0, 352), (352, 704), (704, T)]

    class Blk:
        pass

    def ss_open(T):
        nt = tiles_of(T)
        banks = [7 - i for i in range(len(nt))]
        ps_pool[0] = [i for i in range(8) if i not in banks]
        return banks

    def ss_add(banks, T, src, src_bufs, sq, B_sq, first, last):
        act(sq[:, 0:T], src, AF.Square, src_bufs, [B_sq])
        for (a, bnd), bk in zip(tiles_of(T), banks):
            mm(psb[bk][:, 0:bnd - a], onesb[:], sq[:, a:bnd], first, last, [B_sq, B_const], [ps_buf[bk]])

    def ss_close(banks, T, rstd, B_rstd, div):
        for (a, bnd), bk in zip(tiles_of(T), banks):
            act(rstd[:, a:bnd], psb[bk][:, 0:bnd - a], AF.Sqrt, [ps_buf[bk], B_const], [B_rstd], bias=eps, scale=1.0 / div)
        recip(rstd[:, 0:T], rstd[:, 0:T], [B_rstd], [B_rstd])
        ps_pool[0] = list(range(8))

    def make_h(l, qA, qB, xT, B_x, rstd, B_rstd, T, out, B_out, tmp, B_tmp):
        for c in range(NCH):
            stt("dve", tmp[:, 0:TM], xT[:, c, 0:TM], modv(l, qA, c), rstd[:, 0:TM], ALU.mult, ALU.mult,
                [B_x[c], B_rstd, B_mod], [B_tmp])
            act(out[:, c, 0:TM], tmp[:, 0:TM], AF.Identity, [B_tmp, B_mod], [B_out[c]], bias=modv(l, qB, c), scale=1.0)
            if T > TM:
                tt("dve", tmp[:, TM:T], xT[:, c, TM:T], rstd[:, TM:T], ALU.mult, [B_x[c], B_rstd], [B_tmp])
                tt("dve", tmp[:, TM:T], tmp[:, TM:T], mods(l, qA, c), ALU.mult, [B_tmp, B_mod], [B_tmp])
                tt("dve", out[:, c, TM:T], tmp[:, TM:T], mods(l, qB, c), ALU.add, [B_tmp, B_mod], [B_out[c]])

    def norm_of_x(xT, B_x, T, rstd, B_rstd, sq, B_sq):
        banks = ss_open(T)
        for c in range(NCH):
            ss_add(banks, T, xT[:, c, 0:T], [B_x[c]], sq, B_sq, c == 0, c == NCH - 1)
        ss_close(banks, T, rstd, B_rstd, float(D))

    def residual(l, qG, T, xT, B_x, rstd_y, B_ry, load_x):
        for c in range(NCH):
            ys = PHs["ystage"][c % 2]
            B_ys = PHs["B_ystage"][c % 2]
            S.dma("sp", ys[:, 0:T], Y_d[:, c, 0:T], writes=[B_ys])
            if load_x:
                S.dma("sp", xT[:, c, 0:T], X_d[:, c, 0:T], writes=[B_x[c]])
            stt("dve", ys[:, 0:TM], ys[:, 0:TM], modv(l, qG, c), rstd_y[:, 0:TM], ALU.mult, ALU.mult, [B_ys, B_ry, B_mod], [B_ys])
            if T > TM:
                tt("dve", ys[:, TM:T], ys[:, TM:T], rstd_y[:, TM:T], ALU.mult, [B_ys, B_ry], [B_ys])
                tt("dve", ys[:, TM:T], ys[:, TM:T], mods(l, qG, c), ALU.mult, [B_ys, B_mod], [B_ys])
            tt("dve", xT[:, c, 0:T], xT[:, c, 0:T], ys[:, 0:T], ALU.add, [B_x[c], B_ys], [B_x[c]])

    PHs = {}

    def out_stream(T, nk, wview_fn, wshape, srcT, B_src, rstd_out, B_ro):
        PHs_ystage = [PH.alloc([128, TMAX], F32) for _ in range(2)]
        B_ys = [Buf("ys0"), Buf("ys1")]
        sq = PH.alloc([128, TMAX], BF16)
        B_sq = Buf("sq")
        banks = ss_open(T)
        nt = tiles_of(T)
        nsplit = 2 if nk * 128 * 2 > RING_BYTES else 1
        kh = nk // nsplit
        for c in range(NCH):
            wparts = []
            for sp_ in range(nsplit):
                wparts.append(wload(wview_fn(c)[:, sp_ * kh:(sp_ + 1) * kh, :], [128, kh, 128]))
            ys, bys = PHs_ystage[c % 2], B_ys[c % 2]
            for (a, bnd) in nt:
                b = pbank()
                for k in range(nk):
                    w, wb = wparts[k // kh]
                    mm(psb[b][:, 0:bnd - a], w[:, k % kh, :], srcT[:, k, a:bnd], k == 0, k == nk - 1, [wb, B_src[k]], [ps_buf[b]])
                cp("act", ys[:, a:bnd], psb[b][:, 0:bnd - a], [ps_buf[b]], [bys])
            ss_add(banks, T, ys[:, 0:T], [bys], sq, B_sq, c == 0, c == NCH - 1)
            S.dma("sp", Y_d[:, c, 0:T], ys[:, 0:T], reads=[bys])
        ss_close(banks, T, rstd_out, B_ro, float(D))

    def run_block(bi):
        T = TM + NS if bi == 0 else TM
        row0 = 0 if bi == 0 else TM + NS
        nt = tiles_of(T)
        has_s = bi == 0
        last = bi == 1

        S.fence()
        PH.reset()
        xT = PH.alloc([128, NCH, TMAX], F32)
        B_x = [Buf("x%d" % c) for c in range(NCH)]
        rstd = PH.alloc([128, TMAX], F32)
        B_rstd = Buf("rstd")
        sq = PH.alloc([128, TMAX], BF16)
        B_sq = Buf("sq")
        tmp = PH.alloc([128, TMAX], F32)
        B_tmp = Buf("tmp")
        off_xst = PH.off
        xst = [PH.alloc([128, D], F32) for _ in range(2)]
        B_xst = [Buf("xst0"), Buf("xst1")]
        nrt = (T + 127) // 128
        for r in range(nrt):
            n = min(128, T - r * 128)
            st_, bst = xst[r % 2], B_xst[r % 2]
            S.dma("sp", st_[0:n, :], xin[row0 + r * 128:row0 + r * 128 + n, :], writes=[bst])
            for q4 in range(4):
                b = pbank()
                for jj in range(4):
                    c = q4 * 4 + jj
                    tr(psb[b][:, jj * 128:jj * 128 + n], st_[0:n, c * 128:(c + 1) * 128], ident[0:n, 0:n], [bst, B_const], [ps_buf[b]])
                for jj in range(4):
                    c = q4 * 4 + jj
                    cp("act" if jj % 2 == 0 else "dve", xT[:, c, r * 128:r * 128 + n], psb[b][:, jj * 128:jj * 128 + n], [ps_buf[b]], [B_x[c]])
        for c in range(NCH):
            S.dma("sp", X_d[:, c, 0:T], xT[:, c, 0:T], reads=[B_x[c]])
        if stop_after == "A1":
            return False
        norm_of_x(xT, B_x, T, rstd, B_rstd, sq, B_sq)
        if stop_after == "A2":
            return False
        make_h(0, 1, 0, xT, B_x, rstd, B_rstd, T, xT, B_x, tmp, B_tmp)
        h0, B_h0 = xT, B_x
        if stop_after == "A":
            return False
        S.fence()
        PH.off = off_xst

        if has_s:
            S.dma("sp", pool_s[:, 0:14, :], cpool_d[:, 1:15, :])
            hs_tm = PH.alloc([NS, D], F32)
            B_hs = Buf("hs")
            for q4 in range(4):
                b = pbank()
                for jj in range(4):
                    c = q4 * 4 + jj
                    tr(psb[b][0:NS, jj * 128:(jj + 1) * 128], h0[:, c, TM:T], ident, [B_h0[c], B_const], [ps_buf[b]])
                cp("act", hs_tm[:, q4 * 512:(q4 + 1) * 512], psb[b][0:NS, :], [ps_buf[b]], [B_hs])
            S.dma("sp", pool_s[:, 14, :], hs_tm[:], reads=[B_hs])
            cache = [PH.alloc([120, D], F32) for _ in range(2)]
            B_cache = Buf("cache")
            sel = PH.alloc([120, 2, 4, 16], F32)
            B_sel = Buf("sel")
            for k in range(2):
                S.dma("sp", cache[k][:], cpool_d[8 * k:8 * k + 8].rearrange("b r d -> (b r) d"), writes=[B_cache])
                S.dma("sp", sel[:, k, :, :], poolsel_d[k], writes=[B_sel])
        ext = [PH.alloc([128, 4, 15 + TM], F32) for _ in range(2)]
        B_ext = [Buf("ext0"), Buf("ext1")]
        dT = PH.alloc([128, 4, TMAX], BF16)
        B_dT = Buf("dT")
        ysg_one = PH.alloc([128, TMAX], F32)
        ysg = [ysg_one, ysg_one]
        B_ysg_one = Buf("ysg")
        B_ysg = [B_ysg_one, B_ysg_one]
        rstd_y = RSTDY
        banks = ss_open(T)
        for g in range(4):
            w_ = POOL_W[g]
            cs_ = slice(4 * g, 4 * g + 4)
            E0, E1 = ext
            cp("dve", E0[:, :, 0:15], poolc[:, cs_, :], [B_poolc], [B_ext[0]])
            cp("act", E0[:, :, 15:15 + TM], h0[:, cs_, 0:TM], [B_h0[4 * g + i] for i in range(4)], [B_ext[0]])
            cur, oth = 0, 1
            sh = 1
            while sh < w_:
                A_, Bt = ext[cur], ext[oth]
                tt("dve", Bt[:, :, sh:15 + TM], A_[:, :, sh:15 + TM], A_[:, :, 0:15 + TM - sh], ALU.add, [B_ext[cur]], [B_ext[oth]])
                cur, oth = oth, cur
                sh *= 2
            Sw = ext[cur]
            stt("dve", dT[:, :, 0:TM], Sw[:, :, 15:15 + TM], 1.0 / w_, h0[:, cs_, 0:TM], ALU.mult, ALU.subtract,
                [B_ext[cur]] + [B_h0[4 * g + i] for i in range(4)], [B_dT])
            if bi == 0:
                for i in range(4):
                    tt("dve", Sw[:, i, 15:30], Sw[:, i, 15:30], cst[:, C_FIX + g * 15:C_FIX + (g + 1) * 15], ALU.mult,
                       [B_ext[cur], B_const], [B_ext[cur]])
                tt("dve", dT[:, :, 0:15], Sw[:, :, 15:30], h0[:, cs_, 0:15], ALU.subtract,
                   [B_ext[cur]] + [B_h0[4 * g + i] for i in range(4)], [B_dT])
            if has_s:
                b = pbank()
                for i in range(4):
                    c = 4 * g + i
                    for k in range(2):
                        mm(psb[b][:, i * 16:(i + 1) * 16], cache[k][:, c * 128:(c + 1) * 128], sel[:, k, g, :], k == 0, k == 1,
                           [B_cache, B_sel], [ps_buf[b]])
                for i in range(4):
                    c = 4 * g + i
                    stt("dve", dT[:, i, TM:T], h0[:, c, TM:T], 1.0 / w_ - 1.0, psb[b][:, i * 16:(i + 1) * 16], ALU.mult, ALU.add,
                        [B_h0[c], ps_buf[b]], [B_dT])
            w, wb = wload(pool_w[0, g].rearrange("(k p) n -> p k n", p=128), [128, 4, 512])
            for i in range(4):
                c = 4 * g + i
                ys, bys = ysg[c % 2], B_ysg[c % 2]
                for (a, bnd) in nt:
                    b = pbank()
                    for k in range(4):
                        mm(psb[b][:, 0:bnd - a], w[:, k, i * 128:(i + 1) * 128], dT[:, k, a:bnd], k == 0, k == 3, [wb, B_dT], [ps_buf[b]])
                    act(ys[:, a:bnd], psb[b][:, 0:bnd - a], AF.Copy, [ps_buf[b], B_par], [bys], scale=pscT[:, c:c + 1])
                ss_add(banks, T, ys[:, 0:T], [bys], sq, B_sq, c == 0, c == NCH - 1)
                S.dma("sp", Y_d[:, c, 0:T], ys[:, 0:T], reads=[bys])
        ss_close(banks, T, rstd_y, B_ry, float(D))
        cp("dve", poolc[:], h0[:, :, TM - 15:TM], list(B_h0), [B_poolc])
        if last:
            pp = PH.alloc([16, D], F32)
            B_pp = Buf("pp")
            for q4 in range(4):
                b = pbank()
                for jj in range(4):
                    c = q4 * 4 + jj
                    tr(psb[b][0:16, jj * 128:(jj + 1) * 128], h0[:, c, TM - 16:TM], ident, [B_h0[c], B_const], [ps_buf[b]])
                cp("act", pp[:, q4 * 512:(q4 + 1) * 512], psb[b][0:16, :], [ps_buf[b]], [B_pp])
            S.dma("sp", pool_p[:, :], pp[:], reads=[B_pp])
        if stop_after == "pool":
            return False

        HT_BYTES = NCH * TMAX * 2

        def rn_phase(lG, qG, lH, qA, qB, final=False):
            S.fence()
            PH.reset()
            hT = PH.alloc([128, NCH, TMAX], BF16)
            B_h = [Buf("h%d" % c) for c in range(NCH)]
            xT = PH.alloc([128, NCH, TMAX], F32)
            B_x = [Buf("x%d" % c) for c in range(NCH)]
            PHs["ystage"] = [PH.alloc([128, TMAX], F32) for _ in range(2)]
            PHs["B_ystage"] = [Buf("ys0"), Buf("ys1")]
            residual(lG, qG, T, xT, B_x, RSTDY, B_ry, True)
            if final:
                ost = [PH.alloc([128, D], F32) for _ in range(2)]
                B_ost = [Buf("ost0"), Buf("ost1")]
                for r in range(nrt):
                    n = min(128, T - r * 128)
                    o_, bo = ost[r % 2], B_ost[r % 2]
                    for q4 in range(4):
                        b = pbank()
                        for jj in range(4):
                            c = q4 * 4 + jj
                            tr(psb[b][0:n, jj * 128:(jj + 1) * 128], xT[:, c, r * 128:r * 128 + n], ident, [B_x[c], B_const], [ps_buf[b]])
                        cp("act" if q4 % 2 == 0 else "dve", o_[0:n, q4 * 512:(q4 + 1) * 512], psb[b][0:n, :], [ps_buf[b]], [bo])
                    if r < 8:
                        S.dma("sp", y_p[bi * TM + r * 128:bi * TM + (r + 1) * 128, :], o_[:, :], reads=[bo])
                    else:
                        S.dma("sp", y_s[:, :], o_[0:NS, :], reads=[bo])
                return None, None
            for c in range(NCH):
                S.dma("sp", X_d[:, c, 0:T], xT[:, c, 0:T], reads=[B_x[c]])
            rstd = PH.alloc([128, TMAX], F32)
            B_rstd = Buf("rstd")
            sq = PH.alloc([128, TMAX], BF16)
            B_sq = Buf("sq")
            tmp = PH.alloc([128, TMAX], F32)
            B_tmp = Buf("tmp")
            norm_of_x(xT, B_x, T, rstd, B_rstd, sq, B_sq)
            make_h(lH, qA, qB, xT, B_x, rstd, B_rstd, T, hT, B_h, tmp, B_tmp)
            return hT, B_h

        def ffn_phase(l, hT, B_h):
            S.fence()
            PH.reset()
            PH.off += HT_BYTES
            aT = PH.alloc([128, NFF, TMAX], BF16)
            B_a = [Buf("a%d" % c) for c in range(NFF)]
            ge = PH.alloc([128, 2 + TMAX], F32)
            B_ge = Buf("ge")
            tm = PH.alloc([128, TMAX], F32)
            B_tm = Buf("tm")
            if has_s:
                gcT = PH.alloc([128, NFF, 2 * NS], F32)
                gsT = PH.alloc([128, NFF, NS], F32)
                B_gcT, B_gsT = Buf("gcT"), Buf("gsT")
                S.dma("sp", ffn_s[l, :, 0, :], cffn_d[l, :, 1, :])
                gst = PH.alloc([2 * NS, 1408], F32)
                B_gst = Buf("gst")
                for pc in range(4):
                    for r_ in range(2):
                        S.dma("sp", gst[r_ * NS:(r_ + 1) * NS, :], cffn_d[l, :, r_, pc * 1408:(pc + 1) * 1408], writes=[B_gst])
                    for q in range(3):
                        b = pbank()
                        nn = 4 if q < 2 else 3
                        for jj in range(nn):
                            tr(psb[b][:, jj * 32:(jj + 1) * 32], gst[:, (q * 4 + jj) * 128:(q * 4 + jj + 1) * 128], ident[0:32, 0:32],
                               [B_gst, B_const], [ps_buf[b]])
                        c0 = pc * 11 + q * 4
                        cp("act", gcT[:, c0:c0 + nn, :], psb[b][:, 0:nn * 32].rearrange("p (c n) -> p c n", n=32), [ps_buf[b]], [B_gcT])
            wg_v = ffn_w_gate[l].rearrange("(k p) n -> p k n", p=128)
            wu_v = ffn_w_up[l].rearrange("(k p) n -> p k n", p=128)
            fw = lambda k, ch: fcwT[:, (l * 3 + k) * NFF + ch:(l * 3 + k) * NFF + ch + 1]
            for blk in range(22):
                wg, wgb = wload(wg_v[:, :, blk * 256:(blk + 1) * 256], [128, NCH, 256])
                wu, wub = wload(wu_v[:, :, blk * 256:(blk + 1) * 256], [128, NCH, 256])
                for jj in range(2):
                    ch = blk * 2 + jj
                    for (a, bnd) in nt:
                        b = pbank()
                        for k in range(NCH):
                            mm(psb[b][:, 0:bnd - a], wg[:, k, jj * 128:(jj + 1) * 128], hT[:, k, a:bnd], k == 0, k == NCH - 1,
                               [wgb, B_h[k]], [ps_buf[b]])
                        cp("act", ge[:, 2 + a:2 + bnd], psb[b][:, 0:bnd - a], [ps_buf[b]], [B_ge])
                    cp("dve", ge[:, 0:2], gatec[:, l, ch, :], [B_gatec], [B_ge])
                    cp("dve", gatec[:, l, ch, :], ge[:, TM:TM + 2], [B_ge], [B_gatec])
                    ts("dve", tm[:, 0:TM], ge[:, 2:2 + TM], fw(2, ch), ALU.mult, [B_ge, B_par], [B_tm])
                    stt("dve", tm[:, 0:TM], ge[:, 1:1 + TM], fw(1, ch), tm[:, 0:TM], ALU.mult, ALU.add, [B_ge, B_par, B_tm], [B_tm])
                    stt("dve", tm[:, 0:TM], ge[:, 0:TM], fw(0, ch), tm[:, 0:TM], ALU.mult, ALU.add, [B_ge, B_par, B_tm], [B_tm])
                    if has_s:
                        cp("dve", gsT[:, ch, :], ge[:, 2 + TM:2 + T], [B_ge], [B_gsT])
                        ts("dve", tm[:, TM:T], ge[:, 2 + TM:2 + T], fw(2, ch), ALU.mult, [B_ge, B_par], [B_tm])
                        stt("dve", tm[:, TM:T], gcT[:, ch, NS:2 * NS], fw(1, ch), tm[:, TM:T], ALU.mult, ALU.add, [B_gcT, B_par, B_tm], [B_tm])
                        stt("dve", tm[:, TM:T], gcT[:, ch, 0:NS], fw(0, ch), tm[:, TM:T], ALU.mult, ALU.add, [B_gcT, B_par, B_tm], [B_tm])
                    act(tm[:, 0:T], tm[:, 0:T], AF.Silu, [B_tm], [B_tm])
                    for (a, bnd) in nt:
                        b = pbank()
                        for k in range(NCH):
                            mm(psb[b][:, 0:bnd - a], wu[:, k, jj * 128:(jj + 1) * 128], hT[:, k, a:bnd], k == 0, k == NCH - 1,
                               [wub, B_h[k]], [ps_buf[b]])
                        tt("dve", aT[:, ch, a:bnd], tm[:, a:bnd], psb[b][:, 0:bnd - a], ALU.mult, [B_tm, ps_buf[b]], [B_a[ch]])
            if has_s:
                orow, B_orow = gst[0:NS, :], B_gst
            else:
                orow = PH.alloc([NS, 1408], F32)
                B_orow = Buf("orow")
            for pc in range(4):
                if has_s:
                    for q in range(3):
                        b = pbank()
                        nn = 4 if q < 2 else 3
                        for jj in range(nn):
                            tr(psb[b][0:NS, jj * 128:(jj + 1) * 128], gsT[:, pc * 11 + q * 4 + jj, :], ident, [B_gsT, B_const], [ps_buf[b]])
                        cp("act", orow[:, q * 512:q * 512 + nn * 128], psb[b][0:NS, 0:nn * 128], [ps_buf[b]], [B_orow])
                    S.dma("sp", ffn_s[l, :, 1, pc * 1408:(pc + 1) * 1408], orow[:, :], reads=[B_orow])
                if last:
                    for q in range(3):
                        b = pbank()
                        nn = 4 if q < 2 else 3
                        for jj in range(nn):
                            tr(psb[b][0:2, jj * 128:(jj + 1) * 128], gatec[:, l, pc * 11 + q * 4 + jj, :], ident, [B_gatec, B_const], [ps_buf[b]])
                        cp("act", orow[0:2, q * 512:q * 512 + nn * 128], psb[b][0:2, 0:nn * 128], [ps_buf[b]], [B_orow])
                    S.dma("sp", ffn_p[l, :, pc * 1408:(pc + 1) * 1408], orow[0:2, :], reads=[B_orow])
            wd_v = ffn_w_down[l].rearrange("(k p) n -> p k n", p=128)
            S.fence()
            PH.off = PH.base
            out_stream(T, NFF, lambda c: wd_v[:, :, c * 128:(c + 1) * 128], [128, NFF, 128], aT, B_a, RSTDY, B_ry)

        def delta_phase(hT, B_h):
            S.fence()
            PH.reset()
            PH.off += HT_BYTES
            ntile = 9 if has_s else 8
            win = dn_w_in[0].rearrange("(k p) n -> p k n", p=128)
            names = ("BETA", "G", "GC", "EGC", "KTS", "NBE", "EGL")
            SC = {nm: PH.alloc([128, 9, NH], F32) for nm in names}
            B_sc = Buf("scal")
            Sf = PH.alloc([128, NH, 128], F32)
            Sb = PH.alloc([128, NH, 128], BF16)
            B_S = [Buf("S%d" % h) for h in range(NH)]
            B_Sb = [Buf("Sb%d" % h) for h in range(NH)]
            if bi == 0:
                memset("dve", Sf[:], 0.0, B_S)
                memset("dve", Sb[:], 0.0, B_Sb)
            else:
                S.dma("sp", Sf[:], SD_d[:, :, :], writes=B_S)
                cp("act", Sb[:], Sf[:], B_S, B_Sb)
            t64 = PH.alloc([128, 64], F32)
            B_t64 = Buf("t64")
            w, wb = wload(win[:, :, 12288:12352], [128, NCH, 64])
            for n in range(ntile):
                m = 128 if n < 8 else NS
                a0 = n * 128
                b = pbank()
                for k in range(NCH):
                    mm(psb[b][0:m, 0:64], hT[:, k, a0:a0 + m], w[:, k, :], k == 0, k == NCH - 1, [wb, B_h[k]], [ps_buf[b]])
                act(SC["BETA"][0:m, n, :], psb[b][0:m, 0:32], AF.Sigmoid, [ps_buf[b]], [B_sc])
                tt("dve", t64[0:m, 0:32], psb[b][0:m, 32:64], hb[0:m, 32:64], ALU.add, [ps_buf[b], B_par], [B_t64])
                act(t64[0:m, 0:32], t64[0:m, 0:32], AF.Exp, [B_t64], [B_t64])
                act(t64[0:m, 0:32], t64[0:m, 0:32], AF.Ln, [B_t64, B_const], [B_t64], bias=one[0:m, :], scale=1.0)
                tt("dve", SC["G"][0:m, n, :], t64[0:m, 0:32], hb[0:m, 0:32], ALU.mult, [B_t64, B_par], [B_sc])
            for n in range(8):
                b = pbank()
                mm(psb[b][:, 0:32], tri, SC["G"][:, n, :], True, True, [B_const, B_sc], [ps_buf[b]])
                mm(psb[b][:, 32:64], onesf[:], SC["G"][:, n, :], True, True, [B_const, B_sc], [ps_buf[b]])
                cp("act", SC["GC"][:, n, :], psb[b][:, 0:32], [ps_buf[b]], [B_sc])
                act(SC["EGC"][:, n, :], psb[b][:, 0:32], AF.Exp, [ps_buf[b]], [B_sc])
                act(SC["EGL"][:, n, :], psb[b][:, 32:64], AF.Exp, [ps_buf[b]], [B_sc])
                tt("dve", SC["KTS"][:, n, :], psb[b][:, 32:64], SC["GC"][:, n, :], ALU.subtract, [ps_buf[b], B_sc], [B_sc])
                act(SC["KTS"][:, n, :], SC["KTS"][:, n, :], AF.Exp, [B_sc], [B_sc])
                stt("dve", SC["NBE"][:, n, :], SC["BETA"][:, n, :], -1.0, SC["EGC"][:, n, :], ALU.mult, ALU.mult, [B_sc], [B_sc])
            if has_s:
                S.dma("sp", conv_s[:, 0:2, :], sconv_d[:, 1:3, :])
                act(SC["EGC"][0:NS, 8, :], SC["G"][0:NS, 8, :], AF.Exp, [B_sc], [B_sc])
                rhsb = PH.alloc([NS, NH, NS], F32)
                B_rhsb = Buf("rhsb")
                BETAbc = PH.alloc([128, NH, NS], F32)
                EGbc = PH.alloc([128, NH, NS], F32)
                B_bc = Buf("bc")
                for (src, dst) in ((SC["BETA"], BETAbc), (SC["EGC"], EGbc)):
                    for h in range(NH):
                        ts("dve", rhsb[:, h, :], ident[0:NS, 0:NS], src[0:NS, 8, h:h + 1], ALU.mult, [B_const, B_sc], [B_rhsb])
                    b = pbank()
                    mm(psb[b][:, 0:512], onesf[0:NS, :], rhsb[:].rearrange("p h b -> p (h b)"), True, True, [B_const, B_rhsb], [ps_buf[b]])
                    cp("act", dst[:].rearrange("p h b -> p (h b)"), psb[b][:, 0:512], [ps_buf[b]], [B_bc])
            ge = PH.alloc([128, 3 + TMAX], F32)
            cv = PH.alloc([128, TMAX], F32)
            B_ge, B_cv = Buf("ge"), Buf("cv")
            sq = PH.alloc([128, TMAX], BF16)
            rq = ge
            B_sq, B_rq = Buf("sq"), B_ge
            qTn = PH.alloc([128, TMAX], BF16)
            kTn = PH.alloc([128, TMAX], BF16)
            B_q, B_k = Buf("qTn"), Buf("kTn")
            vT = PH.alloc([128, 2, TMAX], BF16)
            vTs = PH.alloc([128, 2, NS], F32)
            B_v = [Buf("v0"), Buf("v1")]
            zs = PH.alloc([128, 9, 256], BF16)
            B_zs = Buf("zs")
            onTg = PH.alloc([128, 2, TMAX], BF16)
            B_on = [Buf("on0"), Buf("on1")]
            mk = lambda dt: PH.alloc([128, 128], dt)
            mk4 = lambda dt: PH.alloc([128, 4, 128], dt)
            KT = PH.alloc([128, 16, 128], BF16)
            PQ = PH.alloc([128, 16, 128], BF16)
            YS = PH.alloc([128, 16, 128], BF16)
            VB = PH.alloc([128, 16, 128], BF16)
            B_KTq = [Buf("KT%d" % q) for q in range(4)]
            B_PQq = [Buf("PQ%d" % q) for q in range(4)]
            B_YSq = [Buf("YS%d" % q) for q in range(4)]
            B_VBq = [Buf("VB%d" % q) for q in range(4)]
            QCH = [(mk4(F32), mk4(F32), [mk4(BF16), mk4(BF16)], [mk4(BF16), mk4(BF16)], mk4(BF16)) for _ in range(2)]
            B_QCH = [(Buf("INA"), Buf("INB"), [Buf("LP0"), Buf("LP1")], [Buf("UP0"), Buf("UP1")], Buf("YW")) for _ in range(2)]
            MA4, MB4, I4 = mk4(F32), mk4(F32), mk4(BF16)
            ntri = mk(F32)
            for j in range(4):
                cp("dve", MA4[:, j, :], maskA, [B_const], [B_const])
                cp("dve", MB4[:, j, :], maskB, [B_const], [B_const])
                cp("dve", I4[:, j, :], identb[:], [B_const], [B_const])
            ts("dve", ntri[:], tri, -1.0, ALU.mult, [B_const], [B_const])
            SCN = [(mk(BF16), mk(BF16), mk(F32), mk(F32), mk(BF16), PH.alloc([128, 2], F32)) for _ in range(2)]
            B_SCN = [(Buf("R"), Buf("vn"), Buf("t1"), Buf("om"), Buf("onm"), Buf("ssn")) for _ in range(2)]

            def interleave(gens):
                gens = list(gens)
                while gens:
                    for g_ in list(gens):
                        try:
                            next(g_)
                        except StopIteration:
                            gens.remove(g_)

            if has_s:
                sctm = PH.alloc([48, 128], F32)
                scT = PH.alloc([128, 48], F32)
                B_sctm, B_scT = Buf("sctm"), Buf("scT")
                rsm = PH.alloc([NS, 128], F32)
                B_rsm = Buf("rsm")
                kqs = PH.alloc([128, NS, 2], F32)
                B_kqs = Buf("kqs")
                qkbc = PH.alloc([128, NS], F32)
                zsT = PH.alloc([128, 2, NS], F32)
                B_qkbc, B_zsT = Buf("qkbc"), Buf("zsT")
                Ss = PH.alloc([128, 8, 128], F32)
                B_Ss = Buf("Ss")
                Sn, B_Sn = Ss, B_Ss
                sm = {nm: PH.alloc([128, 8], F32) for nm in ("t", "vn", "o", "o2", "sq", "rn")}
                B_sm = Buf("sm")
                vntm = PH.alloc([8, 128], F32)
                B_vntm = Buf("vntm")
                prod = PH.alloc([128, NS], F32)
            dw = lambda k, cq: dcwT[:, k * 64 + cq:k * 64 + cq + 1]
            for gq in range(16):
                for ci in range(4):
                    cq = (gq, 16 + gq, 32 + 2 * gq, 33 + 2 * gq)[ci]
                    if ci < 2:
                        wci, wcib = wload(win[:, :, cq * 128:(cq + 1) * 128], [128, NCH, 128])
                        wc0 = 0
                    elif ci == 2:
                        wci, wcib = wload(win[:, :, cq * 128:(cq + 2) * 128], [128, NCH, 256])
                        wc0 = 0
                    else:
                        wc0 = 128
                    for (a, bnd) in nt:
                        b = pbank()
                        for k in range(NCH):
                            mm(psb[b][:, 0:bnd - a], wci[:, k, wc0:wc0 + 128], hT[:, k, a:bnd], k == 0, k == NCH - 1, [wcib, B_h[k]], [ps_buf[b]])
                        cp("act", ge[:, 3 + a:3 + bnd], psb[b][:, 0:bnd - a], [ps_buf[b]], [B_ge])
                    cp("dve", ge[:, 0:3], convc[:, cq, :], [B_convc], [B_ge])
                    cp("dve", convc[:, cq, :], ge[:, TM:TM + 3], [B_ge], [B_convc])
                    ts("dve", cv[:, 0:TM], ge[:, 3:3 + TM], dw(3, cq), ALU.mult, [B_ge, B_par], [B_cv])
                    for k in (2, 1, 0):
                        stt("dve", cv[:, 0:TM], ge[:, k:k + TM], dw(k, cq), cv[:, 0:TM], ALU.mult, ALU.add, [B_ge, B_par, B_cv], [B_cv])
                    if has_s:
                        for r_ in range(3):
                            S.dma("sp", sctm[r_ * NS:(r_ + 1) * NS, :], sconv_d[:, r_, cq * 128:(cq + 1) * 128], writes=[B_sctm])
                        b = pbank()
                        tr(psb[b][:, 0:48], sctm[:, :], ident[0:48, 0:48], [B_sctm, B_const], [ps_buf[b]])
                        cp("act", scT[:, :], psb[b][:, 0:48], [ps_buf[b]], [B_scT])
                        ts("dve", cv[:, TM:T], ge[:, 3 + TM:3 + T], dw(3, cq), ALU.mult, [B_ge, B_par], [B_cv])
                        for k in (2, 1, 0):
                            stt("dve", cv[:, TM:T], scT[:, k * NS:(k + 1) * NS], dw(k, cq), cv[:, TM:T], ALU.mult, ALU.add,
                                [B_scT, B_par, B_cv], [B_cv])
                        b = pbank()
                        tr(psb[b][0:NS, 0:128], ge[:, 3 + TM:3 + T], ident, [B_ge, B_const], [ps_buf[b]])
                        cp("act", rsm[:, :], psb[b][0:NS, 0:128], [ps_buf[b]], [B_rsm])
                        S.dma("sp", conv_s[:, 2, cq * 128:(cq + 1) * 128], rsm[:, :], reads=[B_rsm])
                    if ci < 2:
                        act(cv[:, 0:T], cv[:, 0:T], AF.Silu, [B_cv], [B_cv])
                        act(sq[:, 0:T], cv[:, 0:T], AF.Square, [B_cv], [B_sq])
                        for (a, bnd) in nt:
                            b = pbank()
                            mm(psb[b][:, 0:bnd - a], onesb[:], sq[:, a:bnd], True, True, [B_sq, B_const], [ps_buf[b]])
                            act(rq[:, a:bnd], psb[b][:, 0:bnd - a], AF.Sqrt, [ps_buf[b], B_const], [B_rq], bias=eps, scale=1.0)
                        recip(rq[:, 0:T], rq[:, 0:T], [B_rq], [B_rq])
                        if ci == 0:
                            stt("dve", qTn[:, 0:T], cv[:, 0:T], 128.0 ** -0.5, rq[:, 0:T], ALU.mult, ALU.mult, [B_cv, B_rq], [B_q])
                            if has_s:
                                stt("dve", kqs[:, :, 1], cv[:, TM:T], 128.0 ** -0.5, rq[:, TM:T], ALU.mult, ALU.mult, [B_cv, B_rq], [B_kqs])
                        else:
                            tt("dve", kTn[:, 0:T], cv[:, 0:T], rq[:, 0:T], ALU.mult, [B_cv, B_rq], [B_k])
                            if has_s:
                                tt("dve", kqs[:, :, 0], cv[:, TM:T], rq[:, TM:T], ALU.mult, [B_cv, B_rq], [B_kqs])
                    else:
                        act(vT[:, ci - 2, 0:T], cv[:, 0:T], AF.Silu, [B_cv], [B_v[ci - 2]])
                        if has_s:
                            act(vTs[:, ci - 2, :], cv[:, TM:T], AF.Silu, [B_cv], [B_v[ci - 2]])
                wz, wzb = wload(win[:, :, 8192 + gq * 256:8192 + (gq + 1) * 256], [128, NCH, 256])
                for n in range(ntile):
                    m = 128 if n < 8 else NS
                    a0 = n * 128
                    b = pbank()
                    for k in range(NCH):
                        mm(psb[b][0:m, 0:256], hT[:, k, a0:a0 + m], wz[:, k, :], k == 0, k == NCH - 1, [wzb, B_h[k]], [ps_buf[b]])
                    act(zs[0:m, n, :], psb[b][0:m, 0:256], AF.Silu, [ps_buf[b]], [B_zs])
                def pre_chain(c):
                    bks = [4 * c + j for j in range(4)]
                    rr = [0]

                    def nb():
                        b_ = bks[rr[0] % 4]
                        rr[0] += 1
                        return b_

                    v4 = lambda b_: psb[b_][:, 0:512].rearrange("p (j n) -> p j n", j=4)
                    v4h = lambda b_: psb16[b_][:, 0:512].rearrange("p (j n) -> p j n", j=4)
                    INA, INB, LP, UP, YW = QCH[c]
                    B_INA, B_INB, B_LP, B_UP, B_YW = B_QCH[c]
                    for q in range(c, 4, 2):
                        prs = [(4 * q + j, 2 * q + j // 2, j % 2) for j in range(4)]
                        jb = lambda j: slice(j * 128, (j + 1) * 128)
                        ck = lambda n: slice(n * 128, (n + 1) * 128)
                        b0 = nb()
                        for jn in range(2):
                            tr(psb16[b0][:, jb(jn)], kTn[:, ck(2 * q + jn)], identb[:], [B_k, B_const], [ps_buf[b0]])
                        for j, (p, n, i) in enumerate(prs):
                            h = 2 * gq + i
                            act(KT[:, p, :], psb16[b0][:, jb(j // 2)], AF.Copy, [ps_buf[b0], B_sc], [B_KTq[q]], scale=SC["KTS"][:, n, h:h + 1])
                        bd = nb()
                        for j, (p, n, i) in enumerate(prs):
                            h = 2 * gq + i
                            gcol = SC["G"][:, n, h:h + 1].broadcast_to([128, 128])
                            mm(psb[bd][:, jb(j)], gcol, tri, True, False, [B_sc, B_const], [ps_buf[bd]])
                            mm(psb[bd][:, jb(j)], ntri[:], gcol, False, True, [B_sc, B_const], [ps_buf[bd]])
                        yield
                        tt("dve", INA[:], v4(bd), MA4[:], ALU.add, [ps_buf[bd], B_const], [B_INA])
                        tt("dve", INB[:], v4(bd), MB4[:], ALU.add, [ps_buf[bd], B_const], [B_INB])
                        bkk = nb()
                        for j, (p, n, i) in enumerate(prs):
                            mm(psb[bkk][:, jb(j)], kTn[:, ck(n)], kTn[:, ck(n)], True, True, [B_k], [ps_buf[bkk]])
                        bqk = nb()
                        for j, (p, n, i) in enumerate(prs):
                            mm(psb[bqk][:, jb(j)], kTn[:, ck(n)], qTn[:, ck(n)], True, True, [B_k, B_q], [ps_buf[bqk]])
                        yield
                        act(INA[:], INA[:], AF.Exp, [B_INA], [B_INA], scale=-1.0)
                        act(INB[:], INB[:], AF.Exp, [B_INB], [B_INB])
                        yield
                        for j, (p, n, i) in enumerate(prs):
                            h = 2 * gq + i
                            stt("dve", LP[0][:, j, :], psb[bkk][:, jb(j)], SC["BETA"][:, n, h:h + 1], INA[:, j, :], ALU.mult, ALU.mult,
                                [ps_buf[bkk], B_sc, B_INA], [B_LP[0]])
                        tt("dve", PQ[:, 4 * q:4 * q + 4, :], v4(bqk), INB[:], ALU.mult, [ps_buf[bqk], B_INB], [B_PQq[q]])
                        yield
                        bu = nb()
                        for j in range(4):
                            tr(psb16[bu][:, jb(j)], LP[0][:, j, :], identb[:], [B_LP[0], B_const], [ps_buf[bu]])
                        yield
                        cp("act", UP[0][:], v4h(bu), [ps_buf[bu]], [B_UP[0]])
                        tt("dve", YW[:], I4[:], v4h(bu), ALU.subtract, [ps_buf[bu], B_const], [B_YW])
                        yield
                        cur = 0
                        for lev in range(6):
                            nx = 1 - cur
                            bP = nb()
                            for j in range(4):
                                mm(psb[bP][:, jb(j)], UP[cur][:, j, :], LP[cur][:, j, :], True, True, [B_UP[cur], B_LP[cur]], [ps_buf[bP]])
                            if lev < 5:
                                bU = nb()
                                for j in range(4):
                                    mm(psb[bU][:, jb(j)], LP[cur][:, j, :], UP[cur][:, j, :], True, True, [B_UP[cur], B_LP[cur]], [ps_buf[bU]])
                            yield
                            cp("act", LP[nx][:], v4(bP), [ps_buf[bP]], [B_LP[nx]])
                            if lev < 5:
                                cp("dve", UP[nx][:], v4(bU), [ps_buf[bU]], [B_UP[nx]])
                            yield
                            bY = nb()
                            for j in range(4):
                                mm(psb[bY][:, jb(j)], LP[nx][:, j, :], YW[:, j, :], True, True, [B_LP[nx], B_YW], [ps_buf[bY]])
                            yield
                            if lev < 5:
                                tt("dve", YW[:], YW[:], v4(bY), ALU.add, [B_YW, ps_buf[bY]], [B_YW])
                            else:
                                tt("dve", YS[:, 4 * q:4 * q + 4, :], YW[:], v4(bY), ALU.add, [B_YW, ps_buf[bY]], [B_YSq[q]])
                            cur = nx
                            yield
                        bv = nb()
                        for j, (p, n, i) in enumerate(prs):
                            tr(psb16[bv][:, jb(j)], vT[:, i, ck(n)], identb[:], [B_v[i], B_const], [ps_buf[bv]])
                        yield
                        for j, (p, n, i) in enumerate(prs):
                            h = 2 * gq + i
                            if j % 2 == 0:
                                act(VB[:, p, :], psb16[bv][:, jb(j)], AF.Copy, [ps_buf[bv], B_sc], [B_VBq[q]], scale=SC["BETA"][:, n, h:h + 1])
                            else:
                                ts("dve", VB[:, p, :], psb16[bv][:, jb(j)], SC["BETA"][:, n, h:h + 1], ALU.mult, [ps_buf[bv], B_sc], [B_VBq[q]])
                        yield

                def scan_chain(i):
                    bk = [4 * i + j for j in range(4)]
                    h = 2 * gq + i
                    Rm, vn, t1, om, onm, ssn = SCN[i]
                    B_R, B_vn, B_t1, B_om, B_onm, B_ssn = B_SCN[i]
                    for n in range(8):
                        p = 2 * n + i
                        c0, c1 = n * 128, (n + 1) * 128
                        sc = (lambda n: (lambda nm: SC[nm][:, n, h:h + 1]))(n)
                        mm(psb[bk[0]][:, 0:128], kTn[:, c0:c1], Sb[:, h, :], True, True, [B_k, B_Sb[h]], [ps_buf[bk[0]]])
                        mm(psb[bk[2]][:, 0:128], qTn[:, c0:c1], Sb[:, h, :], True, True, [B_q, B_Sb[h]], [ps_buf[bk[2]]])
                        yield
                        stt("dve", Rm[:], psb[bk[0]][:, 0:128], sc("NBE"), VB[:, p, :], ALU.mult, ALU.add, [ps_buf[bk[0]], B_sc, B_VBq[p // 4]], [B_R])
                        act(t1[:], psb[bk[2]][:, 0:128], AF.Copy, [ps_buf[bk[2]], B_sc], [B_t1], scale=sc("EGC"))
                        yield
                        mm(psb[bk[1]][:, 0:128], YS[:, p, :], Rm[:], True, True, [B_YSq[p // 4], B_R], [ps_buf[bk[1]]])
                        yield
                        cp("act", vn[:], psb[bk[1]][:, 0:128], [ps_buf[bk[1]]], [B_vn])
                        yield
                        mm(psb[bk[3]][:, 0:128], PQ[:, p, :], vn[:], True, True, [B_PQq[p // 4], B_vn], [ps_buf[bk[3]]])
                        mm(psb[bk[0]][:, 0:128], KT[:, p, :], vn[:], True, True, [B_KTq[p // 4], B_vn], [ps_buf[bk[0]]])
                        yield
                        tt("dve", om[:], t1[:], psb[bk[3]][:, 0:128], ALU.add, [B_t1, ps_buf[bk[3]]], [B_om])
                        stt("dve", Sf[:, h, :], Sf[:, h, :], sc("EGL"), psb[bk[0]][:, 0:128], ALU.mult, ALU.add, [B_S[h], B_sc, ps_buf[bk[0]]], [B_S[h]])
                        yield
                        cp("act", Sb[:, h, :], Sf[:, h, :], [B_S[h]], [B_Sb[h]])
                        act(t1[:], om[:], AF.Square, [B_om], [B_t1, B_ssn], accum=ssn[:, 0:1])
                        yield
                        act(ssn[:, 0:1], ssn[:, 0:1], AF.Sqrt, [B_ssn, B_const], [B_ssn], bias=eps, scale=1.0 / 128.0)
                        yield
                        recip(ssn[:, 0:1], ssn[:, 0:1], [B_ssn], [B_ssn])
                        yield
                        stt("dve", om[:], om[:], ssn[:, 0:1], dnwbc[:], ALU.mult, ALU.mult, [B_om, B_ssn, B_par], [B_om])
                        yield
                        tt("dve", onm[:], om[:], zs[:, n, i * 128:(i + 1) * 128], ALU.mult, [B_om, B_zs], [B_onm])
                        yield
                        tr(psb16[bk[1]][:, 0:128], onm[:], identb[:], [B_onm, B_const], [ps_buf[bk[1]]])
                        yield
                        cp("act", onTg[:, i, c0:c1], psb16[bk[1]][:, 0:128], [ps_buf[bk[1]]], [B_on[i]])
                        yield

                interleave([pre_chain(c) for c in range(2)])
                interleave([scan_chain(i) for i in range(2)])
                if has_s:
                    tt("dve", prod[:, :], kqs[:, :, 0], kqs[:, :, 1], ALU.mult, [B_kqs], [B_prod])
                    b = pbank()
                    mm(psb[b][:, 0:NS], onesf[:], prod[:, :], True, True, [B_prod, B_const], [ps_buf[b]])
                    cp("act", qkbc[:, :], psb[b][:, 0:NS], [ps_buf[b]], [B_qkbc])
                    for i in range(2):
                        b = pbank()
                        tr(psb16[b][:, 0:NS], zs[0:NS, 8, i * 128:(i + 1) * 128], identb[0:NS, 0:NS], [B_zs, B_const], [ps_buf[b]])
                        cp("act", zsT[:, i, :], psb16[b][:, 0:NS], [ps_buf[b]], [B_zsT])
                    for sub in range(4):
                        i, b0 = sub // 2, (sub % 2) * 8
                        h = 2 * gq + i
                        S.dma("sp", Ss[:, :, :], srec_d[b0:b0 + 8, h].rearrange("b k v -> k b v"), writes=[B_Ss])
                        bp = pbank()
                        for j in range(8):
                            mm(psb[bp][:, 2 * j:2 * j + 2], Ss[:, j, :], kqs[:, b0 + j, :], True, True, [B_Ss, B_kqs], [ps_buf[bp]])
                        KQ = psb[bp][:, 0:16].rearrange("p (j two) -> p j two", two=2)
                        eg = EGbc[:, h, b0:b0 + 8]
                        tt("dve", sm["t"][:, :], KQ[:, :, 0], eg, ALU.mult, [ps_buf[bp], B_bc], [B_sm])
                        tt("dve", sm["t"][:, :], vTs[:, i, b0:b0 + 8], sm["t"][:, :], ALU.subtract, [B_v[i], B_sm], [B_sm])
                        tt("dve", sm["vn"][:, :], sm["t"][:, :], BETAbc[:, h, b0:b0 + 8], ALU.mult, [B_sm, B_bc], [B_sm])
                        tt("dve", sm["o"][:, :], KQ[:, :, 1], eg, ALU.mult, [ps_buf[bp], B_bc], [B_sm])
                        tt("dve", sm["o2"][:, :], sm["vn"][:, :], qkbc[:, b0:b0 + 8], ALU.mult, [B_sm, B_qkbc], [B_sm])
                        tt("dve", sm["o"][:, :], sm["o"][:, :], sm["o2"][:, :], ALU.add, [B_sm], [B_sm])
                        tt("dve", sm["sq"][:, :], sm["o"][:, :], sm["o"][:, :], ALU.mult, [B_sm], [B_sm])
                        b = pbank()
                        mm(psb[b][:, 0:8], onesf[:], sm["sq"][:, :], True, True, [B_sm, B_const], [ps_buf[b]])
                        act(sm["rn"][:, :], psb[b][:, 0:8], AF.Sqrt, [ps_buf[b], B_const], [B_sm], bias=eps, scale=1.0 / 128.0)
                        recip(sm["rn"][:, :], sm["rn"][:, :], [B_sm], [B_sm])
                        stt("dve", sm["o"][:, :], sm["o"][:, :], dnwcol[:, 0:1], sm["rn"][:, :], ALU.mult, ALU.mult, [B_sm, B_par], [B_sm])
                        tt("dve", onTg[:, i, TM + b0:TM + b0 + 8], sm["o"][:, :], zsT[:, i, b0:b0 + 8], ALU.mult, [B_sm, B_zsT], [B_on[i]])
                        bt = pbank()
                        tr(psb[bt][0:8, 0:128], sm["vn"][:, :], ident, [B_sm, B_const], [ps_buf[bt]])
                        cp("act", vntm[:, :], psb[bt][0:8, 0:128], [ps_buf[bt]], [B_vntm])
                        for j in range(8):
                            bb = pbank()
                            mm(psb[bb][:, 0:128], ident[0:8, j:j + 1].broadcast_to([8, 128]), vntm[:, :], True, True, [B_vntm, B_const], [ps_buf[bb]])
                            act(Sn[:, j, :], Ss[:, j, :], AF.Copy, [B_Ss, B_bc], [B_Sn], scale=EGbc[:, h, b0 + j:b0 + j + 1])
                            stt("dve", Sn[:, j, :], psb[bb][:, 0:128], kqs[:, b0 + j, 0:1], Sn[:, j, :], ALU.mult, ALU.add,
                                [ps_buf[bb], B_kqs, B_Sn], [B_Sn])
                        S.dma("sp", rec_s[b0:b0 + 8, h].rearrange("b k v -> k b v"), Sn[:, :, :], reads=[B_Sn])
                for i in range(2):
                    S.dma("sp", ON_d[:, 2 * gq + i, 0:T], onTg[:, i, 0:T], reads=[B_on[i]])
            if last:
                S.dma("sp", rec_p.rearrange("h k v -> k h v"), Sf[:, :, :], reads=B_S)
                cpt = PH.alloc([3, 2048], F32)
                B_cpt = Buf("cpt")
                for q in range(4):
                    for q4 in range(4):
                        b = pbank()
                        for jj in range(4):
                            tr(psb[b][0:3, jj * 128:(jj + 1) * 128], convc[:, q * 16 + q4 * 4 + jj, :], ident, [B_convc, B_const], [ps_buf[b]])
                        cp("act", cpt[:, q4 * 512:(q4 + 1) * 512], psb[b][0:3, :], [ps_buf[b]], [B_cpt])
                    S.dma("sp", conv_p[:, q * 2048:(q + 1) * 2048], cpt[:, :], reads=[B_cpt])
            else:
                S.dma("sp", SD_d[:, :, :], Sf[:, :, :], reads=B_S)

        def outproj_phase():
            S.fence()
            PH.reset()
            onT = PH.alloc([128, NH, TMAX], BF16)
            B_onT = [Buf("onT%d" % h) for h in range(NH)]
            for h in range(NH):
                S.dma("sp", onT[:, h, 0:T], ON_d[:, h, 0:T], writes=[B_onT[h]])
            wo_v = dn_w_out[0].rearrange("(k p) n -> p k n", p=128)
            out_stream(T, NH, lambda c: wo_v[:, :, c * 128:(c + 1) * 128], [128, NH, 128], onT, B_onT, RSTDY, B_ry)

        PH_prod = None
        B_prod = Buf("prod")
        hT, B_h = rn_phase(0, 2, 0, 4, 3)
        ffn_phase(0, hT, B_h)
        if stop_after == "ffn0":
            return False
        hT, B_h = rn_phase(0, 5, 1, 1, 0)
        if stop_after == "rn2":
            return False
        if has_s:
            pass
        delta_phase(hT, B_h)
        outproj_phase()
        if stop_after == "oproj":
            return False
        hT, B_h = rn_phase(1, 2, 1, 4, 3)
        ffn_phase(1, hT, B_h)
        rn_phase(1, 5, None, None, None, final=True)
        return True

    if stop_after != "pro" and run_block(0):
        run_block(1)

    S.resolve()
    S.emit(nc)
    return nc, dbg_outs


_PROG = {}
W_NAMES = ("norm_w", "ada_w", "ada_b", "pool_w", "pool_scale", "dn_w_in", "dn_conv_w", "dn_a_log", "dn_dt_bias",
           "dn_norm_w", "dn_w_out", "ffn_w_gate", "ffn_w_up", "ffn_conv_w", "ffn_w_down")


def make_in_maps(inputs, ncores=8):
    f = lambda a: np.ascontiguousarray(np.asarray(a, dtype=np.float32))
    consts = host_consts()
    sel = host_poolsel()
    maps = []
    for c in range(ncores):
        b = c % 4
        sl = slice(NS * c, NS * (c + 1))
        xp = np.asarray(inputs["x_prompt"])[b]
        xs = np.asarray(inputs["x_sample"])[sl, 0, :]
        m = {
            "xin": f(np.concatenate([xp[0:TM], xs, xp[TM:2 * TM]], axis=0)),
            "cin": f(np.concatenate([np.asarray(inputs["c_sample"])[sl], np.asarray(inputs["c_prompt"])[b:b + 1]], axis=0)),
            "consts": consts,
            "poolsel": sel,
            "cache_pool_c": f(np.asarray(inputs["cache_pool"])[0, sl]),
            "state_conv_c": f(np.asarray(inputs["state_conv"])[0, sl]),
            "state_rec_c": f(np.asarray(inputs["state_rec"])[0, sl]),
            "cache_ffn_c": f(np.asarray(inputs["cache_ffn_conv"])[:, sl]),
        }
        for nm in W_NAMES:
            m[nm] = f(inputs[nm])
        maps.append(m)
    return maps


def kernel(**inputs):
    if "nc" not in _PROG:
        _PROG["nc"] = build_program()[0]
    nc = _PROG["nc"]
    maps = make_in_maps(inputs)
    res = run_bass_kernel_spmd(nc, maps, core_ids=list(range(8))).results
    y_prompt = np.stack([res[b]["y_p"] for b in range(4)], 0)
    y_sample = np.concatenate([res[c]["y_s"] for c in range(8)], 0)[:, None, :]
    pool_prompt = np.stack([res[b]["pool_p"][1:16] for b in range(4)], 0)[None]
    pool_sample = np.concatenate([res[c]["pool_s"] for c in range(8)], 0)[None]
    conv_prompt = np.stack([res[b]["conv_p"] for b in range(4)], 0)[None]
    conv_sample = np.concatenate([res[c]["conv_s"] for c in range(8)], 0)[None]
    rec_prompt = np.stack([res[b]["rec_p"] for b in range(4)], 0)[None]
    rec_sample = np.concatenate([res[c]["rec_s"] for c in range(8)], 0)[None]
    ffn_prompt = np.stack([res[b]["ffn_p"] for b in range(4)], 1)
    ffn_sample = np.concatenate([res[c]["ffn_s"] for c in range(8)], 1)
    outs = (y_prompt, y_sample, pool_prompt, pool_sample, conv_prompt, conv_sample, rec_prompt, rec_sample,
            ffn_prompt, ffn_sample)
    return tuple(np.ascontiguousarray(o, dtype=np.float32) for o in outs)
```

```python
import numpy as np
import concourse.bass as bass
import concourse.mybir as mybir
from concourse.bass_utils import run_bass_kernel_spmd

F32 = mybir.dt.float32
BF16 = mybir.dt.bfloat16
AF = mybir.ActivationFunctionType
ALU = mybir.AluOpType

D = 2048
NCH = 16
DFF = 5632
NFF = 44
NH = 32
EPS = 1e-6
TM = 1024
NS = 16
CONV_DIM = 8192
PROJ = 12352
BIG = 30000.0


class Buf:
    __slots__ = ("name", "excl")

    def __init__(self, name="", excl=False):
        self.name = name
        self.excl = excl


class Op:
    __slots__ = ("eng", "fn", "reads", "writes", "dma", "nofence", "deps", "signal", "count", "semidx", "waits")

    def __init__(self, eng, fn, reads, writes, dma, nofence):
        self.eng, self.fn, self.reads, self.writes, self.dma, self.nofence = eng, fn, reads, writes, dma, nofence
        self.deps = set()
        self.signal = False
        self.count = 0
        self.semidx = 0
        self.waits = []


ENGS = ("pe", "act", "dve", "pool", "sp")
NDSEM = 8


class Sched:
    def __init__(self):
        self.ops = []
        self.fences = []

    def op(self, eng, fn, reads=(), writes=(), nofence=False):
        reads, writes = list(reads), list(writes)
        for b in reads:
            if b.excl and b not in writes:
                writes.append(b)
        self.ops.append(Op(eng, fn, reads, writes, False, nofence))

    def dma(self, eng, out, in_, reads=(), writes=(), nofence=False):
        self.ops.append(Op(eng, (lambda e: e.dma_start(out=out, in_=in_)), list(reads), list(writes), True, nofence))

    def fence(self):
        self.fences.append(len(self.ops))

    def resolve(self):
        ops = self.ops
        last_w, readers = {}, {}
        last_comp = {}
        recent_dma = {e: [] for e in ENGS}
        pending = {e: None for e in ENGS}
        fset = set(self.fences)
        for i, op in enumerate(ops):
            if i in fset:
                F = set(last_comp.values())
                for e in ENGS:
                    F.update(recent_dma[e])
                for e in ENGS:
                    pending[e] = set(F) if pending[e] is None else (pending[e] | F)
            deps = set()
            for b in op.reads:
                if b in last_w:
                    deps.add(last_w[b])
            for b in op.writes:
                if b in last_w:
                    deps.add(last_w[b])
                deps.update(readers.get(b, ()))
            if not op.nofence and pending[op.eng] is not None:
                deps.update(pending[op.eng])
                pending[op.eng] = None
            deps.discard(i)
            op.deps = set(p for p in deps if not (ops[p].eng == op.eng == "pe"))
            for b in op.reads:
                readers.setdefault(b, []).append(i)
            for b in op.writes:
                last_w[b] = i
                readers[b] = []
            if op.dma:
                recent_dma[op.eng].append(i)
                if len(recent_dma[op.eng]) > NDSEM:
                    recent_dma[op.eng].pop(0)
            else:
                last_comp[op.eng] = i
        for op in ops:
            for p in op.deps:
                ops[p].signal = True
        cnt = {e: 0 for e in ENGS}
        ndma = {e: 0 for e in ENGS}
        for op in ops:
            if op.dma:
                j = ndma[op.eng]
                op.semidx = j % NDSEM
                op.count = 16 * (j // NDSEM + 1)
                ndma[op.eng] += 1
            elif op.signal:
                cnt[op.eng] += 1
                op.count = cnt[op.eng]
        known = {e: {} for e in ENGS}
        for op in ops:
            need = {}
            if op.dma and op.count > 16:
                need[(op.eng, "d", op.semidx)] = op.count - 16
            for p in op.deps:
                po = ops[p]
                key = (po.eng, "d", po.semidx) if po.dma else (po.eng, "c", 0)
                if need.get(key, 0) < po.count:
                    need[key] = po.count
            kn = known[op.eng]
            op.waits = []
            for key, val in need.items():
                if kn.get(key, 0) < val:
                    kn[key] = val
                    op.waits.append((key, val))
        self.final_dma = {e: ndma[e] for e in ENGS}

    def emit(self, nc):
        import contextlib
        with contextlib.ExitStack() as st:
            csem = {e: st.enter_context(nc.semaphore("c_" + e)) for e in ("pe", "act", "dve", "pool")}
            dsem = {e: [st.enter_context(nc.semaphore("d_%s%d" % (e, i))) for i in range(NDSEM)] for e in ("pool", "sp")}
            block = st.enter_context(nc.Block())
            ops = self.ops
            final_dma = self.final_dma

            def semof(key):
                return dsem[key[0]][key[2]] if key[1] == "d" else csem[key[0]]

            def run(ename, e):
                for op in ops:
                    if op.eng != ename:
                        continue
                    for key, val in op.waits:
                        e.wait_ge(semof(key), val)
                    ins = op.fn(e)
                    if op.dma:
                        ins.then_inc(dsem[ename][op.semidx], 16)
                    elif op.signal:
                        ins.then_inc(csem[ename], 1)
                if ename == "sp":
                    for q in ("pool", "sp"):
                        n = final_dma[q]
                        for r in range(NDSEM):
                            k = (n - r + NDSEM - 1) // NDSEM if n > r else 0
                            if k > 0:
                                e.wait_ge(dsem[q][r], 16 * k)

            @block.tensor
            def _(e):
                run("pe", e)

            @block.scalar
            def _(e):
                run("act", e)

            @block.vector
            def _(e):
                run("dve", e)

            @block.gpsimd
            def _(e):
                run("pool", e)

            @block.sync
            def _(e):
                run("sp", e)


POOL_W = (2, 4, 8, 16)
C_ID, C_TRI, C_MA, C_MB, C_FIX, C_END = 0, 128, 256, 384, 512, 576


def host_consts():
    c = np.zeros((128, C_END), np.float32)
    i = np.arange(128)
    c[:, C_ID:C_ID + 128] = np.eye(128, dtype=np.float32)
    c[:, C_TRI:C_TRI + 128] = (i[:, None] <= i[None, :]).astype(np.float32)
    c[:, C_MA:C_MA + 128] = np.where(i[None, :] >= i[:, None], BIG, 0.0)
    c[:, C_MB:C_MB + 128] = np.where(i[None, :] < i[:, None], -BIG, 0.0)
    for g, w in enumerate(POOL_W):
        t = np.arange(15)
        c[:, C_FIX + g * 15:C_FIX + (g + 1) * 15] = (1.0 / np.minimum(t + 1, w))[None, :]
    return c


def host_poolsel():
    s = np.zeros((2, 120, 4, 16), np.float32)
    for k in range(2):
        for bl in range(8):
            for r in range(15):
                for g, w in enumerate(POOL_W):
                    if r >= 15 - (w - 1):
                        s[k, bl * 15 + r, g, 8 * k + bl] = 1.0 / w
    return s


def build_program(debug=False, stop_after=None):
    nc = bass.Bass("TRN2", target_bir_lowering=False)
    S = Sched()
    dbg_outs = []

    def din(name, shape):
        return nc.dram_tensor(name, list(shape), F32, kind="ExternalInput").ap()

    def dout(name, shape):
        return nc.dram_tensor(name, list(shape), F32, kind="ExternalOutput").ap()

    def dscr(name, shape, dt=F32):
        if debug:
            dbg_outs.append(name)
            return nc.dram_tensor(name, list(shape), dt, kind="ExternalOutput").ap()
        return nc.dram_tensor(name, list(shape), dt).ap()

    xin = din("xin", [2 * TM + NS, D])
    cin = din("cin", [NS + 1, D])
    consts_d = din("consts", [128, C_END])
    poolsel_d = din("poolsel", [2, 120, 4, 16])
    cpool_d = din("cache_pool_c", [NS, 15, D])
    sconv_d = din("state_conv_c", [NS, 3, CONV_DIM])
    srec_d = din("state_rec_c", [NS, NH, 128, 128])
    cffn_d = din("cache_ffn_c", [2, NS, 2, DFF])
    norm_w = din("norm_w", [2, 4, D])
    ada_w = din("ada_w", [2, D, 6 * D])
    ada_b = din("ada_b", [2, 6 * D])
    pool_w = din("pool_w", [1, 4, 512, 512])
    pool_scale = din("pool_scale", [1, D])
    dn_w_in = din("dn_w_in", [1, D, PROJ])
    dn_conv_w = din("dn_conv_w", [1, 4, CONV_DIM])
    dn_a_log = din("dn_a_log", [1, NH])
    dn_dt_bias = din("dn_dt_bias", [1, NH])
    dn_norm_w = din("dn_norm_w", [1, 128])
    dn_w_out = din("dn_w_out", [1, 4096, D])
    ffn_w_gate = din("ffn_w_gate", [2, D, DFF])
    ffn_w_up = din("ffn_w_up", [2, D, DFF])
    ffn_conv_w = din("ffn_conv_w", [2, 3, DFF])
    ffn_w_down = din("ffn_w_down", [2, DFF, D])

    y_p = dout("y_p", [2 * TM, D])
    y_s = dout("y_s", [NS, D])
    pool_p = dout("pool_p", [16, D])
    pool_s = dout("pool_s", [NS, 15, D])
    conv_p = dout("conv_p", [3, CONV_DIM])
    conv_s = dout("conv_s", [NS, 3, CONV_DIM])
    rec_p = dout("rec_p", [NH, 128, 128])
    rec_s = dout("rec_s", [NS, NH, 128, 128])
    ffn_p = dout("ffn_p", [2, 2, DFF])
    ffn_s = dout("ffn_s", [2, NS, 2, DFF])

    TMAX = TM + NS
    X_d = dscr("X_scr", [128, NCH, TMAX])
    Y_d = dscr("Y_scr", [128, NCH, TMAX])
    ON_d = dscr("ON_scr", [128, NH, TMAX], BF16)
    SD_d = dscr("SD_scr", [128, NH, 128])

    class Arena:
        def __init__(self, base, limit):
            self.base, self.limit, self.off, self.n = base, limit, base, 0

        def reset(self):
            self.off = self.base

        def alloc(self, shape, dt):
            nbytes = int(np.prod(shape[1:])) * (4 if dt == F32 else 2)
            nbytes = (nbytes + 63) // 64 * 64
            off = self.off
            assert off + nbytes <= self.limit, ("SBUF arena overflow", shape, off, nbytes, self.limit)
            self.off += nbytes
            self.n += 1
            return nc.alloc_sbuf_tensor_at("t%d" % self.n, list(shape), dt, offset=off)

    PERS = Arena(16512, 16512 + 60 * 1024)
    PH = Arena(16512 + 60 * 1024, 229376)

    cst = PERS.alloc([128, C_END], F32)
    ident = cst[:, C_ID:C_ID + 128]
    tri = cst[:, C_TRI:C_TRI + 128]
    maskA = cst[:, C_MA:C_MA + 128]
    maskB = cst[:, C_MB:C_MB + 128]
    identb = PERS.alloc([128, 128], BF16)
    onesb = PERS.alloc([128, 128], BF16)
    onesf = PERS.alloc([128, 128], F32)
    epsT = PERS.alloc([128, 2], F32)
    normwT = PERS.alloc([128, 128], F32)
    pscT = PERS.alloc([128, 16], F32)
    adabT = PERS.alloc([128, 192], F32)
    fcwT = PERS.alloc([128, 264], F32)
    dcwT = PERS.alloc([128, 256], F32)
    dnwbc = PERS.alloc([128, 128], F32)
    hb = PERS.alloc([128, 64], F32)
    MOD = PERS.alloc([128, 2, 96, NS + 1], F32)
    poolc = PERS.alloc([128, NCH, 15], F32)
    gatec = PERS.alloc([128, 2, NFF, 2], F32)
    convc = PERS.alloc([128, 64, 3], F32)
    RSTDY = PERS.alloc([128, TM + NS], F32)
    dnwcol = PERS.alloc([128, 2], F32)
    RING_BYTES = 8192
    NRING = 4
    ring_off = []
    for r in range(NRING):
        t = PERS.alloc([128, RING_BYTES // 2], BF16)
        ring_off.append(PERS.off - RING_BYTES)
    ring_buf = [Buf("ring%d" % r) for r in range(NRING)]
    ring_views = {}
    ring_next = [0]
    B_const = Buf("const")
    B_par = Buf("par")
    B_mod = Buf("mod")
    B_poolc, B_gatec, B_convc = Buf("poolc"), Buf("gatec"), Buf("convc")
    B_ry = Buf("rstdy")

    def wload(dram_view, shape, alt=None):
        if alt is None:
            offs, bufs, nxt, views, n = ring_off, ring_buf, ring_next, ring_views, NRING
        else:
            offs, bufs, nxt, views, n = alt
        r = nxt[0] % n
        nxt[0] += 1
        key = (r, tuple(shape))
        if key not in views:
            views[key] = nc.alloc_sbuf_tensor_at("rv%d_%d_%d" % (r, len(views), offs[r]), list(shape), BF16, offset=offs[r])
        v = views[key]
        S.dma("pool", v[:], dram_view, writes=[bufs[r]], nofence=True)
        return v, bufs[r]

    psb = [nc.alloc_psum_tensor("ps%d" % i, [128, 512], F32) for i in range(8)]
    psb16 = [p.bitcast(BF16) for p in psb]
    ps_buf = [Buf("ps%d" % i, excl=True) for i in range(8)]
    ps_rr = [0]
    ps_pool = [list(range(8))]

    def pbank():
        lst = ps_pool[0]
        i = lst[ps_rr[0] % len(lst)]
        ps_rr[0] += 1
        return i

    def mm(out, lhsT, rhs, start, stop, reads, writes):
        S.op("pe", lambda e: e.matmul(out, lhsT=lhsT, rhs=rhs, start=start, stop=stop), reads, writes)

    def tr(out, in_, idn, reads, writes):
        S.op("pe", lambda e: e.transpose(out=out, in_=in_, identity=idn), reads, writes)

    def act(out, in_, func, reads, writes, bias=None, scale=None, accum=None):
        kw = {}
        if bias is not None:
            kw["bias"] = bias
        if scale is not None:
            kw["scale"] = scale
        if accum is not None:
            kw["accum_out"] = accum
        S.op("act", lambda e: e.activation(out=out, in_=in_, func=func, **kw), reads, writes)

    def tt(eng, out, in0, in1, op, reads, writes):
        S.op(eng, lambda e: e.tensor_tensor(out=out, in0=in0, in1=in1, op=op), reads, writes)

    def ts(eng, out, in0, s1, op0, reads, writes, s2=None, op1=None):
        if op1 is None:
            S.op(eng, lambda e: e.tensor_scalar(out=out, in0=in0, scalar1=s1, scalar2=None, op0=op0), reads, writes)
        else:
            S.op(eng, lambda e: e.tensor_scalar(out=out, in0=in0, scalar1=s1, scalar2=s2, op0=op0, op1=op1), reads, writes)

    def stt(eng, out, in0, scalar, in1, op0, op1, reads, writes):
        S.op(eng, lambda e: e.scalar_tensor_tensor(out=out, in0=in0, scalar=scalar, in1=in1, op0=op0, op1=op1), reads, writes)

    def cp(eng, out, in_, reads, writes):
        if eng == "act":
            act(out, in_, AF.Copy, reads, writes)
        else:
            S.op(eng, lambda e: e.tensor_copy(out=out, in_=in_), reads, writes)

    def recip(out, in_, reads, writes):
        S.op("dve", lambda e: e.reciprocal(out=out, in_=in_), reads, writes)

    def memset(eng, ap, val, writes):
        S.op(eng, lambda e: e.memset(ap, val), [], writes)

    S.dma("sp", cst[:], consts_d[:, :], writes=[B_const])
    memset("dve", onesf[:], 1.0, [B_const])
    memset("dve", onesb[:], 1.0, [B_const])
    memset("dve", epsT[:, 0:1], EPS, [B_const])
    memset("dve", epsT[:, 1:2], 1.0, [B_const])
    cp("dve", identb[:], ident, [B_const], [B_const])
    memset("dve", poolc[:], 0.0, [B_poolc])
    memset("dve", gatec[:], 0.0, [B_gatec])
    memset("dve", convc[:], 0.0, [B_convc])
    eps = epsT[:, 0:1]
    one = epsT[:, 1:2]

    PH.reset()
    stg = PH.alloc([128, 9, 128], F32)
    B_stg = Buf("stg")
    prm = [
        (norm_w.rearrange("l i (c p) -> (l i c) p", p=128), 128, normwT[:, 0:128]),
        (pool_scale.rearrange("o (c p) -> (o c) p", p=128), 16, pscT[:, 0:16]),
        (ada_b[0].rearrange("(c p) -> c p", p=128), 96, adabT[:, 0:96]),
        (ada_b[1].rearrange("(c p) -> c p", p=128), 96, adabT[:, 96:192]),
        (ffn_conv_w.rearrange("l k (c p) -> (l k c) p", p=128)[0:88], 88, fcwT[:, 0:88]),
        (ffn_conv_w.rearrange("l k (c p) -> (l k c) p", p=128)[88:176], 88, fcwT[:, 88:176]),
        (ffn_conv_w.rearrange("l k (c p) -> (l k c) p", p=128)[176:264], 88, fcwT[:, 176:264]),
        (dn_conv_w.rearrange("o k (c p) -> (o k c) p", p=128)[0:128], 128, dcwT[:, 0:128]),
        (dn_conv_w.rearrange("o k (c p) -> (o k c) p", p=128)[128:256], 128, dcwT[:, 128:256]),
    ]
    for i, (src, n, dst) in enumerate(prm):
        S.dma("sp", stg[0:n, i, :], src, writes=[B_stg])
    for i, (src, n, dst) in enumerate(prm):
        b = pbank()
        tr(psb[b][:, 0:n], stg[0:n, i, :], ident[0:n, 0:n], [B_stg, B_const], [ps_buf[b]])
        cp("act", dst, psb[b][:, 0:n], [ps_buf[b]], [B_par])
    rowt = PH.alloc([1, 256], F32)
    B_row = Buf("row")
    S.dma("sp", rowt[0:1, 0:128], dn_norm_w[0:1, :], writes=[B_row])
    S.dma("sp", rowt[0:1, 128:160], dn_a_log[0:1, :], writes=[B_row])
    S.dma("sp", rowt[0:1, 160:192], dn_dt_bias[0:1, :], writes=[B_row])
    S.dma("sp", dnwcol[:, 0:1], dn_norm_w.rearrange("o p -> p o"), writes=[B_par])
    b = pbank()
    mm(psb[b][:, 0:192], onesf[0:1, :], rowt[0:1, 0:192], True, True, [B_row, B_const], [ps_buf[b]])
    cp("act", dnwbc[:], psb[b][:, 0:128], [ps_buf[b]], [B_par])
    act(hb[:, 0:32], psb[b][:, 128:160], AF.Exp, [ps_buf[b]], [B_par])
    ts("dve", hb[:, 0:32], hb[:, 0:32], -1.0, ALU.mult, [B_par], [B_par])
    cp("act", hb[:, 32:64], psb[b][:, 160:192], [ps_buf[b]], [B_par])

    ctm = PH.alloc([NS + 1, D], F32)
    csT = PH.alloc([128, NCH, NS + 1], BF16)
    B_ctm, B_csT = Buf("ctm"), Buf("csT")
    S.dma("sp", ctm[:], cin[:, :], writes=[B_ctm])
    act(ctm[:], ctm[:], AF.Silu, [B_ctm], [B_ctm])
    b = pbank()
    for c in range(NCH):
        tr(psb[b][:, c * 17:(c + 1) * 17], ctm[0:17, c * 128:(c + 1) * 128], ident[0:17, 0:17], [B_ctm, B_const], [ps_buf[b]])
    cp("act", csT[:].rearrange("p c n -> p (c n)"), psb[b][:, 0:NCH * 17], [ps_buf[b]], [B_csT])

    NPRO = 6
    PRO_BYTES = 16384
    pro_off = []
    for r in range(NPRO):
        PH.alloc([128, PRO_BYTES // 2], BF16)
        pro_off.append(PH.off - PRO_BYTES)
    pro_ring = (pro_off, [Buf("pring%d" % r) for r in range(NPRO)], [0], {}, NPRO)
    for l in range(2):
        wv = ada_w[l].rearrange("(k p) n -> p k n", p=128)
        modtm = [PH.alloc([NS + 1, 512], F32) for _ in range(2)]
        B_modtm = [Buf("modtm0"), Buf("modtm1")]
        for blk in range(24):
            w, wb = wload(wv[:, :, blk * 512:(blk + 1) * 512], [128, NCH, 512], alt=pro_ring)
            b = pbank()
            for k in range(NCH):
                mm(psb[b][0:NS + 1, 0:512], csT[:, k, :], w[:, k, :], k == 0, k == NCH - 1, [wb, B_csT], [ps_buf[b]])
            mt, bmt = modtm[blk % 2], B_modtm[blk % 2]
            cp("act", mt[:, :], psb[b][0:NS + 1, 0:512], [ps_buf[b]], [bmt])
            b2 = pbank()
            for jj in range(4):
                tr(psb[b2][:, jj * 17:(jj + 1) * 17], mt[:, jj * 128:(jj + 1) * 128], ident[0:NS + 1, 0:NS + 1], [bmt, B_const], [ps_buf[b2]])
            j0 = blk * 4
            for jj in range(4):
                ts("dve", MOD[:, l, j0 + jj, :], psb[b2][:, jj * 17:(jj + 1) * 17], adabT[:, l * 96 + j0 + jj:l * 96 + j0 + jj + 1],
                   ALU.add, [ps_buf[b2], B_par], [B_mod])
        for c in range(NCH):
            nw = lambda i: normwT[:, (l * 4 + i) * 16 + c:(l * 4 + i) * 16 + c + 1]
            ts("dve", MOD[:, l, 16 + c, :], MOD[:, l, 16 + c, :], 1.0, ALU.add, [B_mod, B_par], [B_mod], s2=nw(0), op1=ALU.mult)
            ts("dve", MOD[:, l, 32 + c, :], MOD[:, l, 32 + c, :], nw(1), ALU.mult, [B_mod, B_par], [B_mod])
            ts("dve", MOD[:, l, 64 + c, :], MOD[:, l, 64 + c, :], 1.0, ALU.add, [B_mod, B_par], [B_mod], s2=nw(2), op1=ALU.mult)
            ts("dve", MOD[:, l, 80 + c, :], MOD[:, l, 80 + c, :], nw(3), ALU.mult, [B_mod, B_par], [B_mod])

    if debug:
        mod_dbg = dscr("MOD_dbg", [128, 2 * 96 * (NS + 1)])
        S.dma("sp", mod_dbg[:, :], MOD[:].rearrange("p l j n -> p (l j n)"), reads=[B_mod])

    def modv(l, q, c):
        return MOD[:, l, q * 16 + c, NS:NS + 1]

    def mods(l, q, c):
        return MOD[:, l, q * 16 + c, 0:NS]

    def tiles_of(T):
        if T == TM:
            return [(0, 512), (512, 1024)]
        return [(0, 352), (352, 704), (704, T)]

    class Blk:
        pass

    def ss_open(T):
        nt = tiles_of(T)
        banks = [7 - i for i in range(len(nt))]
        ps_pool[0] = [i for i in range(8) if i not in banks]
        return banks

    def ss_add(banks, T, src, src_bufs, sq, B_sq, first, last):
        act(sq[:, 0:T], src, AF.Square, src_bufs, [B_sq])
        for (a, bnd), bk in zip(tiles_of(T), banks):
            mm(psb[bk][:, 0:bnd - a], onesb[:], sq[:, a:bnd], first, last, [B_sq, B_const], [ps_buf[bk]])

    def ss_close(banks, T, rstd, B_rstd, div):
        for (a, bnd), bk in zip(tiles_of(T), banks):
            act(rstd[:, a:bnd], psb[bk][:, 0:bnd - a], AF.Sqrt, [ps_buf[bk], B_const], [B_rstd], bias=eps, scale=1.0 / div)
        recip(rstd[:, 0:T], rstd[:, 0:T], [B_rstd], [B_rstd])
        ps_pool[0] = list(range(8))

    def make_h(l, qA, qB, xT, B_x, rstd, B_rstd, T, out, B_out, tmp, B_tmp):
        for c in range(NCH):
            stt("dve", tmp[:, 0:TM], xT[:, c, 0:TM], modv(l, qA, c), rstd[:, 0:TM], ALU.mult, ALU.mult,
                [B_x[c], B_rstd, B_mod], [B_tmp])
            act(out[:, c, 0:TM], tmp[:, 0:TM], AF.Identity, [B_tmp, B_mod], [B_out[c]], bias=modv(l, qB, c), scale=1.0)
            if T > TM:
                tt("dve", tmp[:, TM:T], xT[:, c, TM:T], rstd[:, TM:T], ALU.mult, [B_x[c], B_rstd], [B_tmp])
                tt("dve", tmp[:, TM:T], tmp[:, TM:T], mods(l, qA, c), ALU.mult, [B_tmp, B_mod], [B_tmp])
                tt("dve", out[:, c, TM:T], tmp[:, TM:T], mods(l, qB, c), ALU.add, [B_tmp, B_mod], [B_out[c]])

    def norm_of_x(xT, B_x, T, rstd, B_rstd, sq, B_sq):
        banks = ss_open(T)
        for c in range(NCH):
            ss_add(banks, T, xT[:, c, 0:T], [B_x[c]], sq, B_sq, c == 0, c == NCH - 1)
        ss_close(banks, T, rstd, B_rstd, float(D))

    def residual(l, qG, T, xT, B_x, rstd_y, B_ry, load_x):
        for c in range(NCH):
            ys = PHs["ystage"][c % 2]
            B_ys = PHs["B_ystage"][c % 2]
            S.dma("sp", ys[:, 0:T], Y_d[:, c, 0:T], writes=[B_ys])
            if load_x:
                S.dma("sp", xT[:, c, 0:T], X_d[:, c, 0:T], writes=[B_x[c]])
            stt("dve", ys[:, 0:TM], ys[:, 0:TM], modv(l, qG, c), rstd_y[:, 0:TM], ALU.mult, ALU.mult, [B_ys, B_ry, B_mod], [B_ys])
            if T > TM:
                tt("dve", ys[:, TM:T], ys[:, TM:T], rstd_y[:, TM:T], ALU.mult, [B_ys, B_ry], [B_ys])
                tt("dve", ys[:, TM:T], ys[:, TM:T], mods(l, qG, c), ALU.mult, [B_ys, B_mod], [B_ys])
            tt("dve", xT[:, c, 0:T], xT[:, c, 0:T], ys[:, 0:T], ALU.add, [B_x[c], B_ys], [B_x[c]])

    PHs = {}

    def out_stream(T, nk, wview_fn, wshape, srcT, B_src, rstd_out, B_ro):
        PHs_ystage = [PH.alloc([128, TMAX], F32) for _ in range(2)]
        B_ys = [Buf("ys0"), Buf("ys1")]
        sq = PH.alloc([128, TMAX], BF16)
        B_sq = Buf("sq")
        banks = ss_open(T)
        nt = tiles_of(T)
        nsplit = 2 if nk * 128 * 2 > RING_BYTES else 1
        kh = nk // nsplit
        for c in range(NCH):
            wparts = []
            for sp_ in range(nsplit):
                wparts.append(wload(wview_fn(c)[:, sp_ * kh:(sp_ + 1) * kh, :], [128, kh, 128]))
            ys, bys = PHs_ystage[c % 2], B_ys[c % 2]
            for (a, bnd) in nt:
                b = pbank()
                for k in range(nk):
                    w, wb = wparts[k // kh]
                    mm(psb[b][:, 0:bnd - a], w[:, k % kh, :], srcT[:, k, a:bnd], k == 0, k == nk - 1, [wb, B_src[k]], [ps_buf[b]])
                cp("act", ys[:, a:bnd], psb[b][:, 0:bnd - a], [ps_buf[b]], [bys])
            ss_add(banks, T, ys[:, 0:T], [bys], sq, B_sq, c == 0, c == NCH - 1)
            S.dma("sp", Y_d[:, c, 0:T], ys[:, 0:T], reads=[bys])
        ss_close(banks, T, rstd_out, B_ro, float(D))

    def run_block(bi):
        T = TM + NS if bi == 0 else TM
        row0 = 0 if bi == 0 else TM + NS
        nt = tiles_of(T)
        has_s = bi == 0
        last = bi == 1

        S.fence()
        PH.reset()
        xT = PH.alloc([128, NCH, TMAX], F32)
        B_x = [Buf("x%d" % c) for c in range(NCH)]
        rstd = PH.alloc([128, TMAX], F32)
        B_rstd = Buf("rstd")
        sq = PH.alloc([128, TMAX], BF16)
        B_sq = Buf("sq")
        tmp = PH.alloc([128, TMAX], F32)
        B_tmp = Buf("tmp")
        off_xst = PH.off
        xst = [PH.alloc([128, D], F32) for _ in range(2)]
        B_xst = [Buf("xst0"), Buf("xst1")]
        nrt = (T + 127) // 128
        for r in range(nrt):
            n = min(128, T - r * 128)
            st_, bst = xst[r % 2], B_xst[r % 2]
            S.dma("sp", st_[0:n, :], xin[row0 + r * 128:row0 + r * 128 + n, :], writes=[bst])
            for q4 in range(4):
                b = pbank()
                for jj in range(4):
                    c = q4 * 4 + jj
                    tr(psb[b][:, jj * 128:jj * 128 + n], st_[0:n, c * 128:(c + 1) * 128], ident[0:n, 0:n], [bst, B_const], [ps_buf[b]])
                for jj in range(4):
                    c = q4 * 4 + jj
                    cp("act" if jj % 2 == 0 else "dve", xT[:, c, r * 128:r * 128 + n], psb[b][:, jj * 128:jj * 128 + n], [ps_buf[b]], [B_x[c]])
        for c in range(NCH):
            S.dma("sp", X_d[:, c, 0:T], xT[:, c, 0:T], reads=[B_x[c]])
        if stop_after == "A1":
            return False
        norm_of_x(xT, B_x, T, rstd, B_rstd, sq, B_sq)
        if stop_after == "A2":
            return False
        make_h(0, 1, 0, xT, B_x, rstd, B_rstd, T, xT, B_x, tmp, B_tmp)
        h0, B_h0 = xT, B_x
        if stop_after == "A":
            return False
        S.fence()
        PH.off = off_xst

        if has_s:
            S.dma("sp", pool_s[:, 0:14, :], cpool_d[:, 1:15, :])
            hs_tm = PH.alloc([NS, D], F32)
            B_hs = Buf("hs")
            for q4 in range(4):
                b = pbank()
                for jj in range(4):
                    c = q4 * 4 + jj
                    tr(psb[b][0:NS, jj * 128:(jj + 1) * 128], h0[:, c, TM:T], ident, [B_h0[c], B_const], [ps_buf[b]])
                cp("act", hs_tm[:, q4 * 512:(q4 + 1) * 512], psb[b][0:NS, :], [ps_buf[b]], [B_hs])
            S.dma("sp", pool_s[:, 14, :], hs_tm[:], reads=[B_hs])
            cache = [PH.alloc([120, D], F32) for _ in range(2)]
            B_cache = Buf("cache")
            sel = PH.alloc([120, 2, 4, 16], F32)
            B_sel = Buf("sel")
            for k in range(2):
                S.dma("sp", cache[k][:], cpool_d[8 * k:8 * k + 8].rearrange("b r d -> (b r) d"), writes=[B_cache])
                S.dma("sp", sel[:, k, :, :], poolsel_d[k], writes=[B_sel])
        ext = [PH.alloc([128, 4, 15 + TM], F32) for _ in range(2)]
        B_ext = [Buf("ext0"), Buf("ext1")]
        dT = PH.alloc([128, 4, TMAX], BF16)
        B_dT = Buf("dT")
        ysg_one = PH.alloc([128, TMAX], F32)
        ysg = [ysg_one, ysg_one]
        B_ysg_one = Buf("ysg")
        B_ysg = [B_ysg_one, B_ysg_one]
        rstd_y = RSTDY
        banks = ss_open(T)
        for g in range(4):
            w_ = POOL_W[g]
            cs_ = slice(4 * g, 4 * g + 4)
            E0, E1 = ext
            cp("dve", E0[:, :, 0:15], poolc[:, cs_, :], [B_poolc], [B_ext[0]])
            cp("act", E0[:, :, 15:15 + TM], h0[:, cs_, 0:TM], [B_h0[4 * g + i] for i in range(4)], [B_ext[0]])
            cur, oth = 0, 1
            sh = 1
            while sh < w_:
                A_, Bt = ext[cur], ext[oth]
                tt("dve", Bt[:, :, sh:15 + TM], A_[:, :, sh:15 + TM], A_[:, :, 0:15 + TM - sh], ALU.add, [B_ext[cur]], [B_ext[oth]])
                cur, oth = oth, cur
                sh *= 2
            Sw = ext[cur]
            stt("dve", dT[:, :, 0:TM], Sw[:, :, 15:15 + TM], 1.0 / w_, h0[:, cs_, 0:TM], ALU.mult, ALU.subtract,
                [B_ext[cur]] + [B_h0[4 * g + i] for i in range(4)], [B_dT])
            if bi == 0:
                for i in range(4):
                    tt("dve", Sw[:, i, 15:30], Sw[:, i, 15:30], cst[:, C_FIX + g * 15:C_FIX + (g + 1) * 15], ALU.mult,
                       [B_ext[cur], B_const], [B_ext[cur]])
                tt("dve", dT[:, :, 0:15], Sw[:, :, 15:30], h0[:, cs_, 0:15], ALU.subtract,
                   [B_ext[cur]] + [B_h0[4 * g + i] for i in range(4)], [B_dT])
            if has_s:
                b = pbank()
                for i in range(4):
                    c = 4 * g + i
                    for k in range(2):
                        mm(psb[b][:, i * 16:(i + 1) * 16], cache[k][:, c * 128:(c + 1) * 128], sel[:, k, g, :], k == 0, k == 1,
                           [B_cache, B_sel], [ps_buf[b]])
                for i in range(4):
                    c = 4 * g + i
                    stt("dve", dT[:, i, TM:T], h0[:, c, TM:T], 1.0 / w_ - 1.0, psb[b][:, i * 16:(i + 1) * 16], ALU.mult, ALU.add,
                        [B_h0[c], ps_buf[b]], [B_dT])
            w, wb = wload(pool_w[0, g].rearrange("(k p) n -> p k n", p=128), [128, 4, 512])
            for i in range(4):
                c = 4 * g + i
                ys, bys = ysg[c % 2], B_ysg[c % 2]
                for (a, bnd) in nt:
                    b = pbank()
                    for k in range(4):
                        mm(psb[b][:, 0:bnd - a], w[:, k, i * 128:(i + 1) * 128], dT[:, k, a:bnd], k == 0, k == 3, [wb, B_dT], [ps_buf[b]])
                    act(ys[:, a:bnd], psb[b][:, 0:bnd - a], AF.Copy, [ps_buf[b], B_par], [bys], scale=pscT[:, c:c + 1])
                ss_add(banks, T, ys[:, 0:T], [bys], sq, B_sq, c == 0, c == NCH - 1)
                S.dma("sp", Y_d[:, c, 0:T], ys[:, 0:T], reads=[bys])
        ss_close(banks, T, rstd_y, B_ry, float(D))
        cp("dve", poolc[:], h0[:, :, TM - 15:TM], list(B_h0), [B_poolc])
        if last:
            pp = PH.alloc([16, D], F32)
            B_pp = Buf("pp")
            for q4 in range(4):
                b = pbank()
                for jj in range(4):
                    c = q4 * 4 + jj
                    tr(psb[b][0:16, jj * 128:(jj + 1) * 128], h0[:, c, TM - 16:TM], ident, [B_h0[c], B_const], [ps_buf[b]])
                cp("act", pp[:, q4 * 512:(q4 + 1) * 512], psb[b][0:16, :], [ps_buf[b]], [B_pp])
            S.dma("sp", pool_p[:, :], pp[:], reads=[B_pp])
        if stop_after == "pool":
            return False

        HT_BYTES = NCH * TMAX * 2

        def rn_phase(lG, qG, lH, qA, qB, final=False):
            S.fence()
            PH.reset()
            hT = PH.alloc([128, NCH, TMAX], BF16)
            B_h = [Buf("h%d" % c) for c in range(NCH)]
            xT = PH.alloc([128, NCH, TMAX], F32)
            B_x = [Buf("x%d" % c) for c in range(NCH)]
            PHs["ystage"] = [PH.alloc([128, TMAX], F32) for _ in range(2)]
            PHs["B_ystage"] = [Buf("ys0"), Buf("ys1")]
            residual(lG, qG, T, xT, B_x, RSTDY, B_ry, True)
            if final:
                ost = [PH.alloc([128, D], F32) for _ in range(2)]
                B_ost = [Buf("ost0"), Buf("ost1")]
                for r in range(nrt):
                    n = min(128, T - r * 128)
                    o_, bo = ost[r % 2], B_ost[r % 2]
                    for q4 in range(4):
                        b = pbank()
                        for jj in range(4):
                            c = q4 * 4 + jj
                            tr(psb[b][0:n, jj * 128:(jj + 1) * 128], xT[:, c, r * 128:r * 128 + n], ident, [B_x[c], B_const], [ps_buf[b]])
                        cp("act" if q4 % 2 == 0 else "dve", o_[0:n, q4 * 512:(q4 + 1) * 512], psb[b][0:n, :], [ps_buf[b]], [bo])
                    if r < 8:
                        S.dma("sp", y_p[bi * TM + r * 128:bi * TM + (r + 1) * 128, :], o_[:, :], reads=[bo])
                    else:
                        S.dma("sp", y_s[:, :], o_[0:NS, :], reads=[bo])
                return None, None
            for c in range(NCH):
                S.dma("sp", X_d[:, c, 0:T], xT[:, c, 0:T], reads=[B_x[c]])
            rstd = PH.alloc([128, TMAX], F32)
            B_rstd = Buf("rstd")
            sq = PH.alloc([128, TMAX], BF16)
            B_sq = Buf("sq")
            tmp = PH.alloc([128, TMAX], F32)
            B_tmp = Buf("tmp")
            norm_of_x(xT, B_x, T, rstd, B_rstd, sq, B_sq)
            make_h(lH, qA, qB, xT, B_x, rstd, B_rstd, T, hT, B_h, tmp, B_tmp)
            return hT, B_h

        def ffn_phase(l, hT, B_h):
            S.fence()
            PH.reset()
            PH.off += HT_BYTES
            aT = PH.alloc([128, NFF, TMAX], BF16)
            B_a = [Buf("a%d" % c) for c in range(NFF)]
            ge = PH.alloc([128, 2 + TMAX], F32)
            B_ge = Buf("ge")
            tm = PH.alloc([128, TMAX], F32)
            B_tm = Buf("tm")
            if has_s:
                gcT = PH.alloc([128, NFF, 2 * NS], F32)
                gsT = PH.alloc([128, NFF, NS], F32)
                B_gcT, B_gsT = Buf("gcT"), Buf("gsT")
                S.dma("sp", ffn_s[l, :, 0, :], cffn_d[l, :, 1, :])
                gst = PH.alloc([2 * NS, 1408], F32)
                B_gst = Buf("gst")
                for pc in range(4):
                    for r_ in range(2):
                        S.dma("sp", gst[r_ * NS:(r_ + 1) * NS, :], cffn_d[l, :, r_, pc * 1408:(pc + 1) * 1408], writes=[B_gst])
                    for q in range(3):
                        b = pbank()
                        nn = 4 if q < 2 else 3
                        for jj in range(nn):
                            tr(psb[b][:, jj * 32:(jj + 1) * 32], gst[:, (q * 4 + jj) * 128:(q * 4 + jj + 1) * 128], ident[0:32, 0:32],
                               [B_gst, B_const], [ps_buf[b]])
                        c0 = pc * 11 + q * 4
                        cp("act", gcT[:, c0:c0 + nn, :], psb[b][:, 0:nn * 32].rearrange("p (c n) -> p c n", n=32), [ps_buf[b]], [B_gcT])
            wg_v = ffn_w_gate[l].rearrange("(k p) n -> p k n", p=128)
            wu_v = ffn_w_up[l].rearrange("(k p) n -> p k n", p=128)
            fw = lambda k, ch: fcwT[:, (l * 3 + k) * NFF + ch:(l * 3 + k) * NFF + ch + 1]
            for blk in range(22):
                wg, wgb = wload(wg_v[:, :, blk * 256:(blk + 1) * 256], [128, NCH, 256])
                wu, wub = wload(wu_v[:, :, blk * 256:(blk + 1) * 256], [128, NCH, 256])
                for jj in range(2):
                    ch = blk * 2 + jj
                    for (a, bnd) in nt:
                        b = pbank()
                        for k in range(NCH):
                            mm(psb[b][:, 0:bnd - a], wg[:, k, jj * 128:(jj + 1) * 128], hT[:, k, a:bnd], k == 0, k == NCH - 1,
                               [wgb, B_h[k]], [ps_buf[b]])
                        cp("act", ge[:, 2 + a:2 + bnd], psb[b][:, 0:bnd - a], [ps_buf[b]], [B_ge])
                    cp("dve", ge[:, 0:2], gatec[:, l, ch, :], [B_gatec], [B_ge])
                    cp("dve", gatec[:, l, ch, :], ge[:, TM:TM + 2], [B_ge], [B_gatec])
                    ts("dve", tm[:, 0:TM], ge[:, 2:2 + TM], fw(2, ch), ALU.mult, [B_ge, B_par], [B_tm])
                    stt("dve", tm[:, 0:TM], ge[:, 1:1 + TM], fw(1, ch), tm[:, 0:TM], ALU.mult, ALU.add, [B_ge, B_par, B_tm], [B_tm])
                    stt("dve", tm[:, 0:TM], ge[:, 0:TM], fw(0, ch), tm[:, 0:TM], ALU.mult, ALU.add, [B_ge, B_par, B_tm], [B_tm])
                    if has_s:
                        cp("dve", gsT[:, ch, :], ge[:, 2 + TM:2 + T], [B_ge], [B_gsT])
                        ts("dve", tm[:, TM:T], ge[:, 2 + TM:2 + T], fw(2, ch), ALU.mult, [B_ge, B_par], [B_tm])
                        stt("dve", tm[:, TM:T], gcT[:, ch, NS:2 * NS], fw(1, ch), tm[:, TM:T], ALU.mult, ALU.add, [B_gcT, B_par, B_tm], [B_tm])
                        stt("dve", tm[:, TM:T], gcT[:, ch, 0:NS], fw(0, ch), tm[:, TM:T], ALU.mult, ALU.add, [B_gcT, B_par, B_tm], [B_tm])
                    act(tm[:, 0:T], tm[:, 0:T], AF.Silu, [B_tm], [B_tm])
                    for (a, bnd) in nt:
                        b = pbank()
                        for k in range(NCH):
                            mm(psb[b][:, 0:bnd - a], wu[:, k, jj * 128:(jj + 1) * 128], hT[:, k, a:bnd], k == 0, k == NCH - 1,
                               [wub, B_h[k]], [ps_buf[b]])
                        tt("dve", aT[:, ch, a:bnd], tm[:, a:bnd], psb[b][:, 0:bnd - a], ALU.mult, [B_tm, ps_buf[b]], [B_a[ch]])
            if has_s:
                orow, B_orow = gst[0:NS, :], B_gst
            else:
                orow = PH.alloc([NS, 1408], F32)
                B_orow = Buf("orow")
            for pc in range(4):
                if has_s:
                    for q in range(3):
                        b = pbank()
                        nn = 4 if q < 2 else 3
                        for jj in range(nn):
                            tr(psb[b][0:NS, jj * 128:(jj + 1) * 128], gsT[:, pc * 11 + q * 4 + jj, :], ident, [B_gsT, B_const], [ps_buf[b]])
                        cp("act", orow[:, q * 512:q * 512 + nn * 128], psb[b][0:NS, 0:nn * 128], [ps_buf[b]], [B_orow])
                    S.dma("sp", ffn_s[l, :, 1, pc * 1408:(pc + 1) * 1408], orow[:, :], reads=[B_orow])
                if last:
                    for q in range(3):
                        b = pbank()
                        nn = 4 if q < 2 else 3
                        for jj in range(nn):
                            tr(psb[b][0:2, jj * 128:(jj + 1) * 128], gatec[:, l, pc * 11 + q * 4 + jj, :], ident, [B_gatec, B_const], [ps_buf[b]])
                        cp("act", orow[0:2, q * 512:q * 512 + nn * 128], psb[b][0:2, 0:nn * 128], [ps_buf[b]], [B_orow])
                    S.dma("sp", ffn_p[l, :, pc * 1408:(pc + 1) * 1408], orow[0:2, :], reads=[B_orow])
            wd_v = ffn_w_down[l].rearrange("(k p) n -> p k n", p=128)
            S.fence()
            PH.off = PH.base
            out_stream(T, NFF, lambda c: wd_v[:, :, c * 128:(c + 1) * 128], [128, NFF, 128], aT, B_a, RSTDY, B_ry)

        def delta_phase(hT, B_h):
            S.fence()
            PH.reset()
            PH.off += HT_BYTES
            ntile = 9 if has_s else 8
            win = dn_w_in[0].rearrange("(k p) n -> p k n", p=128)
            names = ("BETA", "G", "GC", "EGC", "KTS", "NBE", "EGL")
            SC = {nm: PH.alloc([128, 9, NH], F32) for nm in names}
            B_sc = Buf("scal")
            Sf = PH.alloc([128, NH, 128], F32)
            Sb = PH.alloc([128, NH, 128], BF16)
            B_S = [Buf("S%d" % h) for h in range(NH)]
            B_Sb = [Buf("Sb%d" % h) for h in range(NH)]
            if bi == 0:
                memset("dve", Sf[:], 0.0, B_S)
                memset("dve", Sb[:], 0.0, B_Sb)
            else:
                S.dma("sp", Sf[:], SD_d[:, :, :], writes=B_S)
                cp("act", Sb[:], Sf[:], B_S, B_Sb)
            t64 = PH.alloc([128, 64], F32)
            B_t64 = Buf("t64")
            w, wb = wload(win[:, :, 12288:12352], [128, NCH, 64])
            for n in range(ntile):
                m = 128 if n < 8 else NS
                a0 = n * 128
                b = pbank()
                for k in range(NCH):
                    mm(psb[b][0:m, 0:64], hT[:, k, a0:a0 + m], w[:, k, :], k == 0, k == NCH - 1, [wb, B_h[k]], [ps_buf[b]])
                act(SC["BETA"][0:m, n, :], psb[b][0:m, 0:32], AF.Sigmoid, [ps_buf[b]], [B_sc])
                tt("dve", t64[0:m, 0:32], psb[b][0:m, 32:64], hb[0:m, 32:64], ALU.add, [ps_buf[b], B_par], [B_t64])
                act(t64[0:m, 0:32], t64[0:m, 0:32], AF.Exp, [B_t64], [B_t64])
                act(t64[0:m, 0:32], t64[0:m, 0:32], AF.Ln, [B_t64, B_const], [B_t64], bias=one[0:m, :], scale=1.0)
                tt("dve", SC["G"][0:m, n, :], t64[0:m, 0:32], hb[0:m, 0:32], ALU.mult, [B_t64, B_par], [B_sc])
            for n in range(8):
                b = pbank()
                mm(psb[b][:, 0:32], tri, SC["G"][:, n, :], True, True, [B_const, B_sc], [ps_buf[b]])
                mm(psb[b][:, 32:64], onesf[:], SC["G"][:, n, :], True, True, [B_const, B_sc], [ps_buf[b]])
                cp("act", SC["GC"][:, n, :], psb[b][:, 0:32], [ps_buf[b]], [B_sc])
                act(SC["EGC"][:, n, :], psb[b][:, 0:32], AF.Exp, [ps_buf[b]], [B_sc])
                act(SC["EGL"][:, n, :], psb[b][:, 32:64], AF.Exp, [ps_buf[b]], [B_sc])
                tt("dve", SC["KTS"][:, n, :], psb[b][:, 32:64], SC["GC"][:, n, :], ALU.subtract, [ps_buf[b], B_sc], [B_sc])
                act(SC["KTS"][:, n, :], SC["KTS"][:, n, :], AF.Exp, [B_sc], [B_sc])
                stt("dve", SC["NBE"][:, n, :], SC["BETA"][:, n, :], -1.0, SC["EGC"][:, n, :], ALU.mult, ALU.mult, [B_sc], [B_sc])
            if has_s:
                S.dma("sp", conv_s[:, 0:2, :], sconv_d[:, 1:3, :])
                act(SC["EGC"][0:NS, 8, :], SC["G"][0:NS, 8, :], AF.Exp, [B_sc], [B_sc])
                rhsb = PH.alloc([NS, NH, NS], F32)
                B_rhsb = Buf("rhsb")
                BETAbc = PH.alloc([128, NH, NS], F32)
                EGbc = PH.alloc([128, NH, NS], F32)
                B_bc = Buf("bc")
                for (src, dst) in ((SC["BETA"], BETAbc), (SC["EGC"], EGbc)):
                    for h in range(NH):
                        ts("dve", rhsb[:, h, :], ident[0:NS, 0:NS], src[0:NS, 8, h:h + 1], ALU.mult, [B_const, B_sc], [B_rhsb])
                    b = pbank()
                    mm(psb[b][:, 0:512], onesf[0:NS, :], rhsb[:].rearrange("p h b -> p (h b)"), True, True, [B_const, B_rhsb], [ps_buf[b]])
                    cp("act", dst[:].rearrange("p h b -> p (h b)"), psb[b][:, 0:512], [ps_buf[b]], [B_bc])
            ge = PH.alloc([128, 3 + TMAX], F32)
            cv = PH.alloc([128, TMAX], F32)
            B_ge, B_cv = Buf("ge"), Buf("cv")
            sq = PH.alloc([128, TMAX], BF16)
            rq = ge
            B_sq, B_rq = Buf("sq"), B_ge
            qTn = PH.alloc([128, TMAX], BF16)
            kTn = PH.alloc([128, TMAX], BF16)
            B_q, B_k = Buf("qTn"), Buf("kTn")
            vT = PH.alloc([128, 2, TMAX], BF16)
            vTs = PH.alloc([128, 2, NS], F32)
            B_v = [Buf("v0"), Buf("v1")]
            zs = PH.alloc([128, 9, 256], BF16)
            B_zs = Buf("zs")
            onTg = PH.alloc([128, 2, TMAX], BF16)
            B_on = [Buf("on0"), Buf("on1")]
            mk = lambda dt: PH.alloc([128, 128], dt)
            mk4 = lambda dt: PH.alloc([128, 4, 128], dt)
            KT = PH.alloc([128, 16, 128], BF16)
            PQ = PH.alloc([128, 16, 128], BF16)
            YS = PH.alloc([128, 16, 128], BF16)
            VB = PH.alloc([128, 16, 128], BF16)
            B_KTq = [Buf("KT%d" % q) for q in range(4)]
            B_PQq = [Buf("PQ%d" % q) for q in range(4)]
            B_YSq = [Buf("YS%d" % q) for q in range(4)]
            B_VBq = [Buf("VB%d" % q) for q in range(4)]
            QCH = [(mk4(F32), mk4(F32), [mk4(BF16), mk4(BF16)], [mk4(BF16), mk4(BF16)], mk4(BF16)) for _ in range(2)]
            B_QCH = [(Buf("INA"), Buf("INB"), [Buf("LP0"), Buf("LP1")], [Buf("UP0"), Buf("UP1")], Buf("YW")) for _ in range(2)]
            MA4, MB4, I4 = mk4(F32), mk4(F32), mk4(BF16)
            ntri = mk(F32)
            for j in range(4):
                cp("dve", MA4[:, j, :], maskA, [B_const], [B_const])
                cp("dve", MB4[:, j, :], maskB, [B_const], [B_const])
                cp("dve", I4[:, j, :], identb[:], [B_const], [B_const])
            ts("dve", ntri[:], tri, -1.0, ALU.mult, [B_const], [B_const])
            SCN = [(mk(BF16), mk(BF16), mk(F32), mk(F32), mk(BF16), PH.alloc([128, 2], F32)) for _ in range(2)]
            B_SCN = [(Buf("R"), Buf("vn"), Buf("t1"), Buf("om"), Buf("onm"), Buf("ssn")) for _ in range(2)]

            def interleave(gens):
                gens = list(gens)
                while gens:
                    for g_ in list(gens):
                        try:
                            next(g_)
                        except StopIteration:
                            gens.remove(g_)

            if has_s:
                sctm = PH.alloc([48, 128], F32)
                scT = PH.alloc([128, 48], F32)
                B_sctm, B_scT = Buf("sctm"), Buf("scT")
                rsm = PH.alloc([NS, 128], F32)
                B_rsm = Buf("rsm")
                kqs = PH.alloc([128, NS, 2], F32)
                B_kqs = Buf("kqs")
                qkbc = PH.alloc([128, NS], F32)
                zsT = PH.alloc([128, 2, NS], F32)
                B_qkbc, B_zsT = Buf("qkbc"), Buf("zsT")
                Ss = PH.alloc([128, 8, 128], F32)
                B_Ss = Buf("Ss")
                Sn, B_Sn = Ss, B_Ss
                sm = {nm: PH.alloc([128, 8], F32) for nm in ("t", "vn", "o", "o2", "sq", "rn")}
                B_sm = Buf("sm")
                vntm = PH.alloc([8, 128], F32)
                B_vntm = Buf("vntm")
                prod = PH.alloc([128, NS], F32)
            dw = lambda k, cq: dcwT[:, k * 64 + cq:k * 64 + cq + 1]
            for gq in range(16):
                for ci in range(4):
                    cq = (gq, 16 + gq, 32 + 2 * gq, 33 + 2 * gq)[ci]
                    wci, wcib = wload(win[:, :, cq * 128:(cq + 1) * 128], [128, NCH, 128])
                    for (a, bnd) in nt:
                        b = pbank()
                        for k in range(NCH):
                            mm(psb[b][:, 0:bnd - a], wci[:, k, :], hT[:, k, a:bnd], k == 0, k == NCH - 1, [wcib, B_h[k]], [ps_buf[b]])
                        cp("act", ge[:, 3 + a:3 + bnd], psb[b][:, 0:bnd - a], [ps_buf[b]], [B_ge])
                    cp("dve", ge[:, 0:3], convc[:, cq, :], [B_convc], [B_ge])
                    cp("dve", convc[:, cq, :], ge[:, TM:TM + 3], [B_ge], [B_convc])
                    ts("dve", cv[:, 0:TM], ge[:, 3:3 + TM], dw(3, cq), ALU.mult, [B_ge, B_par], [B_cv])
                    for k in (2, 1, 0):
                        stt("dve", cv[:, 0:TM], ge[:, k:k + TM], dw(k, cq), cv[:, 0:TM], ALU.mult, ALU.add, [B_ge, B_par, B_cv], [B_cv])
                    if has_s:
                        for r_ in range(3):
                            S.dma("sp", sctm[r_ * NS:(r_ + 1) * NS, :], sconv_d[:, r_, cq * 128:(cq + 1) * 128], writes=[B_sctm])
                        b = pbank()
                        tr(psb[b][:, 0:48], sctm[:, :], ident[0:48, 0:48], [B_sctm, B_const], [ps_buf[b]])
                        cp("act", scT[:, :], psb[b][:, 0:48], [ps_buf[b]], [B_scT])
                        ts("dve", cv[:, TM:T], ge[:, 3 + TM:3 + T], dw(3, cq), ALU.mult, [B_ge, B_par], [B_cv])
                        for k in (2, 1, 0):
                            stt("dve", cv[:, TM:T], scT[:, k * NS:(k + 1) * NS], dw(k, cq), cv[:, TM:T], ALU.mult, ALU.add,
                                [B_scT, B_par, B_cv], [B_cv])
                        b = pbank()
                        tr(psb[b][0:NS, 0:128], ge[:, 3 + TM:3 + T], ident, [B_ge, B_const], [ps_buf[b]])
                        cp("act", rsm[:, :], psb[b][0:NS, 0:128], [ps_buf[b]], [B_rsm])
                        S.dma("sp", conv_s[:, 2, cq * 128:(cq + 1) * 128], rsm[:, :], reads=[B_rsm])
                    if ci < 2:
                        act(cv[:, 0:T], cv[:, 0:T], AF.Silu, [B_cv], [B_cv])
                        act(sq[:, 0:T], cv[:, 0:T], AF.Square, [B_cv], [B_sq])
                        for (a, bnd) in nt:
                            b = pbank()
                            mm(psb[b][:, 0:bnd - a], onesb[:], sq[:, a:bnd], True, True, [B_sq, B_const], [ps_buf[b]])
                            act(rq[:, a:bnd], psb[b][:, 0:bnd - a], AF.Sqrt, [ps_buf[b], B_const], [B_rq], bias=eps, scale=1.0)
                        recip(rq[:, 0:T], rq[:, 0:T], [B_rq], [B_rq])
                        if ci == 0:
                            stt("dve", qTn[:, 0:T], cv[:, 0:T], 128.0 ** -0.5, rq[:, 0:T], ALU.mult, ALU.mult, [B_cv, B_rq], [B_q])
                            if has_s:
                                stt("dve", kqs[:, :, 1], cv[:, TM:T], 128.0 ** -0.5, rq[:, TM:T], ALU.mult, ALU.mult, [B_cv, B_rq], [B_kqs])
                        else:
                            tt("dve", kTn[:, 0:T], cv[:, 0:T], rq[:, 0:T], ALU.mult, [B_cv, B_rq], [B_k])
                            if has_s:
                                tt("dve", kqs[:, :, 0], cv[:, TM:T], rq[:, TM:T], ALU.mult, [B_cv, B_rq], [B_kqs])
                    else:
                        act(vT[:, ci - 2, 0:T], cv[:, 0:T], AF.Silu, [B_cv], [B_v[ci - 2]])
                        if has_s:
                            act(vTs[:, ci - 2, :], cv[:, TM:T], AF.Silu, [B_cv], [B_v[ci - 2]])
                wz, wzb = wload(win[:, :, 8192 + gq * 256:8192 + (gq + 1) * 256], [128, NCH, 256])
                for n in range(ntile):
                    m = 128 if n < 8 else NS
                    a0 = n * 128
                    b = pbank()
                    for k in range(NCH):
                        mm(psb[b][0:m, 0:256], hT[:, k, a0:a0 + m], wz[:, k, :], k == 0, k == NCH - 1, [wzb, B_h[k]], [ps_buf[b]])
                    act(zs[0:m, n, :], psb[b][0:m, 0:256], AF.Silu, [ps_buf[b]], [B_zs])
                def pre_chain(c):
                    bks = [4 * c + j for j in range(4)]
                    rr = [0]

                    def nb():
                        b_ = bks[rr[0] % 4]
                        rr[0] += 1
                        return b_

                    v4 = lambda b_: psb[b_][:, 0:512].rearrange("p (j n) -> p j n", j=4)
                    v4h = lambda b_: psb16[b_][:, 0:512].rearrange("p (j n) -> p j n", j=4)
                    INA, INB, LP, UP, YW = QCH[c]
                    B_INA, B_INB, B_LP, B_UP, B_YW = B_QCH[c]
                    for q in range(c, 4, 2):
                        prs = [(4 * q + j, 2 * q + j // 2, j % 2) for j in range(4)]
                        jb = lambda j: slice(j * 128, (j + 1) * 128)
                        ck = lambda n: slice(n * 128, (n + 1) * 128)
                        b0 = nb()
                        for jn in range(2):
                            tr(psb16[b0][:, jb(jn)], kTn[:, ck(2 * q + jn)], identb[:], [B_k, B_const], [ps_buf[b0]])
                        for j, (p, n, i) in enumerate(prs):
                            h = 2 * gq + i
                            act(KT[:, p, :], psb16[b0][:, jb(j // 2)], AF.Copy, [ps_buf[b0], B_sc], [B_KTq[q]], scale=SC["KTS"][:, n, h:h + 1])
                        bd = nb()
                        for j, (p, n, i) in enumerate(prs):
                            h = 2 * gq + i
                            gcol = SC["G"][:, n, h:h + 1].broadcast_to([128, 128])
                            mm(psb[bd][:, jb(j)], gcol, tri, True, False, [B_sc, B_const], [ps_buf[bd]])
                            mm(psb[bd][:, jb(j)], ntri[:], gcol, False, True, [B_sc, B_const], [ps_buf[bd]])
                        yield
                        tt("dve", INA[:], v4(bd), MA4[:], ALU.add, [ps_buf[bd], B_const], [B_INA])
                        tt("dve", INB[:], v4(bd), MB4[:], ALU.add, [ps_buf[bd], B_const], [B_INB])
                        bkk = nb()
                        for jn in range(2):
                            mm(psb[bkk][:, jb(jn)], kTn[:, ck(2 * q + jn)], kTn[:, ck(2 * q + jn)], True, True, [B_k], [ps_buf[bkk]])
                        bqk = nb()
                        for jn in range(2):
                            mm(psb[bqk][:, jb(jn)], kTn[:, ck(2 * q + jn)], qTn[:, ck(2 * q + jn)], True, True, [B_k, B_q], [ps_buf[bqk]])
                        yield
                        act(INA[:], INA[:], AF.Exp, [B_INA], [B_INA], scale=-1.0)
                        act(INB[:], INB[:], AF.Exp, [B_INB], [B_INB])
                        yield
                        for j, (p, n, i) in enumerate(prs):
                            h = 2 * gq + i
                            stt("dve", LP[0][:, j, :], psb[bkk][:, jb(j // 2)], SC["BETA"][:, n, h:h + 1], INA[:, j, :], ALU.mult, ALU.mult,
                                [ps_buf[bkk], B_sc, B_INA], [B_LP[0]])
                        for jn in range(2):
                            qk2 = psb[bqk][:, jb(jn)].rearrange("p (o n) -> p o n", o=1).broadcast_to([128, 2, 128])
                            tt("dve", PQ[:, 4 * q + 2 * jn:4 * q + 2 * jn + 2, :], qk2, INB[:, 2 * jn:2 * jn + 2, :], ALU.mult,
                               [ps_buf[bqk], B_INB], [B_PQq[q]])
                        yield
                        bu = nb()
                        for j in range(4):
                            tr(psb16[bu][:, jb(j)], LP[0][:, j, :], identb[:], [B_LP[0], B_const], [ps_buf[bu]])
                        yield
                        cp("act", UP[0][:], v4h(bu), [ps_buf[bu]], [B_UP[0]])
                        tt("dve", YW[:], I4[:], v4h(bu), ALU.subtract, [ps_buf[bu], B_const], [B_YW])
                        yield
                        cur = 0
                        for lev in range(6):
                            nx = 1 - cur
                            bP = nb()
                            for j in range(4):
                                mm(psb[bP][:, jb(j)], UP[cur][:, j, :], LP[cur][:, j, :], True, True, [B_UP[cur], B_LP[cur]], [ps_buf[bP]])
                            if lev < 5:
                                bU = nb()
                                for j in range(4):
                                    mm(psb[bU][:, jb(j)], LP[cur][:, j, :], UP[cur][:, j, :], True, True, [B_UP[cur], B_LP[cur]], [ps_buf[bU]])
                            yield
                            cp("act", LP[nx][:], v4(bP), [ps_buf[bP]], [B_LP[nx]])
                            if lev < 5:
                                cp("dve", UP[nx][:], v4(bU), [ps_buf[bU]], [B_UP[nx]])
                            yield
                            bY = nb()
                            for j in range(4):
                                mm(psb[bY][:, jb(j)], LP[nx][:, j, :], YW[:, j, :], True, True, [B_LP[nx], B_YW], [ps_buf[bY]])
                            yield
                            if lev < 5:
                                tt("dve", YW[:], YW[:], v4(bY), ALU.add, [B_YW, ps_buf[bY]], [B_YW])
                            else:
                                tt("dve", YS[:, 4 * q:4 * q + 4, :], YW[:], v4(bY), ALU.add, [B_YW, ps_buf[bY]], [B_YSq[q]])
                            cur = nx
                            yield
                        bv = nb()
                        for j, (p, n, i) in enumerate(prs):
                            tr(psb16[bv][:, jb(j)], vT[:, i, ck(n)], identb[:], [B_v[i], B_const], [ps_buf[bv]])
                        yield
                        for j, (p, n, i) in enumerate(prs):
                            h = 2 * gq + i
                            if j % 2 == 0:
                                act(VB[:, p, :], psb16[bv][:, jb(j)], AF.Copy, [ps_buf[bv], B_sc], [B_VBq[q]], scale=SC["BETA"][:, n, h:h + 1])
                            else:
                                ts("dve", VB[:, p, :], psb16[bv][:, jb(j)], SC["BETA"][:, n, h:h + 1], ALU.mult, [ps_buf[bv], B_sc], [B_VBq[q]])
                        yield

                def scan_chain(i):
                    bk = [4 * i + j for j in range(4)]
                    h = 2 * gq + i
                    Rm, vn, t1, om, onm, ssn = SCN[i]
                    B_R, B_vn, B_t1, B_om, B_onm, B_ssn = B_SCN[i]
                    for n in range(8):
                        p = 2 * n + i
                        c0, c1 = n * 128, (n + 1) * 128
                        sc = (lambda n: (lambda nm: SC[nm][:, n, h:h + 1]))(n)
                        mm(psb[bk[0]][:, 0:128], kTn[:, c0:c1], Sb[:, h, :], True, True, [B_k, B_Sb[h]], [ps_buf[bk[0]]])
                        mm(psb[bk[2]][:, 0:128], qTn[:, c0:c1], Sb[:, h, :], True, True, [B_q, B_Sb[h]], [ps_buf[bk[2]]])
                        yield
                        stt("dve", Rm[:], psb[bk[0]][:, 0:128], sc("NBE"), VB[:, p, :], ALU.mult, ALU.add, [ps_buf[bk[0]], B_sc, B_VBq[p // 4]], [B_R])
                        act(t1[:], psb[bk[2]][:, 0:128], AF.Copy, [ps_buf[bk[2]], B_sc], [B_t1], scale=sc("EGC"))
                        yield
                        mm(psb[bk[1]][:, 0:128], YS[:, p, :], Rm[:], True, True, [B_YSq[p // 4], B_R], [ps_buf[bk[1]]])
                        yield
                        cp("act", vn[:], psb[bk[1]][:, 0:128], [ps_buf[bk[1]]], [B_vn])
                        yield
                        mm(psb[bk[3]][:, 0:128], PQ[:, p, :], vn[:], True, True, [B_PQq[p // 4], B_vn], [ps_buf[bk[3]]])
                        mm(psb[bk[0]][:, 0:128], KT[:, p, :], vn[:], True, True, [B_KTq[p // 4], B_vn], [ps_buf[bk[0]]])
                        yield
                        tt("dve", om[:], t1[:], psb[bk[3]][:, 0:128], ALU.add, [B_t1, ps_buf[bk[3]]], [B_om])
                        stt("dve", Sf[:, h, :], Sf[:, h, :], sc("EGL"), psb[bk[0]][:, 0:128], ALU.mult, ALU.add, [B_S[h], B_sc, ps_buf[bk[0]]], [B_S[h]])
                        yield
                        cp("act", Sb[:, h, :], Sf[:, h, :], [B_S[h]], [B_Sb[h]])
                        act(t1[:], om[:], AF.Square, [B_om], [B_t1, B_ssn], accum=ssn[:, 0:1])
                        yield
                        act(ssn[:, 0:1], ssn[:, 0:1], AF.Sqrt, [B_ssn, B_const], [B_ssn], bias=eps, scale=1.0 / 128.0)
                        yield
                        recip(ssn[:, 0:1], ssn[:, 0:1], [B_ssn], [B_ssn])
                        yield
                        stt("dve", om[:], om[:], ssn[:, 0:1], dnwbc[:], ALU.mult, ALU.mult, [B_om, B_ssn, B_par], [B_om])
                        yield
                        tt("dve", onm[:], om[:], zs[:, n, i * 128:(i + 1) * 128], ALU.mult, [B_om, B_zs], [B_onm])
                        yield
                        tr(psb16[bk[1]][:, 0:128], onm[:], identb[:], [B_onm, B_const], [ps_buf[bk[1]]])
                        yield
                        cp("act", onTg[:, i, c0:c1], psb16[bk[1]][:, 0:128], [ps_buf[bk[1]]], [B_on[i]])
                        yield

                interleave([pre_chain(c) for c in range(2)])
                interleave([scan_chain(i) for i in range(2)])
                if has_s:
                    tt("dve", prod[:, :], kqs[:, :, 0], kqs[:, :, 1], ALU.mult, [B_kqs], [B_prod])
                    b = pbank()
                    mm(psb[b][:, 0:NS], onesf[:], prod[:, :], True, True, [B_prod, B_const], [ps_buf[b]])
                    cp("act", qkbc[:, :], psb[b][:, 0:NS], [ps_buf[b]], [B_qkbc])
                    for i in range(2):
                        b = pbank()
                        tr(psb16[b][:, 0:NS], zs[0:NS, 8, i * 128:(i + 1) * 128], identb[0:NS, 0:NS], [B_zs, B_const], [ps_buf[b]])
                        cp("act", zsT[:, i, :], psb16[b][:, 0:NS], [ps_buf[b]], [B_zsT])
                    for sub in range(4):
                        i, b0 = sub // 2, (sub % 2) * 8
                        h = 2 * gq + i
                        S.dma("sp", Ss[:, :, :], srec_d[b0:b0 + 8, h].rearrange("b k v -> k b v"), writes=[B_Ss])
                        bp = pbank()
                        for j in range(8):
                            mm(psb[bp][:, 2 * j:2 * j + 2], Ss[:, j, :], kqs[:, b0 + j, :], True, True, [B_Ss, B_kqs], [ps_buf[bp]])
                        KQ = psb[bp][:, 0:16].rearrange("p (j two) -> p j two", two=2)
                        eg = EGbc[:, h, b0:b0 + 8]
                        tt("dve", sm["t"][:, :], KQ[:, :, 0], eg, ALU.mult, [ps_buf[bp], B_bc], [B_sm])
                        tt("dve", sm["t"][:, :], vTs[:, i, b0:b0 + 8], sm["t"][:, :], ALU.subtract, [B_v[i], B_sm], [B_sm])
                        tt("dve", sm["vn"][:, :], sm["t"][:, :], BETAbc[:, h, b0:b0 + 8], ALU.mult, [B_sm, B_bc], [B_sm])
                        tt("dve", sm["o"][:, :], KQ[:, :, 1], eg, ALU.mult, [ps_buf[bp], B_bc], [B_sm])
                        tt("dve", sm["o2"][:, :], sm["vn"][:, :], qkbc[:, b0:b0 + 8], ALU.mult, [B_sm, B_qkbc], [B_sm])
                        tt("dve", sm["o"][:, :], sm["o"][:, :], sm["o2"][:, :], ALU.add, [B_sm], [B_sm])
                        tt("dve", sm["sq"][:, :], sm["o"][:, :], sm["o"][:, :], ALU.mult, [B_sm], [B_sm])
                        b = pbank()
                        mm(psb[b][:, 0:8], onesf[:], sm["sq"][:, :], True, True, [B_sm, B_const], [ps_buf[b]])
                        act(sm["rn"][:, :], psb[b][:, 0:8], AF.Sqrt, [ps_buf[b], B_const], [B_sm], bias=eps, scale=1.0 / 128.0)
                        recip(sm["rn"][:, :], sm["rn"][:, :], [B_sm], [B_sm])
                        stt("dve", sm["o"][:, :], sm["o"][:, :], dnwcol[:, 0:1], sm["rn"][:, :], ALU.mult, ALU.mult, [B_sm, B_par], [B_sm])
                        tt("dve", onTg[:, i, TM + b0:TM + b0 + 8], sm["o"][:, :], zsT[:, i, b0:b0 + 8], ALU.mult, [B_sm, B_zsT], [B_on[i]])
                        bt = pbank()
                        tr(psb[bt][0:8, 0:128], sm["vn"][:, :], ident, [B_sm, B_const], [ps_buf[bt]])
                        cp("act", vntm[:, :], psb[bt][0:8, 0:128], [ps_buf[bt]], [B_vntm])
                        for j in range(8):
                            bb = pbank()
                            mm(psb[bb][:, 0:128], ident[0:8, j:j + 1].broadcast_to([8, 128]), vntm[:, :], True, True, [B_vntm, B_const], [ps_buf[bb]])
                            act(Sn[:, j, :], Ss[:, j, :], AF.Copy, [B_Ss, B_bc], [B_Sn], scale=EGbc[:, h, b0 + j:b0 + j + 1])
                            stt("dve", Sn[:, j, :], psb[bb][:, 0:128], kqs[:, b0 + j, 0:1], Sn[:, j, :], ALU.mult, ALU.add,
                                [ps_buf[bb], B_kqs, B_Sn], [B_Sn])
                        S.dma("sp", rec_s[b0:b0 + 8, h].rearrange("b k v -> k b v"), Sn[:, :, :], reads=[B_Sn])
                for i in range(2):
                    S.dma("sp", ON_d[:, 2 * gq + i, 0:T], onTg[:, i, 0:T], reads=[B_on[i]])
            if last:
                S.dma("sp", rec_p.rearrange("h k v -> k h v"), Sf[:, :, :], reads=B_S)
                cpt = PH.alloc([3, 2048], F32)
                B_cpt = Buf("cpt")
                for q in range(4):
                    for q4 in range(4):
                        b = pbank()
                        for jj in range(4):
                            tr(psb[b][0:3, jj * 128:(jj + 1) * 128], convc[:, q * 16 + q4 * 4 + jj, :], ident, [B_convc, B_const], [ps_buf[b]])
                        cp("act", cpt[:, q4 * 512:(q4 + 1) * 512], psb[b][0:3, :], [ps_buf[b]], [B_cpt])
                    S.dma("sp", conv_p[:, q * 2048:(q + 1) * 2048], cpt[:, :], reads=[B_cpt])
            else:
                S.dma("sp", SD_d[:, :, :], Sf[:, :, :], reads=B_S)

        def outproj_phase():
            S.fence()
            PH.reset()
            onT = PH.alloc([128, NH, TMAX], BF16)
            B_onT = [Buf("onT%d" % h) for h in range(NH)]
            for h in range(NH):
                S.dma("sp", onT[:, h, 0:T], ON_d[:, h, 0:T], writes=[B_onT[h]])
            wo_v = dn_w_out[0].rearrange("(k p) n -> p k n", p=128)
            out_stream(T, NH, lambda c: wo_v[:, :, c * 128:(c + 1) * 128], [128, NH, 128], onT, B_onT, RSTDY, B_ry)

        PH_prod = None
        B_prod = Buf("prod")
        hT, B_h = rn_phase(0, 2, 0, 4, 3)
        ffn_phase(0, hT, B_h)
        if stop_after == "ffn0":
            return False
        hT, B_h = rn_phase(0, 5, 1, 1, 0)
        if stop_after == "rn2":
            return False
        if has_s:
            pass
        delta_phase(hT, B_h)
        outproj_phase()
        if stop_after == "oproj":
            return False
        hT, B_h = rn_phase(1, 2, 1, 4, 3)
        ffn_phase(1, hT, B_h)
        rn_phase(1, 5, None, None, None, final=True)
        return True

    if stop_after != "pro" and run_block(0):
        run_block(1)

    S.resolve()
    S.emit(nc)
    return nc, dbg_outs


_PROG = {}
W_NAMES = ("norm_w", "ada_w", "ada_b", "pool_w", "pool_scale", "dn_w_in", "dn_conv_w", "dn_a_log", "dn_dt_bias",
           "dn_norm_w", "dn_w_out", "ffn_w_gate", "ffn_w_up", "ffn_conv_w", "ffn_w_down")


def make_in_maps(inputs, ncores=8):
    f = lambda a: np.ascontiguousarray(np.asarray(a, dtype=np.float32))
    consts = host_consts()
    sel = host_poolsel()
    maps = []
    for c in range(ncores):
        b = c % 4
        sl = slice(NS * c, NS * (c + 1))
        xp = np.asarray(inputs["x_prompt"])[b]
        xs = np.asarray(inputs["x_sample"])[sl, 0, :]
        m = {
            "xin": f(np.concatenate([xp[0:TM], xs, xp[TM:2 * TM]], axis=0)),
            "cin": f(np.concatenate([np.asarray(inputs["c_sample"])[sl], np.asarray(inputs["c_prompt"])[b:b + 1]], axis=0)),
            "consts": consts,
            "poolsel": sel,
            "cache_pool_c": f(np.asarray(inputs["cache_pool"])[0, sl]),
            "state_conv_c": f(np.asarray(inputs["state_conv"])[0, sl]),
            "state_rec_c": f(np.asarray(inputs["state_rec"])[0, sl]),
            "cache_ffn_c": f(np.asarray(inputs["cache_ffn_conv"])[:, sl]),
        }
        for nm in W_NAMES:
            m[nm] = f(inputs[nm])
        maps.append(m)
    return maps


def kernel(**inputs):
    if "nc" not in _PROG:
        _PROG["nc"] = build_program()[0]
    nc = _PROG["nc"]
    maps = make_in_maps(inputs)
    res = run_bass_kernel_spmd(nc, maps, core_ids=list(range(8))).results
    y_prompt = np.stack([res[b]["y_p"] for b in range(4)], 0)
    y_sample = np.concatenate([res[c]["y_s"] for c in range(8)], 0)[:, None, :]
    pool_prompt = np.stack([res[b]["pool_p"][1:16] for b in range(4)], 0)[None]
    pool_sample = np.concatenate([res[c]["pool_s"] for c in range(8)], 0)[None]
    conv_prompt = np.stack([res[b]["conv_p"] for b in range(4)], 0)[None]
    conv_sample = np.concatenate([res[c]["conv_s"] for c in range(8)], 0)[None]
    rec_prompt = np.stack([res[b]["rec_p"] for b in range(4)], 0)[None]
    rec_sample = np.concatenate([res[c]["rec_s"] for c in range(8)], 0)[None]
    ffn_prompt = np.stack([res[b]["ffn_p"] for b in range(4)], 1)
    ffn_sample = np.concatenate([res[c]["ffn_s"] for c in range(8)], 1)
    outs = (y_prompt, y_sample, pool_prompt, pool_sample, conv_prompt, conv_sample, rec_prompt, rec_sample,
            ffn_prompt, ffn_sample)
    return tuple(np.ascontiguousarray(o, dtype=np.float32) for o in outs)
```

```python
import numpy as np
import concourse.bass as bass
import concourse.mybir as mybir
from concourse.bass_utils import run_bass_kernel_spmd

F32 = mybir.dt.float32
BF16 = mybir.dt.bfloat16
AF = mybir.ActivationFunctionType
ALU = mybir.AluOpType

D = 2048
NCH = 16
DFF = 5632
NFF = 44
NH = 32
EPS = 1e-6
TM = 1024
NS = 16
CONV_DIM = 8192
PROJ = 12352
BIG = 30000.0


class Buf:
    __slots__ = ("name", "excl")

    def __init__(self, name="", excl=False):
        self.name = name
        self.excl = excl


class Op:
    __slots__ = ("eng", "fn", "reads", "writes", "dma", "nofence", "deps", "signal", "count", "semidx", "waits")

    def __init__(self, eng, fn, reads, writes, dma, nofence):
        self.eng, self.fn, self.reads, self.writes, self.dma, self.nofence = eng, fn, reads, writes, dma, nofence
        self.deps = set()
        self.signal = False
        self.count = 0
        self.semidx = 0
        self.waits = []


ENGS = ("pe", "act", "dve", "pool", "sp")
NDSEM = 8


class Sched:
    def __init__(self):
        self.ops = []
        self.fences = []

    def op(self, eng, fn, reads=(), writes=(), nofence=False):
        reads, writes = list(reads), list(writes)
        for b in reads:
            if b.excl and b not in writes:
                writes.append(b)
        self.ops.append(Op(eng, fn, reads, writes, False, nofence))

    def dma(self, eng, out, in_, reads=(), writes=(), nofence=False):
        self.ops.append(Op(eng, (lambda e: e.dma_start(out=out, in_=in_)), list(reads), list(writes), True, nofence))

    def fence(self):
        self.fences.append(len(self.ops))

    def resolve(self):
        ops = self.ops
        last_w, readers = {}, {}
        last_comp = {}
        recent_dma = {e: [] for e in ENGS}
        pending = {e: None for e in ENGS}
        fset = set(self.fences)
        for i, op in enumerate(ops):
            if i in fset:
                F = set(last_comp.values())
                for e in ENGS:
                    F.update(recent_dma[e])
                for e in ENGS:
                    pending[e] = set(F) if pending[e] is None else (pending[e] | F)
            deps = set()
            for b in op.reads:
                if b in last_w:
                    deps.add(last_w[b])
            for b in op.writes:
                if b in last_w:
                    deps.add(last_w[b])
                deps.update(readers.get(b, ()))
            if not op.nofence and pending[op.eng] is not None:
                deps.update(pending[op.eng])
                pending[op.eng] = None
            deps.discard(i)
            op.deps = set(p for p in deps if not (ops[p].eng == op.eng == "pe"))
            for b in op.reads:
                readers.setdefault(b, []).append(i)
            for b in op.writes:
                last_w[b] = i
                readers[b] = []
            if op.dma:
                recent_dma[op.eng].append(i)
                if len(recent_dma[op.eng]) > NDSEM:
                    recent_dma[op.eng].pop(0)
            else:
                last_comp[op.eng] = i
        for op in ops:
            for p in op.deps:
                ops[p].signal = True
        cnt = {e: 0 for e in ENGS}
        ndma = {e: 0 for e in ENGS}
        for op in ops:
            if op.dma:
                j = ndma[op.eng]
                op.semidx = j % NDSEM
                op.count = 16 * (j // NDSEM + 1)
                ndma[op.eng] += 1
            elif op.signal:
                cnt[op.eng] += 1
                op.count = cnt[op.eng]
        known = {e: {} for e in ENGS}
        for op in ops:
            need = {}
            if op.dma and op.count > 16:
                need[(op.eng, "d", op.semidx)] = op.count - 16
            for p in op.deps:
                po = ops[p]
                key = (po.eng, "d", po.semidx) if po.dma else (po.eng, "c", 0)
                if need.get(key, 0) < po.count:
                    need[key] = po.count
            kn = known[op.eng]
            op.waits = []
            for key, val in need.items():
                if kn.get(key, 0) < val:
                    kn[key] = val
                    op.waits.append((key, val))
        self.final_dma = {e: ndma[e] for e in ENGS}

    def emit(self, nc):
        import contextlib
        with contextlib.ExitStack() as st:
            csem = {e: st.enter_context(nc.semaphore("c_" + e)) for e in ("pe", "act", "dve", "pool")}
            dsem = {e: [st.enter_context(nc.semaphore("d_%s%d" % (e, i))) for i in range(NDSEM)] for e in ("pool", "sp")}
            block = st.enter_context(nc.Block())
            ops = self.ops
            final_dma = self.final_dma

            def semof(key):
                return dsem[key[0]][key[2]] if key[1] == "d" else csem[key[0]]

            def run(ename, e):
                for op in ops:
                    if op.eng != ename:
                        continue
                    for key, val in op.waits:
                        e.wait_ge(semof(key), val)
                    ins = op.fn(e)
                    if op.dma:
                        ins.then_inc(dsem[ename][op.semidx], 16)
                    elif op.signal:
                        ins.then_inc(csem[ename], 1)
                if ename == "sp":
                    for q in ("pool", "sp"):
                        n = final_dma[q]
                        for r in range(NDSEM):
                            k = (n - r + NDSEM - 1) // NDSEM if n > r else 0
                            if k > 0:
                                e.wait_ge(dsem[q][r], 16 * k)

            @block.tensor
            def _(e):
                run("pe", e)

            @block.scalar
            def _(e):
                run("act", e)

            @block.vector
            def _(e):
                run("dve", e)

            @block.gpsimd
            def _(e):
                run("pool", e)

            @block.sync
            def _(e):
                run("sp", e)


POOL_W = (2, 4, 8, 16)
C_ID, C_TRI, C_MA, C_MB, C_FIX, C_END = 0, 128, 256, 384, 512, 576


def host_consts():
    c = np.zeros((128, C_END), np.float32)
    i = np.arange(128)
    c[:, C_ID:C_ID + 128] = np.eye(128, dtype=np.float32)
    c[:, C_TRI:C_TRI + 128] = (i[:, None] <= i[None, :]).astype(np.float32)
    c[:, C_MA:C_MA + 128] = np.where(i[None, :] >= i[:, None], BIG, 0.0)
    c[:, C_MB:C_MB + 128] = np.where(i[None, :] < i[:, None], -BIG, 0.0)
    for g, w in enumerate(POOL_W):
        t = np.arange(15)
        c[:, C_FIX + g * 15:C_FIX + (g + 1) * 15] = (1.0 / np.minimum(t + 1, w))[None, :]
    return c


def host_poolsel():
    s = np.zeros((2, 120, 4, 16), np.float32)
    for k in range(2):
        for bl in range(8):
            for r in range(15):
                for g, w in enumerate(POOL_W):
                    if r >= 15 - (w - 1):
                        s[k, bl * 15 + r, g, 8 * k + bl] = 1.0 / w
    return s


def build_program(debug=False, stop_after=None):
    nc = bass.Bass("TRN2", target_bir_lowering=False)
    S = Sched()
    dbg_outs = []

    def din(name, shape):
        return nc.dram_tensor(name, list(shape), F32, kind="ExternalInput").ap()

    def dout(name, shape):
        return nc.dram_tensor(name, list(shape), F32, kind="ExternalOutput").ap()

    def dscr(name, shape, dt=F32):
        if debug:
            dbg_outs.append(name)
            return nc.dram_tensor(name, list(shape), dt, kind="ExternalOutput").ap()
        return nc.dram_tensor(name, list(shape), dt).ap()

    xin = din("xin", [2 * TM + NS, D])
    cin = din("cin", [NS + 1, D])
    consts_d = din("consts", [128, C_END])
    poolsel_d = din("poolsel", [2, 120, 4, 16])
    cpool_d = din("cache_pool_c", [NS, 15, D])
    sconv_d = din("state_conv_c", [NS, 3, CONV_DIM])
    srec_d = din("state_rec_c", [NS, NH, 128, 128])
    cffn_d = din("cache_ffn_c", [2, NS, 2, DFF])
    norm_w = din("norm_w", [2, 4, D])
    ada_w = din("ada_w", [2, D, 6 * D])
    ada_b = din("ada_b", [2, 6 * D])
    pool_w = din("pool_w", [1, 4, 512, 512])
    pool_scale = din("pool_scale", [1, D])
    dn_w_in = din("dn_w_in", [1, D, PROJ])
    dn_conv_w = din("dn_conv_w", [1, 4, CONV_DIM])
    dn_a_log = din("dn_a_log", [1, NH])
    dn_dt_bias = din("dn_dt_bias", [1, NH])
    dn_norm_w = din("dn_norm_w", [1, 128])
    dn_w_out = din("dn_w_out", [1, 4096, D])
    ffn_w_gate = din("ffn_w_gate", [2, D, DFF])
    ffn_w_up = din("ffn_w_up", [2, D, DFF])
    ffn_conv_w = din("ffn_conv_w", [2, 3, DFF])
    ffn_w_down = din("ffn_w_down", [2, DFF, D])

    y_p = dout("y_p", [2 * TM, D])
    y_s = dout("y_s", [NS, D])
    pool_p = dout("pool_p", [16, D])
    pool_s = dout("pool_s", [NS, 15, D])
    conv_p = dout("conv_p", [3, CONV_DIM])
    conv_s = dout("conv_s", [NS, 3, CONV_DIM])
    rec_p = dout("rec_p", [NH, 128, 128])
    rec_s = dout("rec_s", [NS, NH, 128, 128])
    ffn_p = dout("ffn_p", [2, 2, DFF])
    ffn_s = dout("ffn_s", [2, NS, 2, DFF])

    TMAX = TM + NS
    X_d = dscr("X_scr", [128, NCH, TMAX])
    Y_d = dscr("Y_scr", [128, NCH, TMAX])
    ON_d = dscr("ON_scr", [128, NH, TMAX], BF16)
    SD_d = dscr("SD_scr", [128, NH, 128])

    class Arena:
        def __init__(self, base, limit):
            self.base, self.limit, self.off, self.n = base, limit, base, 0

        def reset(self):
            self.off = self.base

        def alloc(self, shape, dt):
            nbytes = int(np.prod(shape[1:])) * (4 if dt == F32 else 2)
            nbytes = (nbytes + 63) // 64 * 64
            off = self.off
            assert off + nbytes <= self.limit, ("SBUF arena overflow", shape, off, nbytes, self.limit)
            self.off += nbytes
            self.n += 1
            return nc.alloc_sbuf_tensor_at("t%d" % self.n, list(shape), dt, offset=off)

    PERS = Arena(16512, 16512 + 60 * 1024)
    PH = Arena(16512 + 60 * 1024, 229376)

    cst = PERS.alloc([128, C_END], F32)
    ident = cst[:, C_ID:C_ID + 128]
    tri = cst[:, C_TRI:C_TRI + 128]
    maskA = cst[:, C_MA:C_MA + 128]
    maskB = cst[:, C_MB:C_MB + 128]
    identb = PERS.alloc([128, 128], BF16)
    onesb = PERS.alloc([128, 128], BF16)
    onesf = PERS.alloc([128, 128], F32)
    epsT = PERS.alloc([128, 2], F32)
    normwT = PERS.alloc([128, 128], F32)
    pscT = PERS.alloc([128, 16], F32)
    adabT = PERS.alloc([128, 192], F32)
    fcwT = PERS.alloc([128, 264], F32)
    dcwT = PERS.alloc([128, 256], F32)
    dnwbc = PERS.alloc([128, 128], F32)
    hb = PERS.alloc([128, 64], F32)
    MOD = PERS.alloc([128, 2, 96, NS + 1], F32)
    poolc = PERS.alloc([128, NCH, 15], F32)
    gatec = PERS.alloc([128, 2, NFF, 2], F32)
    convc = PERS.alloc([128, 64, 3], F32)
    RSTDY = PERS.alloc([128, TM + NS], F32)
    dnwcol = PERS.alloc([128, 2], F32)
    RING_BYTES = 8192
    NRING = 4
    ring_off = []
    for r in range(NRING):
        t = PERS.alloc([128, RING_BYTES // 2], BF16)
        ring_off.append(PERS.off - RING_BYTES)
    ring_buf = [Buf("ring%d" % r) for r in range(NRING)]
    ring_views = {}
    ring_next = [0]
    B_const = Buf("const")
    B_par = Buf("par")
    B_mod = Buf("mod")
    B_poolc, B_gatec, B_convc = Buf("poolc"), Buf("gatec"), Buf("convc")
    B_ry = Buf("rstdy")

    def wload(dram_view, shape, alt=None):
        if alt is None:
            offs, bufs, nxt, views, n = ring_off, ring_buf, ring_next, ring_views, NRING
        else:
            offs, bufs, nxt, views, n = alt
        r = nxt[0] % n
        nxt[0] += 1
        key = (r, tuple(shape))
        if key not in views:
            views[key] = nc.alloc_sbuf_tensor_at("rv%d_%d_%d" % (r, len(views), offs[r]), list(shape), BF16, offset=offs[r])
        v = views[key]
        S.dma("pool", v[:], dram_view, writes=[bufs[r]], nofence=True)
        return v, bufs[r]

    psb = [nc.alloc_psum_tensor("ps%d" % i, [128, 512], F32) for i in range(8)]
    psb16 = [p.bitcast(BF16) for p in psb]
    ps_buf = [Buf("ps%d" % i, excl=True) for i in range(8)]
    ps_rr = [0]
    ps_pool = [list(range(8))]

    def pbank():
        lst = ps_pool[0]
        i = lst[ps_rr[0] % len(lst)]
        ps_rr[0] += 1
        return i

    def mm(out, lhsT, rhs, start, stop, reads, writes):
        S.op("pe", lambda e: e.matmul(out, lhsT=lhsT, rhs=rhs, start=start, stop=stop), reads, writes)

    def tr(out, in_, idn, reads, writes):
        S.op("pe", lambda e: e.transpose(out=out, in_=in_, identity=idn), reads, writes)

    def act(out, in_, func, reads, writes, bias=None, scale=None, accum=None):
        kw = {}
        if bias is not None:
            kw["bias"] = bias
        if scale is not None:
            kw["scale"] = scale
        if accum is not None:
            kw["accum_out"] = accum
        S.op("act", lambda e: e.activation(out=out, in_=in_, func=func, **kw), reads, writes)

    def tt(eng, out, in0, in1, op, reads, writes):
        S.op(eng, lambda e: e.tensor_tensor(out=out, in0=in0, in1=in1, op=op), reads, writes)

    def ts(eng, out, in0, s1, op0, reads, writes, s2=None, op1=None):
        if op1 is None:
            S.op(eng, lambda e: e.tensor_scalar(out=out, in0=in0, scalar1=s1, scalar2=None, op0=op0), reads, writes)
        else:
            S.op(eng, lambda e: e.tensor_scalar(out=out, in0=in0, scalar1=s1, scalar2=s2, op0=op0, op1=op1), reads, writes)

    def stt(eng, out, in0, scalar, in1, op0, op1, reads, writes):
        S.op(eng, lambda e: e.scalar_tensor_tensor(out=out, in0=in0, scalar=scalar, in1=in1, op0=op0, op1=op1), reads, writes)

    def cp(eng, out, in_, reads, writes):
        if eng == "act":
            act(out, in_, AF.Copy, reads, writes)
        else:
            S.op(eng, lambda e: e.tensor_copy(out=out, in_=in_), reads, writes)

    def recip(out, in_, reads, writes):
        S.op("dve", lambda e: e.reciprocal(out=out, in_=in_), reads, writes)

    def memset(eng, ap, val, writes):
        S.op(eng, lambda e: e.memset(ap, val), [], writes)

    S.dma("sp", cst[:], consts_d[:, :], writes=[B_const])
    memset("dve", onesf[:], 1.0, [B_const])
    memset("dve", onesb[:], 1.0, [B_const])
    memset("dve", epsT[:, 0:1], EPS, [B_const])
    memset("dve", epsT[:, 1:2], 1.0, [B_const])
    cp("dve", identb[:], ident, [B_const], [B_const])
    memset("dve", poolc[:], 0.0, [B_poolc])
    memset("dve", gatec[:], 0.0, [B_gatec])
    memset("dve", convc[:], 0.0, [B_convc])
    eps = epsT[:, 0:1]
    one = epsT[:, 1:2]

    PH.reset()
    stg = PH.alloc([128, 9, 128], F32)
    B_stg = Buf("stg")
    prm = [
        (norm_w.rearrange("l i (c p) -> (l i c) p", p=128), 128, normwT[:, 0:128]),
        (pool_scale.rearrange("o (c p) -> (o c) p", p=128), 16, pscT[:, 0:16]),
        (ada_b[0].rearrange("(c p) -> c p", p=128), 96, adabT[:, 0:96]),
        (ada_b[1].rearrange("(c p) -> c p", p=128), 96, adabT[:, 96:192]),
        (ffn_conv_w.rearrange("l k (c p) -> (l k c) p", p=128)[0:88], 88, fcwT[:, 0:88]),
        (ffn_conv_w.rearrange("l k (c p) -> (l k c) p", p=128)[88:176], 88, fcwT[:, 88:176]),
        (ffn_conv_w.rearrange("l k (c p) -> (l k c) p", p=128)[176:264], 88, fcwT[:, 176:264]),
        (dn_conv_w.rearrange("o k (c p) -> (o k c) p", p=128)[0:128], 128, dcwT[:, 0:128]),
        (dn_conv_w.rearrange("o k (c p) -> (o k c) p", p=128)[128:256], 128, dcwT[:, 128:256]),
    ]
    for i, (src, n, dst) in enumerate(prm):
        S.dma("sp", stg[0:n, i, :], src, writes=[B_stg])
    for i, (src, n, dst) in enumerate(prm):
        b = pbank()
        tr(psb[b][:, 0:n], stg[0:n, i, :], ident[0:n, 0:n], [B_stg, B_const], [ps_buf[b]])
        cp("act", dst, psb[b][:, 0:n], [ps_buf[b]], [B_par])
    rowt = PH.alloc([1, 256], F32)
    B_row = Buf("row")
    S.dma("sp", rowt[0:1, 0:128], dn_norm_w[0:1, :], writes=[B_row])
    S.dma("sp", rowt[0:1, 128:160], dn_a_log[0:1, :], writes=[B_row])
    S.dma("sp", rowt[0:1, 160:192], dn_dt_bias[0:1, :], writes=[B_row])
    S.dma("sp", dnwcol[:, 0:1], dn_norm_w.rearrange("o p -> p o"), writes=[B_par])
    b = pbank()
    mm(psb[b][:, 0:192], onesf[0:1, :], rowt[0:1, 0:192], True, True, [B_row, B_const], [ps_buf[b]])
    cp("act", dnwbc[:], psb[b][:, 0:128], [ps_buf[b]], [B_par])
    act(hb[:, 0:32], psb[b][:, 128:160], AF.Exp, [ps_buf[b]], [B_par])
    ts("dve", hb[:, 0:32], hb[:, 0:32], -1.0, ALU.mult, [B_par], [B_par])
    cp("act", hb[:, 32:64], psb[b][:, 160:192], [ps_buf[b]], [B_par])

    ctm = PH.alloc([NS + 1, D], F32)
    csT = PH.alloc([128, NCH, NS + 1], BF16)
    B_ctm, B_csT = Buf("ctm"), Buf("csT")
    S.dma("sp", ctm[:], cin[:, :], writes=[B_ctm])
    act(ctm[:], ctm[:], AF.Silu, [B_ctm], [B_ctm])
    b = pbank()
    for c in range(NCH):
        tr(psb[b][:, c * 17:(c + 1) * 17], ctm[0:17, c * 128:(c + 1) * 128], ident[0:17, 0:17], [B_ctm, B_const], [ps_buf[b]])
    cp("act", csT[:].rearrange("p c n -> p (c n)"), psb[b][:, 0:NCH * 17], [ps_buf[b]], [B_csT])

    NPRO = 7
    PRO_BYTES = 16384
    pro_off = []
    for r in range(NPRO):
        PH.alloc([128, PRO_BYTES // 2], BF16)
        pro_off.append(PH.off - PRO_BYTES)
    pro_ring = (pro_off, [Buf("pring%d" % r) for r in range(NPRO)], [0], {}, NPRO)
    for l in range(2):
        wv = ada_w[l].rearrange("(k p) n -> p k n", p=128)
        modtm = [PH.alloc([NS + 1, 512], F32) for _ in range(2)]
        B_modtm = [Buf("modtm0"), Buf("modtm1")]
        for blk in range(24):
            w, wb = wload(wv[:, :, blk * 512:(blk + 1) * 512], [128, NCH, 512], alt=pro_ring)
            b = pbank()
            for k in range(NCH):
                mm(psb[b][0:NS + 1, 0:512], csT[:, k, :], w[:, k, :], k == 0, k == NCH - 1, [wb, B_csT], [ps_buf[b]])
            mt, bmt = modtm[blk % 2], B_modtm[blk % 2]
            cp("act", mt[:, :], psb[b][0:NS + 1, 0:512], [ps_buf[b]], [bmt])
            b2 = pbank()
            for jj in range(4):
                tr(psb[b2][:, jj * 17:(jj + 1) * 17], mt[:, jj * 128:(jj + 1) * 128], ident[0:NS + 1, 0:NS + 1], [bmt, B_const], [ps_buf[b2]])
            j0 = blk * 4
            for jj in range(4):
                ts("dve", MOD[:, l, j0 + jj, :], psb[b2][:, jj * 17:(jj + 1) * 17], adabT[:, l * 96 + j0 + jj:l * 96 + j0 + jj + 1],
                   ALU.add, [ps_buf[b2], B_par], [B_mod])
        for c in range(NCH):
            nw = lambda i: normwT[:, (l * 4 + i) * 16 + c:(l * 4 + i) * 16 + c + 1]
            ts("dve", MOD[:, l, 16 + c, :], MOD[:, l, 16 + c, :], 1.0, ALU.add, [B_mod, B_par], [B_mod], s2=nw(0), op1=ALU.mult)
            ts("dve", MOD[:, l, 32 + c, :], MOD[:, l, 32 + c, :], nw(1), ALU.mult, [B_mod, B_par], [B_mod])
            ts("dve", MOD[:, l, 64 + c, :], MOD[:, l, 64 + c, :], 1.0, ALU.add, [B_mod, B_par], [B_mod], s2=nw(2), op1=ALU.mult)
            ts("dve", MOD[:, l, 80 + c, :], MOD[:, l, 80 + c, :], nw(3), ALU.mult, [B_mod, B_par], [B_mod])

    if debug:
        mod_dbg = dscr("MOD_dbg", [128, 2 * 96 * (NS + 1)])
        S.dma("sp", mod_dbg[:, :], MOD[:].rearrange("p l j n -> p (l j n)"), reads=[B_mod])

    def modv(l, q, c):
        return MOD[:, l, q * 16 + c, NS:NS + 1]

    def mods(l, q, c):
        return MOD[:, l, q * 16 + c, 0:NS]

    def tiles_of(T):
        if T == TM:
            return [(0, 512), (512, 1024)]
        return [(0, 352), (352, 704), (704, T)]

    class Blk:
        pass

    def ss_open(T):
        nt = tiles_of(T)
        banks = [7 - i for i in range(len(nt))]
        ps_pool[0] = [i for i in range(8) if i not in banks]
        return banks

    def ss_add(banks, T, src, src_bufs, sq, B_sq, first, last):
        act(sq[:, 0:T], src, AF.Square, src_bufs, [B_sq])
        for (a, bnd), bk in zip(tiles_of(T), banks):
            mm(psb[bk][:, 0:bnd - a], onesb[:], sq[:, a:bnd], first, last, [B_sq, B_const], [ps_buf[bk]])

    def ss_close(banks, T, rstd, B_rstd, div):
        for (a, bnd), bk in zip(tiles_of(T), banks):
            act(rstd[:, a:bnd], psb[bk][:, 0:bnd - a], AF.Sqrt, [ps_buf[bk], B_const], [B_rstd], bias=eps, scale=1.0 / div)
        recip(rstd[:, 0:T], rstd[:, 0:T], [B_rstd], [B_rstd])
        ps_pool[0] = list(range(8))

    def make_h(l, qA, qB, xT, B_x, rstd, B_rstd, T, out, B_out, tmp, B_tmp):
        for c in range(NCH):
            stt("dve", tmp[:, 0:TM], xT[:, c, 0:TM], modv(l, qA, c), rstd[:, 0:TM], ALU.mult, ALU.mult,
                [B_x[c], B_rstd, B_mod], [B_tmp])
            act(out[:, c, 0:TM], tmp[:, 0:TM], AF.Identity, [B_tmp, B_mod], [B_out[c]], bias=modv(l, qB, c), scale=1.0)
            if T > TM:
                tt("dve", tmp[:, TM:T], xT[:, c, TM:T], rstd[:, TM:T], ALU.mult, [B_x[c], B_rstd], [B_tmp])
                tt("dve", tmp[:, TM:T], tmp[:, TM:T], mods(l, qA, c), ALU.mult, [B_tmp, B_mod], [B_tmp])
                tt("dve", out[:, c, TM:T], tmp[:, TM:T], mods(l, qB, c), ALU.add, [B_tmp, B_mod], [B_out[c]])

    def norm_of_x(xT, B_x, T, rstd, B_rstd, sq, B_sq):
        banks = ss_open(T)
        for c in range(NCH):
            ss_add(banks, T, xT[:, c, 0:T], [B_x[c]], sq, B_sq, c == 0, c == NCH - 1)
        ss_close(banks, T, rstd, B_rstd, float(D))

    def residual(l, qG, T, xT, B_x, rstd_y, B_ry, load_x):
        for c in range(NCH):
            ys = PHs["ystage"][c % 2]
            B_ys = PHs["B_ystage"][c % 2]
            S.dma("sp", ys[:, 0:T], Y_d[:, c, 0:T], writes=[B_ys])
            if load_x:
                S.dma("sp", xT[:, c, 0:T], X_d[:, c, 0:T], writes=[B_x[c]])
            stt("dve", ys[:, 0:TM], ys[:, 0:TM], modv(l, qG, c), rstd_y[:, 0:TM], ALU.mult, ALU.mult, [B_ys, B_ry, B_mod], [B_ys])
            if T > TM:
                tt("dve", ys[:, TM:T], ys[:, TM:T], rstd_y[:, TM:T], ALU.mult, [B_ys, B_ry], [B_ys])
                tt("dve", ys[:, TM:T], ys[:, TM:T], mods(l, qG, c), ALU.mult, [B_ys, B_mod], [B_ys])
            tt("dve", xT[:, c, 0:T], xT[:, c, 0:T], ys[:, 0:T], ALU.add, [B_x[c], B_ys], [B_x[c]])

    PHs = {}

    def out_stream(T, nk, wview_fn, wshape, srcT, B_src, rstd_out, B_ro):
        PHs_ystage = [PH.alloc([128, TMAX], F32) for _ in range(2)]
        B_ys = [Buf("ys0"), Buf("ys1")]
        sq = PH.alloc([128, TMAX], BF16)
        B_sq = Buf("sq")
        banks = ss_open(T)
        nt = tiles_of(T)
        nsplit = 2 if nk * 128 * 2 > RING_BYTES else 1
        kh = nk // nsplit
        for c in range(NCH):
            wparts = []
            for sp_ in range(nsplit):
                wparts.append(wload(wview_fn(c)[:, sp_ * kh:(sp_ + 1) * kh, :], [128, kh, 128]))
            ys, bys = PHs_ystage[c % 2], B_ys[c % 2]
            for (a, bnd) in nt:
                b = pbank()
                for k in range(nk):
                    w, wb = wparts[k // kh]
                    mm(psb[b][:, 0:bnd - a], w[:, k % kh, :], srcT[:, k, a:bnd], k == 0, k == nk - 1, [wb, B_src[k]], [ps_buf[b]])
                cp("act", ys[:, a:bnd], psb[b][:, 0:bnd - a], [ps_buf[b]], [bys])
            ss_add(banks, T, ys[:, 0:T], [bys], sq, B_sq, c == 0, c == NCH - 1)
            S.dma("sp", Y_d[:, c, 0:T], ys[:, 0:T], reads=[bys])
        ss_close(banks, T, rstd_out, B_ro, float(D))

    def run_block(bi):
        T = TM + NS if bi == 0 else TM
        row0 = 0 if bi == 0 else TM + NS
        nt = tiles_of(T)
        has_s = bi == 0
        last = bi == 1

        S.fence()
        PH.reset()
        xT = PH.alloc([128, NCH, TMAX], F32)
        B_x = [Buf("x%d" % c) for c in range(NCH)]
        rstd = PH.alloc([128, TMAX], F32)
        B_rstd = Buf("rstd")
        sq = PH.alloc([128, TMAX], BF16)
        B_sq = Buf("sq")
        tmp = PH.alloc([128, TMAX], F32)
        B_tmp = Buf("tmp")
        off_xst = PH.off
        xst = [PH.alloc([128, D], F32) for _ in range(2)]
        B_xst = [Buf("xst0"), Buf("xst1")]
        nrt = (T + 127) // 128
        for r in range(nrt):
            n = min(128, T - r * 128)
            st_, bst = xst[r % 2], B_xst[r % 2]
            S.dma("sp", st_[0:n, :], xin[row0 + r * 128:row0 + r * 128 + n, :], writes=[bst])
            for q4 in range(4):
                b = pbank()
                for jj in range(4):
                    c = q4 * 4 + jj
                    tr(psb[b][:, jj * 128:jj * 128 + n], st_[0:n, c * 128:(c + 1) * 128], ident[0:n, 0:n], [bst, B_const], [ps_buf[b]])
                for jj in range(4):
                    c = q4 * 4 + jj
                    cp("act" if jj % 2 == 0 else "dve", xT[:, c, r * 128:r * 128 + n], psb[b][:, jj * 128:jj * 128 + n], [ps_buf[b]], [B_x[c]])
        for c in range(NCH):
            S.dma("sp", X_d[:, c, 0:T], xT[:, c, 0:T], reads=[B_x[c]])
        if stop_after == "A1":
            return False
        norm_of_x(xT, B_x, T, rstd, B_rstd, sq, B_sq)
        if stop_after == "A2":
            return False
        make_h(0, 1, 0, xT, B_x, rstd, B_rstd, T, xT, B_x, tmp, B_tmp)
        h0, B_h0 = xT, B_x
        if stop_after == "A":
            return False
        S.fence()
        PH.off = off_xst

        if has_s:
            S.dma("sp", pool_s[:, 0:14, :], cpool_d[:, 1:15, :])
            hs_tm = PH.alloc([NS, D], F32)
            B_hs = Buf("hs")
            for q4 in range(4):
                b = pbank()
                for jj in range(4):
                    c = q4 * 4 + jj
                    tr(psb[b][0:NS, jj * 128:(jj + 1) * 128], h0[:, c, TM:T], ident, [B_h0[c], B_const], [ps_buf[b]])
                cp("act", hs_tm[:, q4 * 512:(q4 + 1) * 512], psb[b][0:NS, :], [ps_buf[b]], [B_hs])
            S.dma("sp", pool_s[:, 14, :], hs_tm[:], reads=[B_hs])
            cache = [PH.alloc([120, D], F32) for _ in range(2)]
            B_cache = Buf("cache")
            sel = PH.alloc([120, 2, 4, 16], F32)
            B_sel = Buf("sel")
            for k in range(2):
                S.dma("sp", cache[k][:], cpool_d[8 * k:8 * k + 8].rearrange("b r d -> (b r) d"), writes=[B_cache])
                S.dma("sp", sel[:, k, :, :], poolsel_d[k], writes=[B_sel])
        ext = [PH.alloc([128, 4, 15 + TM], F32) for _ in range(2)]
        B_ext = [Buf("ext0"), Buf("ext1")]
        dT = PH.alloc([128, 4, TMAX], BF16)
        B_dT = Buf("dT")
        ysg_one = PH.alloc([128, TMAX], F32)
        ysg = [ysg_one, ysg_one]
        B_ysg_one = Buf("ysg")
        B_ysg = [B_ysg_one, B_ysg_one]
        rstd_y = RSTDY
        banks = ss_open(T)
        for g in range(4):
            w_ = POOL_W[g]
            cs_ = slice(4 * g, 4 * g + 4)
            E0, E1 = ext
            cp("dve", E0[:, :, 0:15], poolc[:, cs_, :], [B_poolc], [B_ext[0]])
            cp("act", E0[:, :, 15:15 + TM], h0[:, cs_, 0:TM], [B_h0[4 * g + i] for i in range(4)], [B_ext[0]])
            cur, oth = 0, 1
            sh = 1
            while sh < w_:
                A_, Bt = ext[cur], ext[oth]
                tt("dve", Bt[:, :, sh:15 + TM], A_[:, :, sh:15 + TM], A_[:, :, 0:15 + TM - sh], ALU.add, [B_ext[cur]], [B_ext[oth]])
                cur, oth = oth, cur
                sh *= 2
            Sw = ext[cur]
            stt("dve", dT[:, :, 0:TM], Sw[:, :, 15:15 + TM], 1.0 / w_, h0[:, cs_, 0:TM], ALU.mult, ALU.subtract,
                [B_ext[cur]] + [B_h0[4 * g + i] for i in range(4)], [B_dT])
            if bi == 0:
                for i in range(4):
                    tt("dve", Sw[:, i, 15:30], Sw[:, i, 15:30], cst[:, C_FIX + g * 15:C_FIX + (g + 1) * 15], ALU.mult,
                       [B_ext[cur], B_const], [B_ext[cur]])
                tt("dve", dT[:, :, 0:15], Sw[:, :, 15:30], h0[:, cs_, 0:15], ALU.subtract,
                   [B_ext[cur]] + [B_h0[4 * g + i] for i in range(4)], [B_dT])
            if has_s:
                b = pbank()
                for i in range(4):
                    c = 4 * g + i
                    for k in range(2):
                        mm(psb[b][:, i * 16:(i + 1) * 16], cache[k][:, c * 128:(c + 1) * 128], sel[:, k, g, :], k == 0, k == 1,
                           [B_cache, B_sel], [ps_buf[b]])
                for i in range(4):
                    c = 4 * g + i
                    stt("dve", dT[:, i, TM:T], h0[:, c, TM:T], 1.0 / w_ - 1.0, psb[b][:, i * 16:(i + 1) * 16], ALU.mult, ALU.add,
                        [B_h0[c], ps_buf[b]], [B_dT])
            w, wb = wload(pool_w[0, g].rearrange("(k p) n -> p k n", p=128), [128, 4, 512])
            for i in range(4):
                c = 4 * g + i
                ys, bys = ysg[c % 2], B_ysg[c % 2]
                for (a, bnd) in nt:
                    b = pbank()
                    for k in range(4):
                        mm(psb[b][:, 0:bnd - a], w[:, k, i * 128:(i + 1) * 128], dT[:, k, a:bnd], k == 0, k == 3, [wb, B_dT], [ps_buf[b]])
                    act(ys[:, a:bnd], psb[b][:, 0:bnd - a], AF.Copy, [ps_buf[b], B_par], [bys], scale=pscT[:, c:c + 1])
                ss_add(banks, T, ys[:, 0:T], [bys], sq, B_sq, c == 0, c == NCH - 1)
                S.dma("sp", Y_d[:, c, 0:T], ys[:, 0:T], reads=[bys])
        ss_close(banks, T, rstd_y, B_ry, float(D))
        cp("dve", poolc[:], h0[:, :, TM - 15:TM], list(B_h0), [B_poolc])
        if last:
            pp = PH.alloc([16, D], F32)
            B_pp = Buf("pp")
            for q4 in range(4):
                b = pbank()
                for jj in range(4):
                    c = q4 * 4 + jj
                    tr(psb[b][0:16, jj * 128:(jj + 1) * 128], h0[:, c, TM - 16:TM], ident, [B_h0[c], B_const], [ps_buf[b]])
                cp("act", pp[:, q4 * 512:(q4 + 1) * 512], psb[b][0:16, :], [ps_buf[b]], [B_pp])
            S.dma("sp", pool_p[:, :], pp[:], reads=[B_pp])
        if stop_after == "pool":
            return False

        HT_BYTES = NCH * TMAX * 2

        def rn_phase(lG, qG, lH, qA, qB, final=False):
            S.fence()
            PH.reset()
            hT = PH.alloc([128, NCH, TMAX], BF16)
            B_h = [Buf("h%d" % c) for c in range(NCH)]
            xT = PH.alloc([128, NCH, TMAX], F32)
            B_x = [Buf("x%d" % c) for c in range(NCH)]
            PHs["ystage"] = [PH.alloc([128, TMAX], F32) for _ in range(2)]
            PHs["B_ystage"] = [Buf("ys0"), Buf("ys1")]
            residual(lG, qG, T, xT, B_x, RSTDY, B_ry, True)
            if final:
                ost = [PH.alloc([128, D], F32) for _ in range(2)]
                B_ost = [Buf("ost0"), Buf("ost1")]
                for r in range(nrt):
                    n = min(128, T - r * 128)
                    o_, bo = ost[r % 2], B_ost[r % 2]
                    for q4 in range(4):
                        b = pbank()
                        for jj in range(4):
                            c = q4 * 4 + jj
                            tr(psb[b][0:n, jj * 128:(jj + 1) * 128], xT[:, c, r * 128:r * 128 + n], ident, [B_x[c], B_const], [ps_buf[b]])
                        cp("act" if q4 % 2 == 0 else "dve", o_[0:n, q4 * 512:(q4 + 1) * 512], psb[b][0:n, :], [ps_buf[b]], [bo])
                    if r < 8:
                        S.dma("sp", y_p[bi * TM + r * 128:bi * TM + (r + 1) * 128, :], o_[:, :], reads=[bo])
                    else:
                        S.dma("sp", y_s[:, :], o_[0:NS, :], reads=[bo])
                return None, None
            for c in range(NCH):
                S.dma("sp", X_d[:, c, 0:T], xT[:, c, 0:T], reads=[B_x[c]])
            rstd = PH.alloc([128, TMAX], F32)
            B_rstd = Buf("rstd")
            sq = PH.alloc([128, TMAX], BF16)
            B_sq = Buf("sq")
            tmp = PH.alloc([128, TMAX], F32)
            B_tmp = Buf("tmp")
            norm_of_x(xT, B_x, T, rstd, B_rstd, sq, B_sq)
            make_h(lH, qA, qB, xT, B_x, rstd, B_rstd, T, hT, B_h, tmp, B_tmp)
            return hT, B_h

        def ffn_phase(l, hT, B_h):
            S.fence()
            PH.reset()
            PH.off += HT_BYTES
            aT = PH.alloc([128, NFF, TMAX], BF16)
            B_a = [Buf("a%d" % c) for c in range(NFF)]
            ge = PH.alloc([128, 2 + TMAX], F32)
            B_ge = Buf("ge")
            tm = PH.alloc([128, TMAX], F32)
            B_tm = Buf("tm")
            if has_s:
                gcT = PH.alloc([128, NFF, 2 * NS], F32)
                gsT = PH.alloc([128, NFF, NS], F32)
                B_gcT, B_gsT = Buf("gcT"), Buf("gsT")
                S.dma("sp", ffn_s[l, :, 0, :], cffn_d[l, :, 1, :])
                gst = PH.alloc([2 * NS, 1408], F32)
                B_gst = Buf("gst")
                for pc in range(4):
                    for r_ in range(2):
                        S.dma("sp", gst[r_ * NS:(r_ + 1) * NS, :], cffn_d[l, :, r_, pc * 1408:(pc + 1) * 1408], writes=[B_gst])
                    for q in range(3):
                        b = pbank()
                        nn = 4 if q < 2 else 3
                        for jj in range(nn):
                            tr(psb[b][:, jj * 32:(jj + 1) * 32], gst[:, (q * 4 + jj) * 128:(q * 4 + jj + 1) * 128], ident[0:32, 0:32],
                               [B_gst, B_const], [ps_buf[b]])
                        c0 = pc * 11 + q * 4
                        cp("act", gcT[:, c0:c0 + nn, :], psb[b][:, 0:nn * 32].rearrange("p (c n) -> p c n", n=32), [ps_buf[b]], [B_gcT])
            wg_v = ffn_w_gate[l].rearrange("(k p) n -> p k n", p=128)
            wu_v = ffn_w_up[l].rearrange("(k p) n -> p k n", p=128)
            fw = lambda k, ch: fcwT[:, (l * 3 + k) * NFF + ch:(l * 3 + k) * NFF + ch + 1]
            for blk in range(22):
                wg, wgb = wload(wg_v[:, :, blk * 256:(blk + 1) * 256], [128, NCH, 256])
                wu, wub = wload(wu_v[:, :, blk * 256:(blk + 1) * 256], [128, NCH, 256])
                for jj in range(2):
                    ch = blk * 2 + jj
                    for (a, bnd) in nt:
                        b = pbank()
                        for k in range(NCH):
                            mm(psb[b][:, 0:bnd - a], wg[:, k, jj * 128:(jj + 1) * 128], hT[:, k, a:bnd], k == 0, k == NCH - 1,
                               [wgb, B_h[k]], [ps_buf[b]])
                        cp("act", ge[:, 2 + a:2 + bnd], psb[b][:, 0:bnd - a], [ps_buf[b]], [B_ge])
                    cp("dve", ge[:, 0:2], gatec[:, l, ch, :], [B_gatec], [B_ge])
                    cp("dve", gatec[:, l, ch, :], ge[:, TM:TM + 2], [B_ge], [B_gatec])
                    ts("dve", tm[:, 0:TM], ge[:, 2:2 + TM], fw(2, ch), ALU.mult, [B_ge, B_par], [B_tm])
                    stt("dve", tm[:, 0:TM], ge[:, 1:1 + TM], fw(1, ch), tm[:, 0:TM], ALU.mult, ALU.add, [B_ge, B_par, B_tm], [B_tm])
                    stt("dve", tm[:, 0:TM], ge[:, 0:TM], fw(0, ch), tm[:, 0:TM], ALU.mult, ALU.add, [B_ge, B_par, B_tm], [B_tm])
                    if has_s:
                        cp("dve", gsT[:, ch, :], ge[:, 2 + TM:2 + T], [B_ge], [B_gsT])
                        ts("dve", tm[:, TM:T], ge[:, 2 + TM:2 + T], fw(2, ch), ALU.mult, [B_ge, B_par], [B_tm])
                        stt("dve", tm[:, TM:T], gcT[:, ch, NS:2 * NS], fw(1, ch), tm[:, TM:T], ALU.mult, ALU.add, [B_gcT, B_par, B_tm], [B_tm])
                        stt("dve", tm[:, TM:T], gcT[:, ch, 0:NS], fw(0, ch), tm[:, TM:T], ALU.mult, ALU.add, [B_gcT, B_par, B_tm], [B_tm])
                    act(tm[:, 0:T], tm[:, 0:T], AF.Silu, [B_tm], [B_tm])
                    for (a, bnd) in nt:
                        b = pbank()
                        for k in range(NCH):
                            mm(psb[b][:, 0:bnd - a], wu[:, k, jj * 128:(jj + 1) * 128], hT[:, k, a:bnd], k == 0, k == NCH - 1,
                               [wub, B_h[k]], [ps_buf[b]])
                        tt("dve", aT[:, ch, a:bnd], tm[:, a:bnd], psb[b][:, 0:bnd - a], ALU.mult, [B_tm, ps_buf[b]], [B_a[ch]])
            if has_s:
                orow, B_orow = gst[0:NS, :], B_gst
            else:
                orow = PH.alloc([NS, 1408], F32)
                B_orow = Buf("orow")
            for pc in range(4):
                if has_s:
                    for q in range(3):
                        b = pbank()
                        nn = 4 if q < 2 else 3
                        for jj in range(nn):
                            tr(psb[b][0:NS, jj * 128:(jj + 1) * 128], gsT[:, pc * 11 + q * 4 + jj, :], ident, [B_gsT, B_const], [ps_buf[b]])
                        cp("act", orow[:, q * 512:q * 512 + nn * 128], psb[b][0:NS, 0:nn * 128], [ps_buf[b]], [B_orow])
                    S.dma("sp", ffn_s[l, :, 1, pc * 1408:(pc + 1) * 1408], orow[:, :], reads=[B_orow])
                if last:
                    for q in range(3):
                        b = pbank()
                        nn = 4 if q < 2 else 3
                        for jj in range(nn):
                            tr(psb[b][0:2, jj * 128:(jj + 1) * 128], gatec[:, l, pc * 11 + q * 4 + jj, :], ident, [B_gatec, B_const], [ps_buf[b]])
                        cp("act", orow[0:2, q * 512:q * 512 + nn * 128], psb[b][0:2, 0:nn * 128], [ps_buf[b]], [B_orow])
                    S.dma("sp", ffn_p[l, :, pc * 1408:(pc + 1) * 1408], orow[0:2, :], reads=[B_orow])
            wd_v = ffn_w_down[l].rearrange("(k p) n -> p k n", p=128)
            S.fence()
            PH.off = PH.base
            out_stream(T, NFF, lambda c: wd_v[:, :, c * 128:(c + 1) * 128], [128, NFF, 128], aT, B_a, RSTDY, B_ry)

        def delta_phase(hT, B_h):
            S.fence()
            PH.reset()
            PH.off += HT_BYTES
            ntile = 9 if has_s else 8
            win = dn_w_in[0].rearrange("(k p) n -> p k n", p=128)
            names = ("BETA", "G", "GC", "EGC", "KTS", "NBE", "EGL")
            SC = {nm: PH.alloc([128, 9, NH], F32) for nm in names}
            B_sc = Buf("scal")
            Sf = PH.alloc([128, NH, 128], F32)
            Sb = PH.alloc([128, NH, 128], BF16)
            B_S = [Buf("S%d" % h) for h in range(NH)]
            B_Sb = [Buf("Sb%d" % h) for h in range(NH)]
            if bi == 0:
                memset("dve", Sf[:], 0.0, B_S)
                memset("dve", Sb[:], 0.0, B_Sb)
            else:
                S.dma("sp", Sf[:], SD_d[:, :, :], writes=B_S)
                cp("act", Sb[:], Sf[:], B_S, B_Sb)
            t64 = PH.alloc([128, 64], F32)
            B_t64 = Buf("t64")
            w, wb = wload(win[:, :, 12288:12352], [128, NCH, 64])
            for n in range(ntile):
                m = 128 if n < 8 else NS
                a0 = n * 128
                b = pbank()
                for k in range(NCH):
                    mm(psb[b][0:m, 0:64], hT[:, k, a0:a0 + m], w[:, k, :], k == 0, k == NCH - 1, [wb, B_h[k]], [ps_buf[b]])
                act(SC["BETA"][0:m, n, :], psb[b][0:m, 0:32], AF.Sigmoid, [ps_buf[b]], [B_sc])
                tt("dve", t64[0:m, 0:32], psb[b][0:m, 32:64], hb[0:m, 32:64], ALU.add, [ps_buf[b], B_par], [B_t64])
                act(t64[0:m, 0:32], t64[0:m, 0:32], AF.Exp, [B_t64], [B_t64])
                act(t64[0:m, 0:32], t64[0:m, 0:32], AF.Ln, [B_t64, B_const], [B_t64], bias=one[0:m, :], scale=1.0)
                tt("dve", SC["G"][0:m, n, :], t64[0:m, 0:32], hb[0:m, 0:32], ALU.mult, [B_t64, B_par], [B_sc])
            for n in range(8):
                b = pbank()
                mm(psb[b][:, 0:32], tri, SC["G"][:, n, :], True, True, [B_const, B_sc], [ps_buf[b]])
                mm(psb[b][:, 32:64], onesf[:], SC["G"][:, n, :], True, True, [B_const, B_sc], [ps_buf[b]])
                cp("act", SC["GC"][:, n, :], psb[b][:, 0:32], [ps_buf[b]], [B_sc])
                act(SC["EGC"][:, n, :], psb[b][:, 0:32], AF.Exp, [ps_buf[b]], [B_sc])
                act(SC["EGL"][:, n, :], psb[b][:, 32:64], AF.Exp, [ps_buf[b]], [B_sc])
                tt("dve", SC["KTS"][:, n, :], psb[b][:, 32:64], SC["GC"][:, n, :], ALU.subtract, [ps_buf[b], B_sc], [B_sc])
                act(SC["KTS"][:, n, :], SC["KTS"][:, n, :], AF.Exp, [B_sc], [B_sc])
                stt("dve", SC["NBE"][:, n, :], SC["BETA"][:, n, :], -1.0, SC["EGC"][:, n, :], ALU.mult, ALU.mult, [B_sc], [B_sc])
            if has_s:
                S.dma("sp", conv_s[:, 0:2, :], sconv_d[:, 1:3, :])
                act(SC["EGC"][0:NS, 8, :], SC["G"][0:NS, 8, :], AF.Exp, [B_sc], [B_sc])
                rhsb = PH.alloc([NS, NH, NS], F32)
                B_rhsb = Buf("rhsb")
                BETAbc = PH.alloc([128, NH, NS], F32)
                EGbc = PH.alloc([128, NH, NS], F32)
                B_bc = Buf("bc")
                for (src, dst) in ((SC["BETA"], BETAbc), (SC["EGC"], EGbc)):
                    for h in range(NH):
                        ts("dve", rhsb[:, h, :], ident[0:NS, 0:NS], src[0:NS, 8, h:h + 1], ALU.mult, [B_const, B_sc], [B_rhsb])
                    b = pbank()
                    mm(psb[b][:, 0:512], onesf[0:NS, :], rhsb[:].rearrange("p h b -> p (h b)"), True, True, [B_const, B_rhsb], [ps_buf[b]])
                    cp("act", dst[:].rearrange("p h b -> p (h b)"), psb[b][:, 0:512], [ps_buf[b]], [B_bc])
            ge = PH.alloc([128, 3 + TMAX], F32)
            cv = PH.alloc([128, TMAX], F32)
            B_ge, B_cv = Buf("ge"), Buf("cv")
            sq = PH.alloc([128, TMAX], BF16)
            rq = ge
            B_sq, B_rq = Buf("sq"), B_ge
            qTn = PH.alloc([128, TMAX], BF16)
            kTn = PH.alloc([128, TMAX], BF16)
            B_q, B_k = Buf("qTn"), Buf("kTn")
            vT = PH.alloc([128, 2, TMAX], BF16)
            vTs = PH.alloc([128, 2, NS], F32)
            B_v = [Buf("v0"), Buf("v1")]
            zs = PH.alloc([128, 9, 256], BF16)
            B_zs = Buf("zs")
            onTg = PH.alloc([128, 2, TMAX], BF16)
            B_on = [Buf("on0"), Buf("on1")]
            mk = lambda dt: PH.alloc([128, 128], dt)
            mk4 = lambda dt: PH.alloc([128, 4, 128], dt)
            KT = PH.alloc([128, 16, 128], BF16)
            PQ = PH.alloc([128, 16, 128], BF16)
            YS = PH.alloc([128, 16, 128], BF16)
            VB = PH.alloc([128, 16, 128], BF16)
            B_KTq = [Buf("KT%d" % q) for q in range(4)]
            B_PQq = [Buf("PQ%d" % q) for q in range(4)]
            B_YSq = [Buf("YS%d" % q) for q in range(4)]
            B_VBq = [Buf("VB%d" % q) for q in range(4)]
            QCH = [(mk4(F32), mk4(F32), [mk4(BF16), mk4(BF16)], [mk4(BF16), mk4(BF16)], mk4(BF16)) for _ in range(2)]
            B_QCH = [(Buf("INA"), Buf("INB"), [Buf("LP0"), Buf("LP1")], [Buf("UP0"), Buf("UP1")], Buf("YW")) for _ in range(2)]
            MA4, MB4, I4 = mk4(F32), mk4(F32), mk4(BF16)
            ntri = mk(F32)
            for j in range(4):
                cp("dve", MA4[:, j, :], maskA, [B_const], [B_const])
                cp("dve", MB4[:, j, :], maskB, [B_const], [B_const])
                cp("dve", I4[:, j, :], identb[:], [B_const], [B_const])
            ts("dve", ntri[:], tri, -1.0, ALU.mult, [B_const], [B_const])
            SCN = [(mk(BF16), mk(BF16), mk(F32), mk(F32), mk(BF16), PH.alloc([128, 2], F32)) for _ in range(2)]
            B_SCN = [(Buf("R"), Buf("vn"), Buf("t1"), Buf("om"), Buf("onm"), Buf("ssn")) for _ in range(2)]

            def interleave(gens):
                gens = list(gens)
                while gens:
                    for g_ in list(gens):
                        try:
                            next(g_)
                        except StopIteration:
                            gens.remove(g_)

            if has_s:
                sctm = PH.alloc([48, 128], F32)
                scT = PH.alloc([128, 48], F32)
                B_sctm, B_scT = Buf("sctm"), Buf("scT")
                rsm = PH.alloc([NS, 128], F32)
                B_rsm = Buf("rsm")
                kqs = PH.alloc([128, NS, 2], F32)
                B_kqs = Buf("kqs")
                qkbc = PH.alloc([128, NS], F32)
                zsT = PH.alloc([128, 2, NS], F32)
                B_qkbc, B_zsT = Buf("qkbc"), Buf("zsT")
                Ss = PH.alloc([128, 8, 128], F32)
                B_Ss = Buf("Ss")
                Sn, B_Sn = Ss, B_Ss
                sm = {nm: PH.alloc([128, 8], F32) for nm in ("t", "vn", "o", "o2", "sq", "rn")}
                B_sm = Buf("sm")
                vntm = PH.alloc([8, 128], F32)
                B_vntm = Buf("vntm")
                prod = PH.alloc([128, NS], F32)
            dw = lambda k, cq: dcwT[:, k * 64 + cq:k * 64 + cq + 1]
            for gq in range(16):
                for ci in range(4):
                    cq = (gq, 16 + gq, 32 + 2 * gq, 33 + 2 * gq)[ci]
                    wci, wcib = wload(win[:, :, cq * 128:(cq + 1) * 128], [128, NCH, 128])
                    for (a, bnd) in nt:
                        b = pbank()
                        for k in range(NCH):
                            mm(psb[b][:, 0:bnd - a], wci[:, k, :], hT[:, k, a:bnd], k == 0, k == NCH - 1, [wcib, B_h[k]], [ps_buf[b]])
                        cp("act", ge[:, 3 + a:3 + bnd], psb[b][:, 0:bnd - a], [ps_buf[b]], [B_ge])
                    cp("dve", ge[:, 0:3], convc[:, cq, :], [B_convc], [B_ge])
                    cp("dve", convc[:, cq, :], ge[:, TM:TM + 3], [B_ge], [B_convc])
                    ts("dve", cv[:, 0:TM], ge[:, 3:3 + TM], dw(3, cq), ALU.mult, [B_ge, B_par], [B_cv])
                    for k in (2, 1, 0):
                        stt("dve", cv[:, 0:TM], ge[:, k:k + TM], dw(k, cq), cv[:, 0:TM], ALU.mult, ALU.add, [B_ge, B_par, B_cv], [B_cv])
                    if has_s:
                        for r_ in range(3):
                            S.dma("sp", sctm[r_ * NS:(r_ + 1) * NS, :], sconv_d[:, r_, cq * 128:(cq + 1) * 128], writes=[B_sctm])
                        b = pbank()
                        tr(psb[b][:, 0:48], sctm[:, :], ident[0:48, 0:48], [B_sctm, B_const], [ps_buf[b]])
                        cp("act", scT[:, :], psb[b][:, 0:48], [ps_buf[b]], [B_scT])
                        ts("dve", cv[:, TM:T], ge[:, 3 + TM:3 + T], dw(3, cq), ALU.mult, [B_ge, B_par], [B_cv])
                        for k in (2, 1, 0):
                            stt("dve", cv[:, TM:T], scT[:, k * NS:(k + 1) * NS], dw(k, cq), cv[:, TM:T], ALU.mult, ALU.add,
                                [B_scT, B_par, B_cv], [B_cv])
                        b = pbank()
                        tr(psb[b][0:NS, 0:128], ge[:, 3 + TM:3 + T], ident, [B_ge, B_const], [ps_buf[b]])
                        cp("act", rsm[:, :], psb[b][0:NS, 0:128], [ps_buf[b]], [B_rsm])
                        S.dma("sp", conv_s[:, 2, cq * 128:(cq + 1) * 128], rsm[:, :], reads=[B_rsm])
                    if ci < 2:
                        act(cv[:, 0:T], cv[:, 0:T], AF.Silu, [B_cv], [B_cv])
                        act(sq[:, 0:T], cv[:, 0:T], AF.Square, [B_cv], [B_sq])
                        for (a, bnd) in nt:
                            b = pbank()
                            mm(psb[b][:, 0:bnd - a], onesb[:], sq[:, a:bnd], True, True, [B_sq, B_const], [ps_buf[b]])
                            act(rq[:, a:bnd], psb[b][:, 0:bnd - a], AF.Sqrt, [ps_buf[b], B_const], [B_rq], bias=eps, scale=1.0)
                        recip(rq[:, 0:T], rq[:, 0:T], [B_rq], [B_rq])
                        if ci == 0:
                            stt("dve", qTn[:, 0:T], cv[:, 0:T], 128.0 ** -0.5, rq[:, 0:T], ALU.mult, ALU.mult, [B_cv, B_rq], [B_q])
                            if has_s:
                                stt("dve", kqs[:, :, 1], cv[:, TM:T], 128.0 ** -0.5, rq[:, TM:T], ALU.mult, ALU.mult, [B_cv, B_rq], [B_kqs])
                        else:
                            tt("dve", kTn[:, 0:T], cv[:, 0:T], rq[:, 0:T], ALU.mult, [B_cv, B_rq], [B_k])
                            if has_s:
                                tt("dve", kqs[:, :, 0], cv[:, TM:T], rq[:, TM:T], ALU.mult, [B_cv, B_rq], [B_kqs])
                    else:
                        act(vT[:, ci - 2, 0:T], cv[:, 0:T], AF.Silu, [B_cv], [B_v[ci - 2]])
                        if has_s:
                            act(vTs[:, ci - 2, :], cv[:, TM:T], AF.Silu, [B_cv], [B_v[ci - 2]])
                wz, wzb = wload(win[:, :, 8192 + gq * 256:8192 + (gq + 1) * 256], [128, NCH, 256])
                for n in range(ntile):
                    m = 128 if n < 8 else NS
                    a0 = n * 128
                    b = pbank()
                    for k in range(NCH):
                        mm(psb[b][0:m, 0:256], hT[:, k, a0:a0 + m], wz[:, k, :], k == 0, k == NCH - 1, [wzb, B_h[k]], [ps_buf[b]])
                    act(zs[0:m, n, :], psb[b][0:m, 0:256], AF.Silu, [ps_buf[b]], [B_zs])
                def pre_chain(c):
                    bks = [4 * c + j for j in range(4)]
                    rr = [0]

                    def nb():
                        b_ = bks[rr[0] % 4]
                        rr[0] += 1
                        return b_

                    v4 = lambda b_: psb[b_][:, 0:512].rearrange("p (j n) -> p j n", j=4)
                    v4h = lambda b_: psb16[b_][:, 0:512].rearrange("p (j n) -> p j n", j=4)
                    INA, INB, LP, UP, YW = QCH[c]
                    B_INA, B_INB, B_LP, B_UP, B_YW = B_QCH[c]
                    for q in range(c, 4, 2):
                        prs = [(4 * q + j, 2 * q + j // 2, j % 2) for j in range(4)]
                        jb = lambda j: slice(j * 128, (j + 1) * 128)
                        ck = lambda n: slice(n * 128, (n + 1) * 128)
                        b0 = nb()
                        for jn in range(2):
                            tr(psb16[b0][:, jb(jn)], kTn[:, ck(2 * q + jn)], identb[:], [B_k, B_const], [ps_buf[b0]])
                        for j, (p, n, i) in enumerate(prs):
                            h = 2 * gq + i
                            act(KT[:, p, :], psb16[b0][:, jb(j // 2)], AF.Copy, [ps_buf[b0], B_sc], [B_KTq[q]], scale=SC["KTS"][:, n, h:h + 1])
                        bd = nb()
                        for j, (p, n, i) in enumerate(prs):
                            h = 2 * gq + i
                            gcol = SC["G"][:, n, h:h + 1].broadcast_to([128, 128])
                            mm(psb[bd][:, jb(j)], gcol, tri, True, False, [B_sc, B_const], [ps_buf[bd]])
                            mm(psb[bd][:, jb(j)], ntri[:], gcol, False, True, [B_sc, B_const], [ps_buf[bd]])
                        yield
                        tt("dve", INA[:], v4(bd), MA4[:], ALU.add, [ps_buf[bd], B_const], [B_INA])
                        tt("dve", INB[:], v4(bd), MB4[:], ALU.add, [ps_buf[bd], B_const], [B_INB])
                        bkk = nb()
                        for j, (p, n, i) in enumerate(prs):
                            mm(psb[bkk][:, jb(j)], kTn[:, ck(n)], kTn[:, ck(n)], True, True, [B_k], [ps_buf[bkk]])
                        bqk = nb()
                        for j, (p, n, i) in enumerate(prs):
                            mm(psb[bqk][:, jb(j)], kTn[:, ck(n)], qTn[:, ck(n)], True, True, [B_k, B_q], [ps_buf[bqk]])
                        yield
                        act(INA[:], INA[:], AF.Exp, [B_INA], [B_INA], scale=-1.0)
                        act(INB[:], INB[:], AF.Exp, [B_INB], [B_INB])
                        yield
                        for j, (p, n, i) in enumerate(prs):
                            h = 2 * gq + i
                            stt("dve", LP[0][:, j, :], psb[bkk][:, jb(j)], SC["BETA"][:, n, h:h + 1], INA[:, j, :], ALU.mult, ALU.mult,
                                [ps_buf[bkk], B_sc, B_INA], [B_LP[0]])
                        tt("dve", PQ[:, 4 * q:4 * q + 4, :], v4(bqk), INB[:], ALU.mult, [ps_buf[bqk], B_INB], [B_PQq[q]])
                        yield
                        bu = nb()
                        for j in range(4):
                            tr(psb16[bu][:, jb(j)], LP[0][:, j, :], identb[:], [B_LP[0], B_const], [ps_buf[bu]])
                        yield
                        cp("act", UP[0][:], v4h(bu), [ps_buf[bu]], [B_UP[0]])
                        tt("dve", YW[:], I4[:], v4h(bu), ALU.subtract, [ps_buf[bu], B_const], [B_YW])
                        yield
                        cur = 0
                        for lev in range(6):
                            nx = 1 - cur
                            bP = nb()
                            for j in range(4):
                                mm(psb[bP][:, jb(j)], UP[cur][:, j, :], LP[cur][:, j, :], True, True, [B_UP[cur], B_LP[cur]], [ps_buf[bP]])
                            if lev < 5:
                                bU = nb()
                                for j in range(4):
                                    mm(psb[bU][:, jb(j)], LP[cur][:, j, :], UP[cur][:, j, :], True, True, [B_UP[cur], B_LP[cur]], [ps_buf[bU]])
                            yield
                            cp("act", LP[nx][:], v4(bP), [ps_buf[bP]], [B_LP[nx]])
                            if lev < 5:
                                cp("dve", UP[nx][:], v4(bU), [ps_buf[bU]], [B_UP[nx]])
                            yield
                            bY = nb()
                            for j in range(4):
                                mm(psb[bY][:, jb(j)], LP[nx][:, j, :], YW[:, j, :], True, True, [B_LP[nx], B_YW], [ps_buf[bY]])
                            yield
                            if lev < 5:
                                tt("dve", YW[:], YW[:], v4(bY), ALU.add, [B_YW, ps_buf[bY]], [B_YW])
                            else:
                                tt("dve", YS[:, 4 * q:4 * q + 4, :], YW[:], v4(bY), ALU.add, [B_YW, ps_buf[bY]], [B_YSq[q]])
                            cur = nx
                            yield
                        bv = nb()
                        for j, (p, n, i) in enumerate(prs):
                            tr(psb16[bv][:, jb(j)], vT[:, i, ck(n)], identb[:], [B_v[i], B_const], [ps_buf[bv]])
                        yield
                        for j, (p, n, i) in enumerate(prs):
                            h = 2 * gq + i
                            if j % 2 == 0:
                                act(VB[:, p, :], psb16[bv][:, jb(j)], AF.Copy, [ps_buf[bv], B_sc], [B_VBq[q]], scale=SC["BETA"][:, n, h:h + 1])
                            else:
                                ts("dve", VB[:, p, :], psb16[bv][:, jb(j)], SC["BETA"][:, n, h:h + 1], ALU.mult, [ps_buf[bv], B_sc], [B_VBq[q]])
                        yield

                def scan_chain(i):
                    bk = [4 * i + j for j in range(4)]
                    h = 2 * gq + i
                    Rm, vn, t1, om, onm, ssn = SCN[i]
                    B_R, B_vn, B_t1, B_om, B_onm, B_ssn = B_SCN[i]
                    for n in range(8):
                        p = 2 * n + i
                        c0, c1 = n * 128, (n + 1) * 128
                        sc = (lambda n: (lambda nm: SC[nm][:, n, h:h + 1]))(n)
                        mm(psb[bk[0]][:, 0:128], kTn[:, c0:c1], Sb[:, h, :], True, True, [B_k, B_Sb[h]], [ps_buf[bk[0]]])
                        mm(psb[bk[2]][:, 0:128], qTn[:, c0:c1], Sb[:, h, :], True, True, [B_q, B_Sb[h]], [ps_buf[bk[2]]])
                        yield
                        stt("dve", Rm[:], psb[bk[0]][:, 0:128], sc("NBE"), VB[:, p, :], ALU.mult, ALU.add, [ps_buf[bk[0]], B_sc, B_VBq[p // 4]], [B_R])
                        act(t1[:], psb[bk[2]][:, 0:128], AF.Copy, [ps_buf[bk[2]], B_sc], [B_t1], scale=sc("EGC"))
                        yield
                        mm(psb[bk[1]][:, 0:128], YS[:, p, :], Rm[:], True, True, [B_YSq[p // 4], B_R], [ps_buf[bk[1]]])
                        yield
                        cp("act", vn[:], psb[bk[1]][:, 0:128], [ps_buf[bk[1]]], [B_vn])
                        yield
                        mm(psb[bk[3]][:, 0:128], PQ[:, p, :], vn[:], True, True, [B_PQq[p // 4], B_vn], [ps_buf[bk[3]]])
                        mm(psb[bk[0]][:, 0:128], KT[:, p, :], vn[:], True, True, [B_KTq[p // 4], B_vn], [ps_buf[bk[0]]])
                        yield
                        tt("dve", om[:], t1[:], psb[bk[3]][:, 0:128], ALU.add, [B_t1, ps_buf[bk[3]]], [B_om])
                        stt("dve", Sf[:, h, :], Sf[:, h, :], sc("EGL"), psb[bk[0]][:, 0:128], ALU.mult, ALU.add, [B_S[h], B_sc, ps_buf[bk[0]]], [B_S[h]])
                        yield
                        cp("act", Sb[:, h, :], Sf[:, h, :], [B_S[h]], [B_Sb[h]])
                        act(t1[:], om[:], AF.Square, [B_om], [B_t1, B_ssn], accum=ssn[:, 0:1])
                        yield
                        act(ssn[:, 0:1], ssn[:, 0:1], AF.Sqrt, [B_ssn, B_const], [B_ssn], bias=eps, scale=1.0 / 128.0)
                        yield
                        recip(ssn[:, 0:1], ssn[:, 0:1], [B_ssn], [B_ssn])
                        yield
                        stt("dve", om[:], om[:], ssn[:, 0:1], dnwbc[:], ALU.mult, ALU.mult, [B_om, B_ssn, B_par], [B_om])
                        yield
                        tt("dve", onm[:], om[:], zs[:, n, i * 128:(i + 1) * 128], ALU.mult, [B_om, B_zs], [B_onm])
                        yield
                        tr(psb16[bk[1]][:, 0:128], onm[:], identb[:], [B_onm, B_const], [ps_buf[bk[1]]])
                        yield
                        cp("act", onTg[:, i, c0:c1], psb16[bk[1]][:, 0:128], [ps_buf[bk[1]]], [B_on[i]])
                        yield

                interleave([pre_chain(c) for c in range(2)])
                interleave([scan_chain(i) for i in range(2)])
                if has_s:
                    tt("dve", prod[:, :], kqs[:, :, 0], kqs[:, :, 1], ALU.mult, [B_kqs], [B_prod])
                    b = pbank()
                    mm(psb[b][:, 0:NS], onesf[:], prod[:, :], True, True, [B_prod, B_const], [ps_buf[b]])
                    cp("act", qkbc[:, :], psb[b][:, 0:NS], [ps_buf[b]], [B_qkbc])
                    for i in range(2):
                        b = pbank()
                        tr(psb16[b][:, 0:NS], zs[0:NS, 8, i * 128:(i + 1) * 128], identb[0:NS, 0:NS], [B_zs, B_const], [ps_buf[b]])
                        cp("act", zsT[:, i, :], psb16[b][:, 0:NS], [ps_buf[b]], [B_zsT])
                    for sub in range(4):
                        i, b0 = sub // 2, (sub % 2) * 8
                        h = 2 * gq + i
                        S.dma("sp", Ss[:, :, :], srec_d[b0:b0 + 8, h].rearrange("b k v -> k b v"), writes=[B_Ss])
                        bp = pbank()
                        for j in range(8):
                            mm(psb[bp][:, 2 * j:2 * j + 2], Ss[:, j, :], kqs[:, b0 + j, :], True, True, [B_Ss, B_kqs], [ps_buf[bp]])
                        KQ = psb[bp][:, 0:16].rearrange("p (j two) -> p j two", two=2)
                        eg = EGbc[:, h, b0:b0 + 8]
                        tt("dve", sm["t"][:, :], KQ[:, :, 0], eg, ALU.mult, [ps_buf[bp], B_bc], [B_sm])
                        tt("dve", sm["t"][:, :], vTs[:, i, b0:b0 + 8], sm["t"][:, :], ALU.subtract, [B_v[i], B_sm], [B_sm])
                        tt("dve", sm["vn"][:, :], sm["t"][:, :], BETAbc[:, h, b0:b0 + 8], ALU.mult, [B_sm, B_bc], [B_sm])
                        tt("dve", sm["o"][:, :], KQ[:, :, 1], eg, ALU.mult, [ps_buf[bp], B_bc], [B_sm])
                        tt("dve", sm["o2"][:, :], sm["vn"][:, :], qkbc[:, b0:b0 + 8], ALU.mult, [B_sm, B_qkbc], [B_sm])
                        tt("dve", sm["o"][:, :], sm["o"][:, :], sm["o2"][:, :], ALU.add, [B_sm], [B_sm])
                        tt("dve", sm["sq"][:, :], sm["o"][:, :], sm["o"][:, :], ALU.mult, [B_sm], [B_sm])
                        b = pbank()
                        mm(psb[b][:, 0:8], onesf[:], sm["sq"][:, :], True, True, [B_sm, B_const], [ps_buf[b]])
                        act(sm["rn"][:, :], psb[b][:, 0:8], AF.Sqrt, [ps_buf[b], B_const], [B_sm], bias=eps, scale=1.0 / 128.0)
                        recip(sm["rn"][:, :], sm["rn"][:, :], [B_sm], [B_sm])
                        stt("dve", sm["o"][:, :], sm["o"][:, :], dnwcol[:, 0:1], sm["rn"][:, :], ALU.mult, ALU.mult, [B_sm, B_par], [B_sm])
                        tt("dve", onTg[:, i, TM + b0:TM + b0 + 8], sm["o"][:, :], zsT[:, i, b0:b0 + 8], ALU.mult, [B_sm, B_zsT], [B_on[i]])
                        bt = pbank()
                        tr(psb[bt][0:8, 0:128], sm["vn"][:, :], ident, [B_sm, B_const], [ps_buf[bt]])
                        cp("act", vntm[:, :], psb[bt][0:8, 0:128], [ps_buf[bt]], [B_vntm])
                        for j in range(8):
                            bb = pbank()
                            mm(psb[bb][:, 0:128], ident[0:8, j:j + 1].broadcast_to([8, 128]), vntm[:, :], True, True, [B_vntm, B_const], [ps_buf[bb]])
                            act(Sn[:, j, :], Ss[:, j, :], AF.Copy, [B_Ss, B_bc], [B_Sn], scale=EGbc[:, h, b0 + j:b0 + j + 1])
                            stt("dve", Sn[:, j, :], psb[bb][:, 0:128], kqs[:, b0 + j, 0:1], Sn[:, j, :], ALU.mult, ALU.add,
                                [ps_buf[bb], B_kqs, B_Sn], [B_Sn])
                        S.dma("sp", rec_s[b0:b0 + 8, h].rearrange("b k v -> k b v"), Sn[:, :, :], reads=[B_Sn])
                for i in range(2):
                    S.dma("sp", ON_d[:, 2 * gq + i, 0:T], onTg[:, i, 0:T], reads=[B_on[i]])
            if last:
                S.dma("sp", rec_p.rearrange("h k v -> k h v"), Sf[:, :, :], reads=B_S)
                cpt = PH.alloc([3, 2048], F32)
                B_cpt = Buf("cpt")
                for q in range(4):
                    for q4 in range(4):
                        b = pbank()
                        for jj in range(4):
                            tr(psb[b][0:3, jj * 128:(jj + 1) * 128], convc[:, q * 16 + q4 * 4 + jj, :], ident, [B_convc, B_const], [ps_buf[b]])
                        cp("act", cpt[:, q4 * 512:(q4 + 1) * 512], psb[b][0:3, :], [ps_buf[b]], [B_cpt])
                    S.dma("sp", conv_p[:, q * 2048:(q + 1) * 2048], cpt[:, :], reads=[B_cpt])
            else:
                S.dma("sp", SD_d[:, :, :], Sf[:, :, :], reads=B_S)

        def outproj_phase():
            S.fence()
            PH.reset()
            onT = PH.alloc([128, NH, TMAX], BF16)
            B_onT = [Buf("onT%d" % h) for h in range(NH)]
            for h in range(NH):
                S.dma("sp", onT[:, h, 0:T], ON_d[:, h, 0:T], writes=[B_onT[h]])
            wo_v = dn_w_out[0].rearrange("(k p) n -> p k n", p=128)
            out_stream(T, NH, lambda c: wo_v[:, :, c * 128:(c + 1) * 128], [128, NH, 128], onT, B_onT, RSTDY, B_ry)

        PH_prod = None
        B_prod = Buf("prod")
        hT, B_h = rn_phase(0, 2, 0, 4, 3)
        ffn_phase(0, hT, B_h)
        if stop_after == "ffn0":
            return False
        hT, B_h = rn_phase(0, 5, 1, 1, 0)
        if stop_after == "rn2":
            return False
        if has_s:
            pass
        delta_phase(hT, B_h)
        outproj_phase()
        if stop_after == "oproj":
            return False
        hT, B_h = rn_phase(1, 2, 1, 4, 3)
        ffn_phase(1, hT, B_h)
        rn_phase(1, 5, None, None, None, final=True)
        return True

    if stop_after != "pro" and run_block(0):
        run_block(1)

    S.resolve()
    S.emit(nc)
    return nc, dbg_outs


_PROG = {}
W_NAMES = ("norm_w", "ada_w", "ada_b", "pool_w", "pool_scale", "dn_w_in", "dn_conv_w", "dn_a_log", "dn_dt_bias",
           "dn_norm_w", "dn_w_out", "ffn_w_gate", "ffn_w_up", "ffn_conv_w", "ffn_w_down")


def make_in_maps(inputs, ncores=8):
    f = lambda a: np.ascontiguousarray(np.asarray(a, dtype=np.float32))
    consts = host_consts()
    sel = host_poolsel()
    maps = []
    for c in range(ncores):
        b = c % 4
        sl = slice(NS * c, NS * (c + 1))
        xp = np.asarray(inputs["x_prompt"])[b]
        xs = np.asarray(inputs["x_sample"])[sl, 0, :]
        m = {
            "xin": f(np.concatenate([xp[0:TM], xs, xp[TM:2 * TM]], axis=0)),
            "cin": f(np.concatenate([np.asarray(inputs["c_sample"])[sl], np.asarray(inputs["c_prompt"])[b:b + 1]], axis=0)),
            "consts": consts,
            "poolsel": sel,
            "cache_pool_c": f(np.asarray(inputs["cache_pool"])[0, sl]),
            "state_conv_c": f(np.asarray(inputs["state_conv"])[0, sl]),
            "state_rec_c": f(np.asarray(inputs["state_rec"])[0, sl]),
            "cache_ffn_c": f(np.asarray(inputs["cache_ffn_conv"])[:, sl]),
        }
        for nm in W_NAMES:
            m[nm] = f(inputs[nm])
        maps.append(m)
    return maps


def kernel(**inputs):
    if "nc" not in _PROG:
        _PROG["nc"] = build_program()[0]
    nc = _PROG["nc"]
    maps = make_in_maps(inputs)
    res = run_bass_kernel_spmd(nc, maps, core_ids=list(range(8))).results
    y_prompt = np.stack([res[b]["y_p"] for b in range(4)], 0)
    y_sample = np.concatenate([res[c]["y_s"] for c in range(8)], 0)[:, None, :]
    pool_prompt = np.stack([res[b]["pool_p"][1:16] for b in range(4)], 0)[None]
    pool_sample = np.concatenate([res[c]["pool_s"] for c in range(8)], 0)[None]
    conv_prompt = np.stack([res[b]["conv_p"] for b in range(4)], 0)[None]
    conv_sample = np.concatenate([res[c]["conv_s"] for c in range(8)], 0)[None]
    rec_prompt = np.stack([res[b]["rec_p"] for b in range(4)], 0)[None]
    rec_sample = np.concatenate([res[c]["rec_s"] for c in range(8)], 0)[None]
    ffn_prompt = np.stack([res[b]["ffn_p"] for b in range(4)], 1)
    ffn_sample = np.concatenate([res[c]["ffn_s"] for c in range(8)], 1)
    outs = (y_prompt, y_sample, pool_prompt, pool_sample, conv_prompt, conv_sample, rec_prompt, rec_sample,
            ffn_prompt, ffn_sample)
    return tuple(np.ascontiguousarray(o, dtype=np.float32) for o in outs)
```
